# Optimizing a Trainium2 kernel written in Bass

```python
import math
import jax, jax.numpy as jnp
from jax import lax
import numpy as np

D_MODEL = 2048
BATCH = 4
SEQ = 2048
DEPTH = 4
DEC_BATCH = 128
DEC_SEQ = 8
PAST_LEN = 16384
PAGE_SIZE = 128

N_MIXERS = 2
N_A_LAYERS = (DEPTH + 1) // N_MIXERS
N_B_LAYERS = DEPTH // N_MIXERS
HEAD_DIM = 128
MIX_W = D_MODEL
X_HEADS = 4
X_W = X_HEADS * HEAD_DIM
N_MEM = 256
DELTA_HEADS = (MIX_W - X_W) // HEAD_DIM
DELTA_W = DELTA_HEADS * HEAD_DIM
CONV_W = 4
DELTA_CHUNK = 64
GMLP_GROUPS = DELTA_HEADS
GMLP_GD = HEAD_DIM
GMLP_W = GMLP_GROUPS * GMLP_GD
GMLP_CHUNK = 128
FFN_DIM = 5632
IN_A = 4 * DELTA_W + 2 * DELTA_HEADS + X_W
IN_B = 2 * GMLP_W + X_W
EPS = 1e-6

kernel_name = 'hybrid_deltanet_gmlp_memory_decoder_step'


def _rmsnorm(x, gain):
    xf = x.astype(jnp.float32)
    y = xf * lax.rsqrt(jnp.mean(xf * xf, axis=-1, keepdims=True) + EPS)
    return (y * gain.astype(jnp.float32)).astype(x.dtype)


def _l2norm(x):
    xf = x.astype(jnp.float32)
    return (xf * lax.rsqrt(jnp.sum(xf * xf, axis=-1, keepdims=True) + EPS)).astype(x.dtype)


def _layernorm(x, gain, bias):
    xf = x.astype(jnp.float32)
    mu = jnp.mean(xf, axis=-1, keepdims=True)
    xc = xf - mu
    var = jnp.mean(xc * xc, axis=-1, keepdims=True)
    y = xc * lax.rsqrt(var + EPS) * gain.astype(jnp.float32) + bias.astype(jnp.float32)
    return y.astype(x.dtype)


def _swiglu(x, w_gu, w_dn):
    gate, up = jnp.split(x @ w_gu, 2, axis=-1)
    return (jax.nn.silu(gate) * up) @ w_dn


def _short_conv(x, buf, w):
    T = x.shape[1]
    xp = jnp.concatenate([buf.astype(x.dtype), x], axis=1)
    y = xp[:, 0:T] * w[0]
    for j in range(1, CONV_W):
        y = y + xp[:, j:j + T] * w[j]
    return jax.nn.silu(y), xp[:, -(CONV_W - 1):]


def _gated_delta(q, k, v, beta, g, s0):
    Bn, T, H, D = q.shape
    C = min(DELTA_CHUNK, T)
    n = -(-T // C)
    pad = n * C - T

    def prep(a):
        a = a.astype(jnp.float32)
        a = jnp.pad(a, [(0, 0), (0, pad)] + [(0, 0)] * (a.ndim - 2))
        a = jnp.moveaxis(a, 2, 1)
        return a.reshape((Bn, H, n, C) + a.shape[3:])

    q, k, v, beta, g = prep(q), prep(k), prep(v), prep(beta), prep(g)
    gc = jnp.cumsum(g, axis=-1)
    diff = gc[..., :, None] - gc[..., None, :]
    strict = jnp.tril(jnp.ones((C, C), bool), -1)
    causal = jnp.tril(jnp.ones((C, C), bool))
    kk = jnp.einsum('bhnid,bhnjd->bhnij', k, k)
    lmat = jnp.where(strict, beta[..., :, None] * kk * jnp.exp(jnp.where(strict, diff, 0.0)), 0.0)
    eye = jnp.eye(C, dtype=jnp.float32)
    rhs = jnp.concatenate([v * beta[..., None], k * (beta * jnp.exp(gc))[..., None]], axis=-1)
    sol = lax.linalg.triangular_solve(lmat + eye, rhs, left_side=True, lower=True)
    u, w = sol[..., :D], sol[..., D:]
    qk = jnp.einsum('bhnid,bhnjd->bhnij', q, k)
    a_intra = jnp.where(causal, qk * jnp.exp(jnp.where(causal, diff, 0.0)), 0.0)
    q_dec = q * jnp.exp(gc)[..., None]
    k_dec = k * jnp.exp(gc[..., -1:] - gc)[..., None]
    g_last = jnp.exp(gc[..., -1])

    def step(s, inp):
        q_c, k_c, u_c, w_c, a_c, gl = inp
        v_new = u_c - jnp.einsum('bhcd,bhde->bhce', w_c, s)
        o = jnp.einsum('bhcd,bhde->bhce', q_c, s) + jnp.einsum('bhij,bhje->bhie', a_c, v_new)
        s = s * gl[..., None, None] + jnp.einsum('bhcd,bhce->bhde', k_c, v_new)
        return s, o

    xs = (jnp.moveaxis(q_dec, 2, 0), jnp.moveaxis(k_dec, 2, 0), jnp.moveaxis(u, 2, 0),
          jnp.moveaxis(w, 2, 0), jnp.moveaxis(a_intra, 2, 0), jnp.moveaxis(g_last, 2, 0))
    s_fin, o = lax.scan(step, s0.astype(jnp.float32), xs)
    o = jnp.moveaxis(o, 0, 2).reshape(Bn, H, n * C, D)[:, :, :T]
    return jnp.moveaxis(o, 1, 2), s_fin


def _delta_mixer(p_mix, conv_buf, s0, conv_w, a_log, dt_bias, out_gain):
    Bn, T, _ = p_mix.shape
    qkv, z, a, b = jnp.split(p_mix, [3 * DELTA_W, 4 * DELTA_W, 4 * DELTA_W + DELTA_HEADS], axis=-1)
    qkv, new_buf = _short_conv(qkv, conv_buf, conv_w)
    q, k, v = [t.reshape(Bn, T, DELTA_HEADS, HEAD_DIM) for t in jnp.split(qkv, 3, axis=-1)]
    q = _l2norm(q) * (HEAD_DIM ** -0.5)
    k = _l2norm(k)
    beta = jax.nn.sigmoid(b.astype(jnp.float32))
    g = -jnp.exp(a_log.astype(jnp.float32)) * jax.nn.softplus(a.astype(jnp.float32) + dt_bias.astype(jnp.float32))
    o, s_new = _gated_delta(q, k, v, beta, g, s0)
    o = _rmsnorm(o, out_gain) * jax.nn.silu(z.reshape(Bn, T, DELTA_HEADS, HEAD_DIM).astype(jnp.float32))
    return o.reshape(Bn, T, DELTA_W).astype(p_mix.dtype), new_buf, s_new


def _gmlp_mixer(p_mix, ln_gain, ln_bias, w_s, b_s):
    Bn, T, _ = p_mix.shape
    u, v = jnp.split(jax.nn.gelu(p_mix, approximate=False), 2, axis=-1)
    v = _layernorm(v, ln_gain, ln_bias)
    C = min(GMLP_CHUNK, T)
    n = T // C
    w = jnp.where(jnp.tril(jnp.ones((C, C), bool)), w_s[:, :C, :C], 0.0)
    vg = v.reshape(Bn, n, C, GMLP_GROUPS, GMLP_GD)
    mixed = jnp.einsum('gij,bnjgc->bnigc', w, vg) + b_s[:, :C].T[None, None, :, :, None]
    return u * mixed.reshape(Bn, T, GMLP_W), v


def _mem_kv(mem, gain, w_kv):
    Bn, N, _ = mem.shape
    k, v = jnp.split(_rmsnorm(mem, gain) @ w_kv, 2, axis=-1)
    return k.reshape(Bn, N, X_HEADS, HEAD_DIM), v.reshape(Bn, N, X_HEADS, HEAD_DIM)


def _mem_attn(q, mem_k, mem_v):
    Bn, T, _ = q.shape
    q = q.reshape(Bn, T, X_HEADS, HEAD_DIM)
    s = jnp.einsum('bthd,bnhd->bhtn', q, mem_k).astype(jnp.float32) * (HEAD_DIM ** -0.5)
    p = jax.nn.softmax(s, axis=-1).astype(mem_v.dtype)
    return jnp.einsum('bhtn,bnhd->bthd', p, mem_v).reshape(Bn, T, X_W)


def _trunk(x, mem_ks, mem_vs, conv_bufs, delta_states, p):
    new_conv, new_delta, new_v = [], [], []
    for i in range(DEPTH):
        g = p['norm_gains'][i]
        j = i // N_MIXERS
        x = x + 0.5 * _rmsnorm(_swiglu(_rmsnorm(x, g[0]), p['w_ffn_gu'][i, 0], p['w_ffn_dn'][i, 0]), g[1])
        h = _rmsnorm(x, g[2])
        if i % N_MIXERS == 0:
            proj = h @ p['w_in_a'][j]
            mix, buf, s = _delta_mixer(proj[..., :-X_W], conv_bufs[j], delta_states[j], p['conv_w'][j],
                                       p['a_log'][j], p['dt_bias'][j], p['delta_norm_gain'][j])
            new_conv.append(buf)
            new_delta.append(s)
        else:
            proj = h @ p['w_in_b'][j]
            mix, v_rows = _gmlp_mixer(proj[..., :-X_W], p['gmlp_ln_gain'][j], p['gmlp_ln_bias'][j],
                                      p['w_spatial'][j], p['b_spatial'][j])
            new_v.append(v_rows)
        mem_out = _mem_attn(proj[..., -X_W:], mem_ks[i], mem_vs[i])
        x = x + _rmsnorm(jnp.concatenate([mix, mem_out], axis=-1) @ p['w_out'][i], g[3])
        x = x + 0.5 * _rmsnorm(_swiglu(_rmsnorm(x, g[4]), p['w_ffn_gu'][i, 1], p['w_ffn_dn'][i, 1]), g[5])
    return x, jnp.stack(new_conv), jnp.stack(new_delta), jnp.stack(new_v)


def setup_inputs(seed: int = 0) -> dict:
    key = jax.random.key(seed)
    ks = jax.random.split(key, 24)

    def nrm(k, shape, scale):
        return jax.random.normal(k, shape, jnp.float32) * scale

    dt = jnp.exp(jax.random.uniform(ks[13], (N_A_LAYERS, DELTA_HEADS), jnp.float32,
                                    math.log(1e-3), math.log(1e-1)))
    return {
        'x_prompt': nrm(ks[0], (BATCH, SEQ, D_MODEL), 1.0),
        'x_sample': nrm(ks[1], (DEC_BATCH, DEC_SEQ, D_MODEL), 1.0),
        'mem_prompt': nrm(ks[2], (BATCH, N_MEM, D_MODEL), 1.0),
        'cache_mem_k': nrm(ks[3], (DEPTH, DEC_BATCH, N_MEM, X_HEADS, HEAD_DIM), 1.0),
        'cache_mem_v': nrm(ks[4], (DEPTH, DEC_BATCH, N_MEM, X_HEADS, HEAD_DIM), 1.0),
        'state_delta': nrm(ks[5], (N_A_LAYERS, DEC_BATCH, DELTA_HEADS, HEAD_DIM, HEAD_DIM), 0.1),
        'state_conv': nrm(ks[6], (N_A_LAYERS, DEC_BATCH, CONV_W - 1, 3 * DELTA_W), 1.0),
        'norm_gains': 1.0 + nrm(ks[7], (DEPTH, 6, D_MODEL), 0.02),
        'w_ffn_gu': nrm(ks[8], (DEPTH, 2, D_MODEL, 2 * FFN_DIM), D_MODEL ** -0.5),
        'w_ffn_dn': nrm(ks[9], (DEPTH, 2, FFN_DIM, D_MODEL), FFN_DIM ** -0.5),
        'w_in_a': nrm(ks[10], (N_A_LAYERS, D_MODEL, IN_A), D_MODEL ** -0.5),
        'conv_w': nrm(ks[11], (N_A_LAYERS, CONV_W, 3 * DELTA_W), CONV_W ** -0.5),
        'a_log': jnp.log(jax.random.uniform(ks[12], (N_A_LAYERS, DELTA_HEADS), jnp.float32, 1.0, 16.0)),
        'dt_bias': dt + jnp.log(-jnp.expm1(-dt)),
        'delta_norm_gain': 1.0 + nrm(ks[14], (N_A_LAYERS, HEAD_DIM), 0.02),
        'w_in_b': nrm(ks[15], (N_B_LAYERS, D_MODEL, IN_B), D_MODEL ** -0.5),
        'gmlp_ln_gain': 1.0 + nrm(ks[16], (N_B_LAYERS, GMLP_W), 0.02),
        'gmlp_ln_bias': nrm(ks[17], (N_B_LAYERS, GMLP_W), 0.02),
        'w_spatial': nrm(ks[18], (N_B_LAYERS, GMLP_GROUPS, GMLP_CHUNK, GMLP_CHUNK), GMLP_CHUNK ** -0.5),
        'b_spatial': 1.0 + nrm(ks[19], (N_B_LAYERS, GMLP_GROUPS, GMLP_CHUNK), 0.02),
        'mem_norm_gain': 1.0 + nrm(ks[20], (DEPTH, D_MODEL), 0.02),
        'w_mem_kv': nrm(ks[21], (DEPTH, D_MODEL, 2 * X_W), D_MODEL ** -0.5),
        'w_out': nrm(ks[22], (DEPTH, MIX_W, D_MODEL), MIX_W ** -0.5),
    }


def reference(x_prompt, x_sample, mem_prompt, cache_mem_k, cache_mem_v, state_delta, state_conv,
              norm_gains, w_ffn_gu, w_ffn_dn, w_in_a, conv_w, a_log, dt_bias, delta_norm_gain,
              w_in_b, gmlp_ln_gain, gmlp_ln_bias, w_spatial, b_spatial, mem_norm_gain, w_mem_kv, w_out):
    p = {
        'norm_gains': norm_gains, 'w_ffn_gu': w_ffn_gu, 'w_ffn_dn': w_ffn_dn,
        'w_in_a': w_in_a, 'conv_w': conv_w, 'a_log': a_log, 'dt_bias': dt_bias,
        'delta_norm_gain': delta_norm_gain, 'w_in_b': w_in_b, 'gmlp_ln_gain': gmlp_ln_gain,
        'gmlp_ln_bias': gmlp_ln_bias, 'w_spatial': w_spatial, 'b_spatial': b_spatial, 'w_out': w_out,
    }
    mkv = [_mem_kv(mem_prompt, mem_norm_gain[i], w_mem_kv[i]) for i in range(DEPTH)]
    mem_k_prompt = jnp.stack([kv[0] for kv in mkv])
    mem_v_prompt = jnp.stack([kv[1] for kv in mkv])
    conv0 = jnp.zeros((N_A_LAYERS, BATCH, CONV_W - 1, 3 * DELTA_W), x_prompt.dtype)
    delta0 = jnp.zeros((N_A_LAYERS, BATCH, DELTA_HEADS, HEAD_DIM, HEAD_DIM), jnp.float32)
    y_prompt, conv_prompt, delta_prompt, _ = _trunk(x_prompt, mem_k_prompt, mem_v_prompt, conv0, delta0, p)
    y_sample, conv_sample, delta_sample, gmlp_v_sample = _trunk(x_sample, cache_mem_k, cache_mem_v,
                                                                state_conv, state_delta, p)
    return (y_prompt, y_sample, mem_k_prompt, mem_v_prompt, delta_prompt, conv_prompt,
            delta_sample, conv_sample, gmlp_v_sample)
```

```python
import contextlib
import numpy as np
import ml_dtypes
import concourse.bass as bass
import concourse.mybir as mybir
from concourse.bass_utils import run_bass_kernel_spmd

F32 = mybir.dt.float32
BF16 = mybir.dt.bfloat16
AF = mybir.ActivationFunctionType
ALU = mybir.AluOpType

D = 2048
KC = 16
FF = 5632
FC = 44
TP = 1024
TS = 128
T = TP + TS
NH = 12
XH = 4
NMEM = 256
DEPTH = 4
EPS = 1e-6
IN_A = 6680
IN_B = 3584


class Sched:
    ENGS = ("pe", "act", "dve", "pool", "sp")
    NDS = 6

    def __init__(self, nc, es):
        self.nc = nc
        self.ops = {e: [] for e in self.ENGS}
        self.res = {}
        self.waited = {e: {} for e in self.ENGS}
        self.needed = set()
        self.csem = {e: es.enter_context(nc.semaphore("c_" + e)) for e in ("pe", "act", "dve", "pool")}
        self.dsem = {q: [es.enter_context(nc.semaphore("d_%s%d" % (q, i))) for i in range(self.NDS)]
                     for q in ("sp", "pool", "act")}
        self.dcnt = {q: 0 for q in ("sp", "pool", "act")}
        self.dlast = {}

    def _dep(self, eng, ev, waits):
        if ev is None:
            return
        if ev[0] == "c":
            _, pe, idx = ev
            if pe == eng == "pe":
                return
            if pe == eng and idx == len(self.ops[eng]) - 0 - 1 and False:
                return
            key = ("c", pe)
            if self.waited[eng].get(key, -1) >= idx:
                return
            self.waited[eng][key] = idx
            self.needed.add((pe, idx))
            waits.append(ev)
        else:
            _, q, si, val = ev
            key = ("d", q, si)
            if self.waited[eng].get(key, -1) >= val:
                return
            self.waited[eng][key] = val
            waits.append(ev)

    def _track(self, eng, r, w, ev, waits):
        for k in r:
            st = self.res.get(k)
            if st is not None:
                self._dep(eng, st[0], waits)
                if isinstance(k, tuple) and k[0] == "ps":
                    for e2 in st[1]:
                        if e2[0] != "c" or e2[1] != eng:
                            self._dep(eng, e2, waits)
        for k in w:
            st = self.res.get(k)
            if st is not None:
                self._dep(eng, st[0], waits)
                for e2 in st[1]:
                    self._dep(eng, e2, waits)
        for k in r:
            st = self.res.setdefault(k, [None, []])
            st[1].append(ev)
            if len(st[1]) > 24:
                st[1] = st[1][-24:] if False else st[1]
        for k in w:
            self.res[k] = [ev, []]

    def barrier(self):
        bt = self.bar_tile
        self.add("dve", lambda h: h.memset(bt, 0.0), w=["PHASE"])

    def add(self, eng, fn, r=(), w=()):
        r = list(r) + ["PHASE"]
        idx = len(self.ops[eng])
        waits = []
        ev = ("c", eng, idx)
        self._track(eng, r, w, ev, waits)
        self.ops[eng].append(("c", fn, waits, None))

    def dma(self, q, out, in_, r=(), w=()):
        r = list(r) + ["PHASE"]
        n = self.dcnt[q]
        self.dcnt[q] = n + 1
        si = n % self.NDS
        val = 16 * (n // self.NDS + 1)
        waits = []
        if n >= self.NDS:
            self._dep(q, ("d", q, si, val - 16), waits)
        ev = ("d", q, si, val)
        self._track(q, r, w, ev, waits)
        self.dlast[(q, si)] = val
        self.ops[q].append(("d", (out, in_), waits, (q, si)))

    def emit(self, block):
        nc = self.nc
        cnt = {}
        for e in ("pe", "act", "dve", "pool"):
            c = 0
            for i in range(len(self.ops[e])):
                if (e, i) in self.needed:
                    c += 1
                    cnt[(e, i)] = c
        final_waits = [(self.dsem[q][si], v) for (q, si), v in self.dlast.items()]

        def run(eng_name, h):
            for i, (kind, fn, waits, dinfo) in enumerate(self.ops[eng_name]):
                for ev in waits:
                    if ev[0] == "c":
                        h.wait_ge(self.csem[ev[1]], cnt[(ev[1], ev[2])])
                    else:
                        h.wait_ge(self.dsem[ev[1]][ev[2]], ev[3])
                if kind == "c":
                    ins = fn(h)
                    if (eng_name, i) in self.needed:
                        ins.then_inc(self.csem[eng_name], 1)
                else:
                    out, in_ = fn
                    h.dma_start(out=out, in_=in_).then_inc(self.dsem[dinfo[0]][dinfo[1]], 16)
            if eng_name == "sp":
                for s, v in final_waits:
                    h.wait_ge(s, v)

        @block.tensor
        def _(h):
            run("pe", h)

        @block.scalar
        def _(h):
            run("act", h)

        @block.vector
        def _(h):
            run("dve", h)

        @block.gpsimd
        def _(h):
            run("pool", h)

        @block.sync
        def _(h):
            run("sp", h)


def make_consts():
    c = {}
    idx = np.arange(128)
    c["ident_f"] = np.eye(128, dtype=np.float32)
    c["ones_f"] = np.ones((128, 128), np.float32)
    for B in (64, 8):
        nb = 128 // B
        blk = idx // B
        same = blk[:, None] == blk[None, :]
        i = idx[:, None]
        j = idx[None, :]
        c["neg%d" % B] = np.where(same & (j <= i), 0.0, -30000.0).astype(np.float32)
        c["strict%d" % B] = (same & (j < i)).astype(np.float32)
        c["ublk%d" % B] = (same & (i <= j)).astype(np.float32)
        c["bblk%d" % B] = same.astype(np.float32)
        cm = np.zeros((128, nb, 128), np.float32)
        for s in range(nb):
            cm[:, s, s * B:(s + 1) * B] = 1.0
        c["colmask%d" % B] = cm.reshape(128, nb * 128)
        rm = np.zeros((128, nb), np.float32)
        rm[idx, blk] = 1.0
        c["rowmask%d" % B] = rm
    c["tril_t"] = (idx[:, None] <= idx[None, :]).astype(np.float32)
    m8 = np.zeros((128, 16, 8), np.float32)
    for p in range(128):
        s, jj = p // 8, p % 8
        m8[p, s, jj:] = 1.0
    c["mask8"] = m8.reshape(128, 128)
    rep = np.zeros((128, 128), np.float32)
    for p in range(128):
        rep[p % 8, p] = 1.0
    c["rep8"] = rep
    return c


SBS = [(0, 576), (576, 576)]
MSBS = [(0, 512), (512, 512), (1024, 128)]


def blocks_of(sb):
    off, n = sb
    return [(off, n // 2), (off + n // 2, n // 2)]


class Prog:
    def __init__(self, stage=9, nlayers=DEPTH, passes=("A", "B"), feat=("mix", "attn")):
        self.feat = set(feat)
        self.stage = stage
        self.nlayers = nlayers
        self.passes = passes
        nc = self.nc = bass.Bass("TRN2", target_bir_lowering=False)
        es = self.es = contextlib.ExitStack()
        self.S = Sched(nc, es)
        self.consts = make_consts()
        self.build()

    def din(self, name, shape, dt=F32):
        return self.nc.dram_tensor(name, list(shape), dt, kind="ExternalInput").ap()

    def dout(self, name, shape):
        return self.nc.dram_tensor(name, list(shape), F32, kind="ExternalOutput").ap()

    def sb(self, name, shape, dt):
        return self.es.enter_context(self.nc.sbuf_tensor(name, list(shape), dt))

    def build(self):
        nc, S = self.nc, self.S
        I = self.I = {}
        O = self.O = {}
        I["xp"] = self.din("xp", [2 * TP, D])
        I["xs"] = self.din("xs", [2 * TS, D])
        I["mem"] = self.din("mem", [NMEM, D])
        I["ck"] = self.din("ck", [DEPTH, 32, NMEM, 512])
        I["cv"] = self.din("cv", [DEPTH, 32, NMEM, 512])
        I["sdelta"] = self.din("sdelta", [2, 32, NH, 128, 128])
        I["sconv"] = self.din("sconv", [2, 32, 3, 4608])
        I["norm_gains"] = self.din("norm_gains", [DEPTH, 6, D])
        I["w_ffn_gu"] = self.din("w_ffn_gu", [DEPTH, 2, D, 2 * FF])
        I["w_ffn_dn"] = self.din("w_ffn_dn", [DEPTH, 2, FF, D])
        I["w_in_a"] = self.din("w_in_a", [2, D, IN_A])
        I["conv_w"] = self.din("conv_w", [2, 4, 4608])
        I["a_log"] = self.din("a_log", [2, NH])
        I["dt_bias"] = self.din("dt_bias", [2, NH])
        I["delta_norm_gain"] = self.din("delta_norm_gain", [2, 128])
        I["w_in_b"] = self.din("w_in_b", [2, D, IN_B])
        I["gmlp_ln_gain"] = self.din("gmlp_ln_gain", [2, 1536])
        I["gmlp_ln_bias"] = self.din("gmlp_ln_bias", [2, 1536])
        I["w_spatial"] = self.din("w_spatial", [2, NH, 128, 128])
        I["b_spatial"] = self.din("b_spatial", [2, NH, 128])
        I["mem_norm_gain"] = self.din("mem_norm_gain", [DEPTH, D])
        I["w_mem_kv"] = self.din("w_mem_kv", [DEPTH, D, 1024])
        I["w_out"] = self.din("w_out", [DEPTH, D, D])
        for k, v in self.consts.items():
            I["c_" + k] = self.din("c_" + k, v.shape)
        O["yp"] = self.dout("yp", [2 * TP, D])
        O["ys"] = self.dout("ys", [2 * TS, D])
        O["memk"] = self.dout("memk", [DEPTH, NMEM, 512])
        O["memv"] = self.dout("memv", [DEPTH, NMEM, 512])
        O["delta_p"] = self.dout("delta_p", [2, NH, 128, 128])
        O["conv_p"] = self.dout("conv_p", [2, 3, 4608])
        O["delta_s"] = self.dout("delta_s", [2, 32, NH, 128, 128])
        O["conv_s"] = self.dout("conv_s", [2, 32, 3, 4608])
        O["gv_s"] = self.dout("gv_s", [2, 2 * TS, 1536])

        self.XT = self.sb("XT", [128, KC, T], F32)
        self.R1 = self.sb("R1", [128, 19456], BF16)
        self.ARENA = self.sb("ARENA", [128, FC * 576], BF16)
        self.WD = self.sb("WD", [128, 2, FC * 128], BF16)
        self.GAIN = self.sb("GAIN", [128, DEPTH * 6 * KC], F32)
        self.MG = self.sb("MG", [128, DEPTH * KC], F32)
        self.IDF = self.sb("IDF", [128, 128], F32)
        self.IDB = self.sb("IDB", [128, 128], BF16)
        self.ONF = self.sb("ONF", [128, 128], F32)
        self.ONB = self.sb("ONB", [128, 128], BF16)
        self.STAT = self.sb("STAT", [128, 2, 576], F32)
        self.SQ = self.sb("SQ", [128, 2, 576], F32)
        self.EPSC = self.sb("EPSC", [128, 2], F32)
        self.PS = self.es.enter_context(nc.psum_tensor("PS", [128, 8, 512], F32))
        self.ONEC = self.sb("ONEC", [128, 2], F32)
        self.SM = self.sb("SM", [128, 8], F32)
        self.BAR = self.sb("BAR", [128, 2], F32)
        S.bar_tile = self.BAR[:, 0:1]
        m64 = []
        for nm, shp, dt in (("neg64", [128, 128], F32), ("strict64", [128, 128], F32), ("ublk64", [128, 128], F32),
                            ("bblk64", [128, 128], F32), ("colmask64", [128, 256], BF16), ("rowmask64", [128, 2], F32)):
            tl = self.sb("M_" + nm, shp, dt)
            S.dma("pool" if dt == BF16 else "sp", tl[:], I["c_" + nm], w=["M64"])
            m64.append(tl[:])
        self.M64 = m64
        S.add("dve", lambda h: h.memset(self.EPSC[:, 0:1], EPS), w=["EPSC"])
        S.add("dve", lambda h: h.memset(self.ONEC[:, 0:1], 1.0), w=["ONEC"])
        self.wa_i = 0
        self.ps_reserved = set()
        self.wd_i = 0
        self.ps_i = 0

        S.dma("sp", self.IDF[:], I["c_ident_f"], w=["IDF"])
        S.dma("sp", self.ONF[:], I["c_ones_f"], w=["ONF"])
        S.dma("pool", self.IDB[:], I["c_ident_f"], w=["IDB"])
        S.dma("pool", self.ONB[:], I["c_ones_f"], w=["ONB"])
        with nc.allow_non_contiguous_dma(reason="small gain vectors, feature-major"):
            pass
        self.load_featmajor(self.GAIN, I["norm_gains"].rearrange("l s (c p) -> (l s c) p", p=128), DEPTH * 6 * KC, "GAIN")
        self.load_featmajor(self.MG, I["mem_norm_gain"].rearrange("l (c p) -> (l c) p", p=128), DEPTH * KC, "MG")

        self.scrS = nc.dram_tensor("scrS", [2, 128, NH * 128], F32, kind="Internal").ap()
        self.scrCT = nc.dram_tensor("scrCT", [2, 128, 108], F32, kind="Internal").ap()
        for pi, pname in enumerate(self.passes):
            self.pname = pname
            self.first_pass = (pi == 0)
            self.last_pass = (pi == len(self.passes) - 1)
            self.tok0 = 0 if pname == "A" else TP
            self.sq0 = 0 if pname == "A" else 16
            self.has_sample = True
            self.Tp = T if self.has_sample else TP
            sbs = [(0, 576), (576, 576)] if self.has_sample else [(0, 512), (512, 512)]
            msbs = [(0, 512), (512, 512)] + ([(1024, 128)] if self.has_sample else [])
            if pi > 0:
                S.barrier()
            self.load_x()
            for layer in range(self.nlayers):
                for sbi, sb in enumerate(sbs):
                    self.ffn(layer, 0, sb)
                if self.stage >= 2:
                    for sbi, sb in enumerate(msbs):
                        self.mixer(layer, sbi, sb)
                for sbi, sb in enumerate(sbs):
                    self.ffn(layer, 1, sb)
            self.store_y()

        blk = self.es.enter_context(nc.Block())
        S.emit(blk)
        self.es.close()

    def psum(self):
        while True:
            b = self.ps_i % 8
            self.ps_i += 1
            if b not in self.ps_reserved:
                return b

    def load_featmajor(self, dst, src_rows, nrows, key):
        S = self.S
        done = 0
        while done < nrows:
            n = min(128, nrows - done)
            st = self.ARENA[:, 0:256].bitcast(F32)
            S.dma("sp", st[0:n, :], src_rows[done:done + n, :], w=[("AT", 0)])
            b = self.psum()
            S.add("pe", lambda h, b=b, n=n, st=st: h.transpose(self.PS[:, b, 0:n], st[0:n, :], self.IDF[0:n, 0:n]),
                  r=[("AT", 0), "IDF"], w=[("ps", b)])
            S.add("dve", lambda h, b=b, n=n, d0=done: h.tensor_copy(out=dst[:, d0:d0 + n], in_=self.PS[:, b, 0:n]),
                  r=[("ps", b)], w=[key])
            done += n

    def load_x(self):
        S, I = self.S, self.I
        for ti in range(self.Tp // 128):
            src = I["xp"][self.tok0 + ti * 128:self.tok0 + (ti + 1) * 128, :] if ti < 8 else I["xs"][self.sq0 * 8:self.sq0 * 8 + 128, :]
            st = self.WD[:, ti % 2, 0:4096].bitcast(F32)
            key = ("WD", ti % 2)
            S.dma("sp", st, src, w=[key])
            for c4 in range(4):
                b = self.psum()
                for c in range(4):
                    cc = c4 * 4 + c
                    S.add("pe", lambda h, b=b, c=c, cc=cc, st=st: h.transpose(
                        self.PS[:, b, c * 128:(c + 1) * 128], st[:, cc * 128:(cc + 1) * 128], self.IDF[:]),
                        r=[key, "IDF"], w=[("ps", b)])
                S.add("dve" if c4 % 2 == 0 else "act",
                      (lambda h, b=b, c4=c4, ti=ti: h.tensor_copy(
                          out=self.XT[:, c4 * 4:(c4 + 1) * 4, ti * 128:(ti + 1) * 128],
                          in_=self.PS[:, b, :].rearrange("p (c t) -> p c t", c=4))) if c4 % 2 == 0 else
                      (lambda h, b=b, c4=c4, ti=ti: h.activation(
                          out=self.XT[:, c4 * 4:(c4 + 1) * 4, ti * 128:(ti + 1) * 128],
                          in_=self.PS[:, b, :].rearrange("p (c t) -> p c t", c=4), func=AF.Copy)),
                      r=[("ps", b)], w=[("XT", ti)])

    def store_y(self):
        S, O = self.S, self.O
        for ti in range(self.Tp // 128):
            dst = O["yp"][self.tok0 + ti * 128:self.tok0 + (ti + 1) * 128, :] if ti < 8 else O["ys"][self.sq0 * 8:self.sq0 * 8 + 128, :]
            st = self.WD[:, ti % 2, 0:4096].bitcast(F32)
            key = ("WD", ti % 2)
            for c4 in range(4):
                b = self.psum()
                for c in range(4):
                    cc = c4 * 4 + c
                    S.add("pe", lambda h, b=b, c=c, cc=cc, ti=ti: h.transpose(
                        self.PS[:, b, c * 128:(c + 1) * 128], self.XT[:, cc, ti * 128:(ti + 1) * 128], self.IDF[:]),
                        r=[("XT", ti), "IDF"], w=[("ps", b)])
                S.add("dve", lambda h, b=b, c4=c4, st=st: h.tensor_copy(
                    out=st[:, c4 * 512:(c4 + 1) * 512], in_=self.PS[:, b, :]),
                    r=[("ps", b)], w=[key])
            S.dma("sp", dst, st, r=[key])

    def xt_keys(self, off, n):
        return [("XT", t) for t in range(off // 128, (off + n + 127) // 128)]

    def rstd_bcast(self, src_fn, nch, off, n, inv_d, slot, rkeys):
        S = self.S
        cb = [(0, n)] if n <= 512 else [(0, n // 2), (n // 2, n - n // 2)]
        banks = [self.psum() for _ in cb]
        for c in range(nch):
            sqt = self.SQ[:, c % 2, 0:n]
            S.add("pool" if c % 2 else "dve",
                  lambda h, c=c, sqt=sqt: h.tensor_tensor(out=sqt, in0=src_fn(c), in1=src_fn(c), op=ALU.mult),
                  r=rkeys, w=[("SQ", c % 2)])
            for (o, m), b in zip(cb, banks):
                S.add("pe", lambda h, c=c, b=b, sqt=sqt, o=o, m=m: h.matmul(
                    self.PS[:, b, 0:m], self.ONF[:], sqt[:, o:o + m], start=(c == 0), stop=(c == nch - 1)),
                    r=[("SQ", c % 2), "ONF"], w=[("ps", b)])
        st = self.STAT[:, slot, 0:n]
        for (o, m), b in zip(cb, banks):
            S.add("act", lambda h, b=b, o=o, m=m: h.activation(
                out=st[:, o:o + m], in_=self.PS[:, b, 0:m], func=AF.Ln, bias=self.EPSC[:, 0:1], scale=inv_d),
                r=[("ps", b), "EPSC"], w=[("STAT", slot)])
        S.add("act", lambda h: h.activation(out=st, in_=st, func=AF.Exp, scale=-0.5),
              r=[("STAT", slot)], w=[("STAT", slot)])
        return st

    def r1_keys(self, lo, hi):
        return [("R1", p) for p in range(lo // 2048, (hi + 2047) // 2048)]

    def ht_view(self, n):
        return self.R1[:, 0:KC * n].rearrange("p (c t) -> p c t", c=KC), self.r1_keys(0, KC * n)

    def wa_load(self, src, base):
        nslots = (19456 - base) // 2048
        s = self.wa_i % nslots
        self.wa_i += 1
        lo = base + s * 2048
        view = self.R1[:, lo:lo + 2048].rearrange("p (c n) -> p c n", c=KC)
        keys = self.r1_keys(lo, lo + 2048)
        self.S.dma("pool", view, src.rearrange("(c p) n -> p c n", p=128), w=keys)
        return view, keys

    def at_keys(self, lo, hi):
        return [("AT", j) for j in range(lo // 576, (hi + 575) // 576)]

    def prenorm(self, gidx, off, n, hview, hkeys):
        S = self.S
        xk = self.xt_keys(off, n)
        rstd = self.rstd_bcast(lambda c: self.XT[:, c, off:off + n], KC, off, n, 1.0 / D, 0, xk)
        for c in range(KC):
            S.add("dve", lambda h, c=c: h.scalar_tensor_tensor(
                out=hview[:, c, 0:n], in0=self.XT[:, c, off:off + n],
                scalar=self.GAIN[:, gidx * KC + c:gidx * KC + c + 1], in1=rstd, op0=ALU.mult, op1=ALU.mult),
                r=xk + [("STAT", 0), "GAIN"], w=hkeys)

    def postnorm_add(self, gidx, off, n, half):
        S = self.S
        outt = self.R1[:, 0:2 * KC * n].bitcast(F32).rearrange("p (c t) -> p c t", c=KC)
        okeys = self.r1_keys(0, 2 * KC * n)
        xk = self.xt_keys(off, n)
        rstd = self.rstd_bcast(lambda c: outt[:, c, 0:n], KC, off, n, 1.0 / D, 1, okeys)
        if half != 1.0:
            S.add("pool", lambda h: h.tensor_scalar(out=rstd, in0=rstd, scalar1=half, scalar2=None, op0=ALU.mult),
                  r=[("STAT", 1)], w=[("STAT", 1)])
        for c in range(KC):
            S.add("dve", lambda h, c=c: h.scalar_tensor_tensor(
                out=outt[:, c, 0:n], in0=outt[:, c, 0:n],
                scalar=self.GAIN[:, gidx * KC + c:gidx * KC + c + 1], in1=rstd, op0=ALU.mult, op1=ALU.mult),
                r=okeys + [("STAT", 1), "GAIN"], w=okeys)
            S.add("pool", lambda h, c=c: h.tensor_tensor(
                out=self.XT[:, c, off:off + n], in0=self.XT[:, c, off:off + n], in1=outt[:, c, 0:n], op=ALU.add),
                r=okeys + xk, w=xk)

    def ffn(self, layer, which, sb):
        S, I = self.S, self.I
        off, n = sb
        S.barrier()
        blks = blocks_of((0, n))
        g0 = layer * 6 + (0 if which == 0 else 4)
        hview, hkeys = self.ht_view(n)
        self.prenorm(g0, off, n, hview, hkeys)
        wgu = I["w_ffn_gu"][layer, which]
        wdn = I["w_ffn_dn"][layer, which]
        actT = self.ARENA[:, 0:FC * n].rearrange("p (f t) -> p f t", f=FC)
        sil = self.SQ
        for j in range(FC):
            gv, gk = self.wa_load(wgu[:, j * 128:(j + 1) * 128], KC * 576)
            uv, uk = self.wa_load(wgu[:, FF + j * 128:FF + (j + 1) * 128], KC * 576)
            for bi, (bo, bn) in enumerate(blks):
                bg, bu = self.psum(), self.psum()
                for c in range(KC):
                    S.add("pe", lambda h, c=c, bg=bg, gv=gv, bo=bo, bn=bn: h.matmul(
                        self.PS[:, bg, 0:bn], gv[:, c, :], hview[:, c, bo:bo + bn], start=(c == 0), stop=(c == KC - 1)),
                        r=gk + hkeys, w=[("ps", bg)])
                for c in range(KC):
                    S.add("pe", lambda h, c=c, bu=bu, uv=uv, bo=bo, bn=bn: h.matmul(
                        self.PS[:, bu, 0:bn], uv[:, c, :], hview[:, c, bo:bo + bn], start=(c == 0), stop=(c == KC - 1)),
                        r=uk + hkeys, w=[("ps", bu)])
                S.add("act", lambda h, bg=bg, bi=bi, bn=bn: h.activation(
                    out=sil[:, bi, 0:bn], in_=self.PS[:, bg, 0:bn], func=AF.Silu),
                    r=[("ps", bg)], w=[("SQ", bi)])
                S.add("dve", lambda h, bu=bu, bi=bi, bo=bo, bn=bn, j=j: h.tensor_tensor(
                    out=actT[:, j, bo:bo + bn], in0=sil[:, bi, 0:bn], in1=self.PS[:, bu, 0:bn], op=ALU.mult),
                    r=[("ps", bu), ("SQ", bi)], w=[("AT", j)])
        outt = self.R1[:, 0:2 * KC * n].bitcast(F32).rearrange("p (c t) -> p c t", c=KC)
        okeys = self.r1_keys(0, 2 * KC * n)
        atk = [("AT", j) for j in range(FC)]
        for dc in range(KC):
            s = self.wd_i % 2
            self.wd_i += 1
            wv = self.WD[:, s, :].rearrange("p (f n) -> p f n", f=FC)
            S.dma("pool", wv, wdn[:, dc * 128:(dc + 1) * 128].rearrange("(f p) n -> p f n", p=128), w=[("WD", s)])
            for bi, (bo, bn) in enumerate(blks):
                b = self.psum()
                for f in range(FC):
                    S.add("pe", lambda h, f=f, b=b, wv=wv, bo=bo, bn=bn: h.matmul(
                        self.PS[:, b, 0:bn], wv[:, f, :], actT[:, f, bo:bo + bn], start=(f == 0), stop=(f == FC - 1)),
                        r=[("WD", s)] + atk, w=[("ps", b)])
                S.add("act" if bi else "dve",
                      (lambda h, b=b, dc=dc, bo=bo, bn=bn: h.activation(out=outt[:, dc, bo:bo + bn], in_=self.PS[:, b, 0:bn], func=AF.Copy))
                      if bi else
                      (lambda h, b=b, dc=dc, bo=bo, bn=bn: h.tensor_copy(out=outt[:, dc, bo:bo + bn], in_=self.PS[:, b, 0:bn])),
                      r=[("ps", b)], w=okeys)
        self.postnorm_add(g0 + 1, off, n, 0.5)

    def ar(self, lo, n, dt=BF16):
        if dt == F32:
            return self.ARENA[:, lo:lo + 2 * n].bitcast(F32), self.at_keys(lo, lo + 2 * n)
        return self.ARENA[:, lo:lo + n], self.at_keys(lo, lo + n)

    def mem_kv(self, layer):
        S, I, O = self.S, self.I, self.O
        KT = self.WD[:, 0, 0:1024].rearrange("p (h n) -> p h n", h=4)
        V = self.WD[:, 0, 1024:2048].rearrange("p (c f) -> p c f", c=2)
        wdk = [("WD", 0)]
        st, stk = [], []
        for i in range(2):
            v, k = self.ar(10240 + i * 4096, 2048, F32)
            st.append(v)
            stk.append(k)
        mh, mhk = self.ar(18432, 4096)
        mh = mh.rearrange("p (c n) -> p c n", c=KC)
        rs = self.EPSC[:, 1:2]
        for i in range(2):
            S.dma("sp", st[i], I["mem"][i * 128:(i + 1) * 128, :], w=stk[i])
            junk = self.WD[:, 1, 0:4096].bitcast(F32)
            self.tt("pool", junk, st[i], st[i], ALU.mult, stk[i], [("WD", 1)])
            S.add("dve", lambda h, junk=junk: h.tensor_reduce(out=rs, in_=junk, axis=mybir.AxisListType.X, op=ALU.add),
                  r=[("WD", 1)], w=["EPSC2"])
            S.add("act", lambda h: h.activation(out=rs, in_=rs, func=AF.Ln, bias=self.EPSC[:, 0:1], scale=1.0 / D),
                  r=["EPSC2", "EPSC"], w=["EPSC2"])
            S.add("act", lambda h: h.activation(out=rs, in_=rs, func=AF.Exp, scale=-0.5), r=["EPSC2"], w=["EPSC2"])
            S.add("dve", lambda h, i=i: h.tensor_scalar(out=st[i], in0=st[i], scalar1=rs, scalar2=None, op0=ALU.mult),
                  r=stk[i] + ["EPSC2"], w=stk[i])
            for c4 in range(4):
                b = self.psum()
                for c in range(4):
                    cc = c4 * 4 + c
                    S.add("pe", lambda h, b=b, c=c, cc=cc, i=i: h.transpose(
                        self.PS[:, b, c * 128:(c + 1) * 128], st[i][:, cc * 128:(cc + 1) * 128], self.IDF[:]),
                        r=stk[i] + ["IDF"], w=[("ps", b)])
                for c in range(4):
                    cc = c4 * 4 + c
                    S.add("dve", lambda h, b=b, c=c, cc=cc, i=i: h.tensor_scalar(
                        out=mh[:, cc, i * 128:(i + 1) * 128], in0=self.PS[:, b, c * 128:(c + 1) * 128],
                        scalar1=self.MG[:, layer * KC + cc:layer * KC + cc + 1], scalar2=None, op0=ALU.mult),
                        r=[("ps", b), "MG"], w=mhk)
        kvf, kvfk = self.ar(10240, 2048, F32)
        kvf = kvf.rearrange("p (f n) -> p f n", f=8)
        for fc in range(8):
            wv, wk = self.wa_load(I["w_mem_kv"][layer][:, fc * 128:(fc + 1) * 128], 10240)
            b = self.psum()
            for c in range(KC):
                S.add("pe", lambda h, c=c, b=b, wv=wv: h.matmul(self.PS[:, b, 0:256], wv[:, c, :], mh[:, c, :],
                                                                 start=(c == 0), stop=(c == KC - 1)),
                      r=wk + mhk, w=[("ps", b)])
            S.add("dve", lambda h, b=b, fc=fc: h.tensor_copy(out=kvf[:, fc, :], in_=self.PS[:, b, 0:256]),
                  r=[("ps", b)], w=kvfk)
            if fc < 4:
                S.add("act", lambda h, b=b, fc=fc: h.activation(out=KT[:, fc, :], in_=self.PS[:, b, 0:256], func=AF.Copy),
                      r=[("ps", b)], w=wdk)
        tok, tokk = self.ar(14336, 1024, F32)
        for nc_ in range(2):
            for half in range(2):
                b = self.psum()
                for f in range(4):
                    S.add("pe", lambda h, b=b, f=f, half=half, nc_=nc_: h.transpose(
                        self.PS[:, b, f * 128:(f + 1) * 128], kvf[:, half * 4 + f, nc_ * 128:(nc_ + 1) * 128], self.IDF[:]),
                        r=kvfk + ["IDF"], w=[("ps", b)])
                S.add("dve", lambda h, b=b, half=half: h.tensor_copy(out=tok[:, half * 512:(half + 1) * 512], in_=self.PS[:, b, :]),
                      r=[("ps", b)], w=tokk)
                if half == 1:
                    S.add("act", lambda h, b=b, nc_=nc_: h.activation(out=V[:, nc_, :], in_=self.PS[:, b, :], func=AF.Copy),
                          r=[("ps", b)], w=wdk)
            if self.first_pass:
                S.dma("sp", O["memk"][layer, nc_ * 128:(nc_ + 1) * 128, :], tok[:, 0:512], r=tokk)
                S.dma("sp", O["memv"][layer, nc_ * 128:(nc_ + 1) * 128, :], tok[:, 512:1024], r=tokk)

    def mm(self, out, lhsT, rhs, start, stop, r, w):
        self.S.add("pe", lambda h: h.matmul(out, lhsT, rhs, start=start, stop=stop), r=r, w=w)

    def tr(self, out, in_, ident, r, w):
        self.S.add("pe", lambda h: h.transpose(out, in_, ident), r=r, w=w)

    def ts(self, eng, out, in0, s1, s2, op0, op1, r, w):
        if op1 is None:
            self.S.add(eng, lambda h: h.tensor_scalar(out=out, in0=in0, scalar1=s1, scalar2=None, op0=op0), r=r, w=w)
        else:
            self.S.add(eng, lambda h: h.tensor_scalar(out=out, in0=in0, scalar1=s1, scalar2=s2, op0=op0, op1=op1), r=r, w=w)

    def tt(self, eng, out, in0, in1, op, r, w):
        self.S.add(eng, lambda h: h.tensor_tensor(out=out, in0=in0, in1=in1, op=op), r=r, w=w)

    def stt(self, out, in0, scalar, in1, op0, op1, r, w):
        self.S.add("dve", lambda h: h.scalar_tensor_tensor(out=out, in0=in0, scalar=scalar, in1=in1, op0=op0, op1=op1), r=r, w=w)

    def af(self, out, in_, func, r, w, bias=None, scale=None):
        kw = {}
        if bias is not None:
            kw["bias"] = bias
        if scale is not None:
            kw["scale"] = scale
        self.S.add("act", lambda h: h.activation(out=out, in_=in_, func=func, **kw), r=r, w=w)

    def cp(self, eng, out, in_, r, w):
        if eng == "act":
            self.af(out, in_, AF.Copy, r, w)
        else:
            self.S.add(eng, lambda h: h.tensor_copy(out=out, in_=in_), r=r, w=w)

    def psb(self, b):
        return self.PS[:, b, :].bitcast(BF16)

    def sc_reset(self):
        self.sc_off = 8192
        self.sc2_off = 2048
        self.sc3_off = 18432
        self.sc4_off = 0

    def sc(self, name, n, dt=BF16, pool=0):
        ne = n * (2 if dt == F32 else 1)
        ne = (ne + 15) // 16 * 16
        if pool == 0:
            lo = self.sc_off
            self.sc_off += ne
            assert self.sc_off <= 25344, ("ARENA scratch overflow", name, self.sc_off)
            v = self.ARENA[:, lo:lo + ne]
        elif pool == 1:
            lo = self.sc2_off
            self.sc2_off += ne
            assert self.sc2_off <= 8192, ("R1 scratch overflow", name, self.sc2_off)
            v = self.R1[:, lo:lo + ne]
        elif pool == 2:
            lo = self.sc3_off
            self.sc3_off += ne
            assert self.sc3_off <= 19456, ("R1 tail overflow", name, self.sc3_off)
            v = self.R1[:, lo:lo + ne]
        else:
            lo = self.sc4_off
            self.sc4_off += ne
            assert self.sc4_off <= 5120, ("WD0 scratch overflow", name, self.sc4_off)
            v = self.WD[:, 0, lo:lo + ne]
        if dt == F32:
            v = v.bitcast(F32)
        return v[:, 0:n], [("sc", name)]

    def rows_to_featmajor(self, dst, dkeys, src_rows, nrows, stage, skeys):
        S = self.S
        S.dma("sp", stage[0:nrows, :], src_rows, w=skeys)
        b = self.psum()
        self.tr(self.PS[:, b, 0:nrows], stage[0:nrows, :], self.IDF[0:nrows, 0:nrows], skeys + ["IDF"], [("ps", b)])
        self.cp("dve", dst, self.PS[:, b, 0:nrows], [("ps", b)], dkeys)

    def mixer(self, layer, sbi, sb, commit=True):
        S, I, O = self.S, self.I, self.O
        off, n = sb
        nt = n // 128
        sample = (off >= TP)
        isA = (layer % 2 == 0)
        j = layer // 2
        S.barrier()
        self.sc_reset()
        if sbi == 0:
            self.mem_kv(layer)
            S.barrier()
            self.sc_reset()
        hview, hkeys = self.ht_view(n)
        self.prenorm(layer * 6 + 2, off, n, hview, hkeys)
        mixt = self.ARENA[:, 0:KC * n].rearrange("p (c t) -> p c t", c=KC)
        mk = lambda c: [("MIXT", c)]
        ctx = dict(layer=layer, j=j, off=off, n=n, nt=nt, sample=sample, hview=hview, hkeys=hkeys, mixt=mixt, mk=mk,
                   sbi=sbi, commit=commit)
        if "mix" in self.feat:
            if isA:
                self.delta_sb(ctx)
            else:
                self.gmlp_sb(ctx)
        else:
            for c in range(12):
                S.add("pool", lambda h, c=c: h.memset(mixt[:, c, :], 0.0), w=mk(c))
        if "attn" in self.feat:
            self.attn_sb(ctx)
        else:
            for c in range(12, 16):
                S.add("pool", lambda h, c=c: h.memset(mixt[:, c, :], 0.0), w=mk(c))
        self.outproj_sb(ctx)

    def outproj_sb(self, ctx):
        S, I = self.S, self.I
        layer, off, n, mixt, mk = ctx["layer"], ctx["off"], ctx["n"], ctx["mixt"], ctx["mk"]
        S.barrier()
        outt = self.R1[:, 0:2 * KC * n].bitcast(F32).rearrange("p (c t) -> p c t", c=KC)
        okeys = self.r1_keys(0, 2 * KC * n)
        blks = blocks_of((0, n)) if n > 256 else [(0, n)]
        wo = I["w_out"][layer]
        allmk = [("MIXT", c) for c in range(KC)]
        for dc in range(KC):
            s = dc % 2
            lo = 1536 + s * 2048
            wv = self.WD[:, 1, lo:lo + 2048].rearrange("p (c n) -> p c n", c=KC)
            wk = [("WO", s)]
            S.dma("pool", wv, wo[:, dc * 128:(dc + 1) * 128].rearrange("(c p) n -> p c n", p=128), w=wk)
            for bi, (bo, bn) in enumerate(blks):
                b = self.psum()
                for c in range(KC):
                    self.mm(self.PS[:, b, 0:bn], wv[:, c, :], mixt[:, c, bo:bo + bn], c == 0, c == KC - 1,
                            wk + allmk, [("ps", b)])
                self.cp("act" if bi else "dve", outt[:, dc, bo:bo + bn], self.PS[:, b, 0:bn], [("ps", b)], okeys)
        self.postnorm_add(layer * 6 + 3, off, n, 1.0)

    def attn_sb(self, ctx):
        S, I = self.S, self.I
        layer, off, n, nt, sample = ctx["layer"], ctx["off"], ctx["n"], ctx["nt"], ctx["sample"]
        hview, hkeys, mixt, mk = ctx["hview"], ctx["hkeys"], ctx["mixt"], ctx["mk"]
        S.barrier()
        self.sc_reset()
        isA = (layer % 2 == 0)
        w_in = I["w_in_a"][layer // 2] if isA else I["w_in_b"][layer // 2]
        qcol0 = (IN_A - 512) if isA else (IN_B - 512)
        qT, qk = self.sc("qT", 4 * n)
        qT = qT.rearrange("p (h t) -> p h t", h=4)
        blks = blocks_of((0, n)) if n > 256 else [(0, n)]
        for hx in range(4):
            wv, wk = self.wa_load(w_in[:, qcol0 + hx * 128:qcol0 + (hx + 1) * 128], 8192)
            for bi, (bo, bn) in enumerate(blks):
                b = self.psum()
                for c in range(KC):
                    self.mm(self.PS[:, b, 0:bn], wv[:, c, :], hview[:, c, bo:bo + bn], c == 0, c == KC - 1,
                            wk + hkeys, [("ps", b)])
                self.cp("act", qT[:, hx, bo:bo + bn], self.PS[:, b, 0:bn], [("ps", b)], qk)
        P, Pk = self.sc("P", 512, F32)
        Pn, Pnk = self.sc("Pn", 512)
        PT, PTk = self.sc("PT", 512)
        sm, smk = self.sc("sm", 8, F32)
        P3 = P.rearrange("p (h n) -> p h n", h=2)
        Pn3 = Pn.rearrange("p (h n) -> p h n", h=2)
        scale = 128.0 ** -0.5

        def attend(col0, m, KT, V, kvk):
            for hp in range(2):
                b = self.psum()
                for hh in range(2):
                    self.mm(self.PS[0:m, b, hh * 256:(hh + 1) * 256], qT[:, 2 * hp + hh, col0:col0 + m], KT[:, 2 * hp + hh, :],
                            True, True, qk + kvk, [("ps", b)])
                ps3 = self.PS[0:m, b, :].rearrange("p (h n) -> p h n", h=2)
                S.add("dve", lambda h, ps3=ps3, m=m: h.tensor_reduce(out=sm[0:m, 0:2], in_=ps3, axis=mybir.AxisListType.X, op=ALU.max),
                      r=[("ps", b)], w=smk)
                self.tt("dve", P3[0:m], ps3, sm[0:m, 0:2].unsqueeze(2).broadcast_to([m, 2, 256]), ALU.subtract,
                        [("ps", b)] + smk, Pk)
                self.af(P[0:m], P[0:m], AF.Exp, Pk, Pk, scale=scale)
                S.add("dve", lambda h, m=m: h.tensor_reduce(out=sm[0:m, 2:4], in_=P3[0:m], axis=mybir.AxisListType.X, op=ALU.add),
                      r=Pk, w=smk)
                S.add("dve", lambda h, m=m: h.reciprocal(out=sm[0:m, 4:6], in_=sm[0:m, 2:4]), r=smk, w=smk)
                self.tt("dve", Pn3[0:m], P3[0:m], sm[0:m, 4:6].unsqueeze(2).broadcast_to([m, 2, 256]), ALU.mult,
                        Pk + smk, Pnk)
                b2 = self.psum()
                pb = self.psb(b2)
                for hh in range(2):
                    for ncnk in range(2):
                        q = hh * 2 + ncnk
                        self.tr(pb[:, q * m:(q + 1) * m], Pn3[0:m, hh, ncnk * 128:(ncnk + 1) * 128], self.IDB[0:m, 0:m],
                                Pnk + ["IDB"], [("ps", b2)])
                self.cp("act", PT[:, 0:4 * m], pb[:, 0:4 * m], [("ps", b2)], PTk)
                b3 = self.psum()
                for hh in range(2):
                    for ncnk in range(2):
                        q = hh * 2 + ncnk
                        hd = 2 * hp + hh
                        self.mm(self.PS[:, b3, hh * m:(hh + 1) * m], V[:, ncnk, hd * 128:(hd + 1) * 128], PT[:, q * m:(q + 1) * m],
                                ncnk == 0, ncnk == 1, PTk + kvk, [("ps", b3)])
                self.cp("dve", mixt[:, 12 + 2 * hp:12 + 2 * hp + 2, col0:col0 + m],
                        self.PS[:, b3, 0:2 * m].rearrange("p (h t) -> p h t", h=2), [("ps", b3)],
                        [("MIXT", 12 + 2 * hp), ("MIXT", 13 + 2 * hp)])

        if not sample:
            KT = self.WD[:, 0, 0:1024].rearrange("p (h n) -> p h n", h=4)
            V = self.WD[:, 0, 1024:2048].rearrange("p (c f) -> p c f", c=2)
            for t in range(nt):
                attend(t * 128, 128, KT, V, [("WD", 0)])
        else:
            bufs = []
            for i in range(2):
                kt_, ktk = self.sc("Ktok%d" % i, 1024)
                v_, vk = self.sc("Vs%d" % i, 1024)
                kT_, kTk = self.sc("KTs%d" % i, 1024)
                bufs.append((kt_, ktk, v_, vk, kT_, kTk))
            for s in range(16):
                kt_, ktk, v_, vk, kT_, kTk = bufs[s % 2]
                kt3 = kt_.rearrange("p (c f) -> p c f", c=2)
                v3 = v_.rearrange("p (c f) -> p c f", c=2)
                kT3 = kT_.rearrange("p (h n) -> p h n", h=4)
                S.dma("pool", kt3, I["ck"][layer, self.sq0 + s].rearrange("(c p) f -> p c f", p=128), w=ktk)
                S.dma("pool", v3, I["cv"][layer, self.sq0 + s].rearrange("(c p) f -> p c f", p=128), w=vk)
                b = self.psum()
                pb = self.psb(b)
                for hd in range(4):
                    for ncnk in range(2):
                        self.tr(pb[:, hd * 256 + ncnk * 128:hd * 256 + (ncnk + 1) * 128], kt3[:, ncnk, hd * 128:(hd + 1) * 128],
                                self.IDB[:], ktk + ["IDB"], [("ps", b)])
                self.cp("act", kT_, pb[:, 0:1024], [("ps", b)], kTk)
                attend(s * 8, 8, kT3, v3, kTk + vk)

    def gmlp_sb(self, ctx):
        S, I, O = self.S, self.I, self.O
        j, off, n, nt, sample = ctx["j"], ctx["off"], ctx["n"], ctx["nt"], ctx["sample"]
        hview, hkeys, mixt, mk = ctx["hview"], ctx["hkeys"], ctx["mixt"], ctx["mk"]
        w_in = I["w_in_b"][j]
        blks = blocks_of((0, n)) if n > 256 else [(0, n)]
        G = 12
        VG, VGk = self.sc("VG", G * n, F32)
        VG = VG.rearrange("p (g t) -> p g t", g=G)
        LG, LGk = self.sc("LG", 16, F32)
        LB, LBk = self.sc("LB", 16, F32)
        stg, stgk = self.sc("wsf", 128, F32)
        self.rows_to_featmajor(LG[:, 0:G], LGk, I["gmlp_ln_gain"][j].rearrange("(g p) -> g p", p=128), G, stg, stgk)
        self.rows_to_featmajor(LB[:, 0:G], LBk, I["gmlp_ln_bias"][j].rearrange("(g p) -> g p", p=128), G, stg, stgk)
        BROW, BRk = self.sc("BROW", 1536)
        S.dma("pool", BROW[0:1, :], I["b_spatial"][j:j + 1].rearrange("a g i -> a (g i)"), w=BRk)
        sq, sqk = self.sc("tmpA", n, F32)
        bsum = [self.psum() for _ in blks]
        bsq = [self.psum() for _ in blks]
        self.ps_reserved = set(bsum + bsq)
        for g in range(G):
            wv, wk = self.wa_load(w_in[:, 1536 + g * 128:1536 + (g + 1) * 128], 8192)
            for bi, (bo, bn) in enumerate(blks):
                b = self.psum()
                for c in range(KC):
                    self.mm(self.PS[:, b, 0:bn], wv[:, c, :], hview[:, c, bo:bo + bn], c == 0, c == KC - 1, wk + hkeys, [("ps", b)])
                self.af(VG[:, g, bo:bo + bn], self.PS[:, b, 0:bn], AF.Gelu, [("ps", b)], VGk)
            self.tt("pool", sq, VG[:, g, :], VG[:, g, :], ALU.mult, VGk, sqk)
            for bi, (bo, bn) in enumerate(blks):
                self.mm(self.PS[:, bsum[bi], 0:bn], self.ONF[:], VG[:, g, bo:bo + bn], g == 0, g == G - 1, VGk + ["ONF"], [("ps", bsum[bi])])
                self.mm(self.PS[:, bsq[bi], 0:bn], self.ONF[:], sq[:, bo:bo + bn], g == 0, g == G - 1, sqk + ["ONF"], [("ps", bsq[bi])])
        mean = self.STAT[:, 0, 0:n]
        rstd = self.STAT[:, 1, 0:n]
        for bi, (bo, bn) in enumerate(blks):
            self.ts("dve", mean[:, bo:bo + bn], self.PS[:, bsum[bi], 0:bn], 1.0 / 1536, None, ALU.mult, None, [("ps", bsum[bi])], [("STAT", 0)])
            self.tt("dve", sq[:, bo:bo + bn], mean[:, bo:bo + bn], mean[:, bo:bo + bn], ALU.mult, [("STAT", 0)], sqk)
            self.stt(rstd[:, bo:bo + bn], self.PS[:, bsq[bi], 0:bn], 1.0 / 1536, sq[:, bo:bo + bn], ALU.mult, ALU.subtract,
                     [("ps", bsq[bi])] + sqk, [("STAT", 1)])
        self.af(rstd, rstd, AF.Ln, [("STAT", 1), "EPSC"], [("STAT", 1)], bias=self.EPSC[:, 0:1], scale=1.0)
        self.af(rstd, rstd, AF.Exp, [("STAT", 1)], [("STAT", 1)], scale=-0.5)
        self.ps_reserved = set()
        vn, vnk = self.sc("vn", n, F32)
        vnb, vnbk = self.sc("vnb", n, BF16, 2)
        ug, ugk = sq, sqk
        wsf, wsfk = stg, stgk
        wsb, wsbk = self.sc("wsb", 128, BF16, 2)
        vtok, vtokk = self.sc("vtok", 128, BF16, 2)
        if sample:
            m8, m8k = self.sc("m8", 128, F32)
            rep, repk = self.sc("rep", 128, F32)
            S.dma("sp", m8, I["c_mask8"], w=m8k)
            S.dma("sp", rep, I["c_rep8"], w=repk)
            wst, wstk = self.sc("wst", 128, F32)
            wrep, wrepk = self.sc("wrep", 8, F32)
            brow8, br8k = self.sc("brow8", 128)
            gvst, gvstk = self.sc("gvst", 1536, F32)
        else:
            trl, trlk = self.sc("trl", 128, F32, 2)
            S.dma("sp", trl, I["c_tril_t"], w=trlk)
        for g in range(G):
            self.tt("dve", vn, VG[:, g, :], mean, ALU.subtract, VGk + [("STAT", 0)], vnk)
            self.tt("pool", vn, vn, rstd, ALU.mult, vnk + [("STAT", 1)], vnk)
            self.ts("dve", vn, vn, LG[:, g:g + 1], LB[:, g:g + 1], ALU.mult, ALU.add, vnk + LGk + LBk, vnk)
            self.cp("act", vnb, vn, vnk, vnbk)
            S.dma("sp", wsf, I["w_spatial"][j, g], w=wsfk)
            b = self.psum()
            self.tr(self.PS[:, b, 0:128], wsf, self.IDF[:], wsfk + ["IDF"], [("ps", b)])
            wv, wk = self.wa_load(w_in[:, g * 128:(g + 1) * 128], 8192)
            for bi, (bo, bn) in enumerate(blks):
                bu = self.psum()
                for c in range(KC):
                    self.mm(self.PS[:, bu, 0:bn], wv[:, c, :], hview[:, c, bo:bo + bn], c == 0, c == KC - 1, wk + hkeys, [("ps", bu)])
                self.af(ug[:, bo:bo + bn], self.PS[:, bu, 0:bn], AF.Gelu, [("ps", bu)], ugk)
            if not sample:
                self.tt("dve", wsb, self.PS[:, b, 0:128], trl, ALU.mult, [("ps", b)] + trlk, wsbk)
                for t in range(nt):
                    b2 = self.psum()
                    pb = self.psb(b2)
                    self.tr(pb[:, 0:128], vnb[:, t * 128:(t + 1) * 128], self.IDB[:], vnbk + ["IDB"], [("ps", b2)])
                    self.cp("act", vtok, pb[:, 0:128], [("ps", b2)], vtokk)
                    b3 = self.psum()
                    self.mm(self.PS[:, b3, 0:128], vtok, wsb, True, False, vtokk + wsbk, [("ps", b3)])
                    self.mm(self.PS[:, b3, 0:128], self.ONB[0:1, :], BROW[0:1, g * 128:(g + 1) * 128], False, True,
                            BRk + ["ONB"], [("ps", b3)])
                    self.tt("dve", mixt[:, g, t * 128:(t + 1) * 128], ug[:, t * 128:(t + 1) * 128], self.PS[:, b3, 0:128], ALU.mult,
                            ugk + [("ps", b3)], mk(g))
            else:
                self.cp("dve", wst, self.PS[:, b, 0:128], [("ps", b)], wstk)
                b4 = self.psum()
                self.mm(self.PS[:, b4, 0:8], rep, wst[:, 0:8], True, True, repk + wstk, [("ps", b4)])
                self.cp("dve", wrep, self.PS[:, b4, 0:8], [("ps", b4)], wrepk)
                self.tt("dve", wsb.rearrange("p (s i) -> p s i", s=16), wrep.unsqueeze(1).broadcast_to([128, 16, 8]),
                        m8.rearrange("p (s i) -> p s i", s=16), ALU.mult, wrepk + m8k, wsbk)
                self.cp("dve", brow8[0:1, :].rearrange("p (s i) -> p s i", s=16),
                        BROW[0:1, g * 128:g * 128 + 8].unsqueeze(1).broadcast_to([1, 16, 8]), BRk, br8k)
                b2 = self.psum()
                pb = self.psb(b2)
                self.tr(pb[:, 0:128], vnb[:, 0:128], self.IDB[:], vnbk + ["IDB"], [("ps", b2)])
                self.cp("act", vtok, pb[:, 0:128], [("ps", b2)], vtokk)
                b3 = self.psum()
                self.mm(self.PS[:, b3, 0:128], vtok, wsb, True, False, vtokk + wsbk, [("ps", b3)])
                self.mm(self.PS[:, b3, 0:128], self.ONB[0:1, :], brow8[0:1, :], False, True, br8k + ["ONB"], [("ps", b3)])
                self.tt("dve", mixt[:, g, 0:128], ug[:, 0:128], self.PS[:, b3, 0:128], ALU.mult, ugk + [("ps", b3)], mk(g))
                b5 = self.psum()
                self.tr(self.PS[:, b5, 0:128], vn[:, 0:128], self.IDF[:], vnk + ["IDF"], [("ps", b5)])
                self.cp("act", gvst[:, g * 128:(g + 1) * 128], self.PS[:, b5, 0:128], [("ps", b5)], gvstk)
        if sample:
            S.dma("sp", O["gv_s"][j, self.sq0 * 8:self.sq0 * 8 + 128, :], gvst, r=gvstk)

    def delta_sb(self, ctx):
        S, I, O = self.S, self.I, self.O
        j, off, n, nt, sample, sbi = ctx["j"], ctx["off"], ctx["n"], ctx["nt"], ctx["sample"], ctx["sbi"]
        hview, hkeys, mixt, mk, commit = ctx["hview"], ctx["hkeys"], ctx["mixt"], ctx["mk"], ctx["commit"]
        w_in = I["w_in_a"][j]
        B = 8 if sample else 64
        nb = 128 // B
        nlev = 2 if sample else 5
        nseq, L = (16, 8) if sample else (1, n)
        blks = blocks_of((0, n)) if n > 256 else [(0, n)]
        Sf = self.WD[:, 0, 2048:5120].bitcast(F32).rearrange("p (h v) -> p h v", h=NH)
        Sb = self.WD[:, 1, 0:1536].rearrange("p (h v) -> p h v", h=NH)
        CT = self.WD[:, 0, 5120:5336].bitcast(F32).rearrange("p (c r) -> p c r", c=36)
        Sfk = lambda h: [("Sf", h)]
        Sbk = lambda h: [("Sb", h)]
        CTk = ["CT"]
        allSf = [("Sf", h) for h in range(NH)]
        allSb = [("Sb", h) for h in range(NH)]
        if sbi == 0 and self.first_pass:
            S.add("pool", lambda h: h.memset(self.WD[:, 0, 2048:5120].bitcast(F32), 0.0), w=allSf)
            S.add("pool", lambda h: h.memset(self.WD[:, 1, 0:1536], 0.0), w=allSb)
            S.add("pool", lambda h: h.memset(self.WD[:, 0, 5120:5336].bitcast(F32), 0.0), w=CTk)
        elif sbi == 0:
            S.dma("sp", self.WD[:, 0, 2048:5120].bitcast(F32), self.scrS[j], r=[("scrS", j)], w=allSf)
            S.dma("pool", self.WD[:, 1, 0:1536], self.scrS[j], r=[("scrS", j)], w=allSb)
            S.dma("sp", self.WD[:, 0, 5120:5336].bitcast(F32), self.scrCT[j], r=[("scrCT", j)], w=CTk)
        if sample:
            NEG, NEGk = self.sc("NEG8", 128, F32)
            STR, STRk = self.sc("STR8", 128, F32)
            UBL, UBLk = self.sc("UBL8", 128, F32)
            BBL, BBLk = self.sc("BBL8", 128, F32)
            COLM, COLMk = self.sc("COLM8", nb * 128)
            ROWM, ROWMk = self.sc("ROWM8", nb, F32)
            for v_, k_, nm in ((NEG, NEGk, "neg8"), (STR, STRk, "strict8"), (UBL, UBLk, "ublk8"), (BBL, BBLk, "bblk8"),
                               (ROWM, ROWMk, "rowmask8")):
                S.dma("sp", v_, I["c_" + nm], w=k_)
            S.dma("pool", COLM, I["c_colmask8"], w=COLMk)
        else:
            NEG, STR, UBL, BBL, COLM, ROWM = self.M64
            NEGk = STRk = UBLk = BBLk = COLMk = ROWMk = ["M64"]
        COLM3 = COLM.rearrange("p (s t) -> p s t", s=nb)
        stg, stgk = self.sc("stg", 128, F32)
        CW = []
        for r_ in range(4):
            cw, cwk = self.sc("CW%d" % r_, 36, F32)
            self.rows_to_featmajor(cw, cwk, I["conv_w"][j, r_].rearrange("(c p) -> c p", p=128), 36, stg, stgk)
            CW.append((cw, cwk))
        DNG, DNGk = self.sc("DNG", 1, F32)
        self.rows_to_featmajor(DNG, DNGk, I["delta_norm_gain"][j:j + 1, :], 1, stg, stgk)
        DTB, DTBk = self.sc("DTB", NH, F32)
        NEGA, NEGAk = self.sc("NEGA", NH, F32)
        S.dma("sp", DTB, I["dt_bias"][j:j + 1, :].broadcast_to([128, NH]), w=DTBk)
        S.dma("sp", NEGA, I["a_log"][j:j + 1, :].broadcast_to([128, NH]), w=NEGAk)
        self.af(NEGA, NEGA, AF.Exp, NEGAk, NEGAk)
        self.ts("dve", NEGA, NEGA, -1.0, None, ALU.mult, None, NEGAk, NEGAk)
        Wab, Wabk = self.sc("Wab", KC * 24)
        Wab = Wab.rearrange("p (c n) -> p c n", c=KC)
        S.dma("pool", Wab, w_in[:, 6144:6168].rearrange("(c p) n -> p c n", p=128), w=Wabk)

        def scal(name):
            v_, k_ = self.sc(name, nt * NH, F32)
            return v_.rearrange("p (t h) -> p t h", t=nt), k_
        BETA, BETAk = scal("BETA")
        G, Gk = scal("G")
        NG, NGk = scal("NG")
        GC, GCk = scal("GC")
        EDEC, EDECk = scal("EDEC")
        EGC, EGCk = scal("EGC")
        BE, BEk = scal("BE")
        GLB, GLBk = self.sc("GLB", nt * nb * NH, F32)
        GLB = GLB.rearrange("p (t s h) -> p t s h", t=nt, s=nb)
        G2, G2k = self.sc("G2", nb * NH, F32)
        for t in range(nt):
            b = self.psum()
            for c in range(KC):
                self.mm(self.PS[:, b, 0:24], hview[:, c, t * 128:(t + 1) * 128], Wab[:, c, :], c == 0, c == KC - 1,
                        hkeys + Wabk, [("ps", b)])
            self.af(BETA[:, t, :], self.PS[:, b, 12:24], AF.Sigmoid, [("ps", b)], BETAk)
            self.tt("dve", G[:, t, :], self.PS[:, b, 0:12], DTB, ALU.add, [("ps", b)] + DTBk, Gk)
            self.af(G[:, t, :], G[:, t, :], AF.Exp, Gk, Gk)
            self.af(G[:, t, :], G[:, t, :], AF.Ln, Gk + ["ONEC"], Gk, bias=self.ONEC[:, 0:1], scale=1.0)
            self.tt("dve", G[:, t, :], G[:, t, :], NEGA, ALU.mult, Gk + NEGAk, Gk)
            self.ts("dve", NG[:, t, :], G[:, t, :], -1.0, None, ALU.mult, None, Gk, NGk)
            b1 = self.psum()
            self.mm(self.PS[:, b1, 0:12], UBL, G[:, t, :], True, True, UBLk + Gk, [("ps", b1)])
            self.mm(self.PS[:, b1, 16:28], BBL, G[:, t, :], True, True, BBLk + Gk, [("ps", b1)])
            self.cp("dve", GC[:, t, :], self.PS[:, b1, 0:12], [("ps", b1)], GCk)
            self.af(EGC[:, t, :], self.PS[:, b1, 0:12], AF.Exp, [("ps", b1)], EGCk)
            self.tt("dve", EDEC[:, t, :], self.PS[:, b1, 16:28], GC[:, t, :], ALU.subtract, [("ps", b1)] + GCk, EDECk)
            self.af(EDEC[:, t, :], EDEC[:, t, :], AF.Exp, EDECk, EDECk)
            self.tt("dve", BE[:, t, :], BETA[:, t, :], EGC[:, t, :], ALU.mult, BETAk + EGCk, BEk)
            self.tt("dve", G2.rearrange("p (s h) -> p s h", s=nb), G[:, t, :].unsqueeze(1).broadcast_to([128, nb, NH]),
                    ROWM.unsqueeze(2).broadcast_to([128, nb, NH]), ALU.mult, Gk + ROWMk, G2k)
            b2 = self.psum()
            self.mm(self.PS[:, b2, 0:nb * NH], self.ONF[:], G2, True, True, G2k + ["ONF"], [("ps", b2)])
            self.af(GLB[:, t].rearrange("p s h -> p (s h)"), self.PS[:, b2, 0:nb * NH], AF.Exp, [("ps", b2)], GLBk)
        if sample:
            SCT = self.WD[:, 1, 1536:1536 + 2 * 36 * 48].bitcast(F32).rearrange("p (c q) -> p c q", c=36)
            SCTk = ["SCT"]
            cst, cstk = self.sc("cst", 512, F32)
            src = I["sconv"][j, self.sq0:self.sq0 + 16].rearrange("s r f -> (s r) f")
            for c9 in range(9):
                S.dma("sp", cst[0:48, :], src[:, c9 * 512:(c9 + 1) * 512], w=cstk)
                b = self.psum()
                for c in range(4):
                    self.tr(self.PS[:, b, c * 48:(c + 1) * 48], cst[0:48, c * 128:(c + 1) * 128], self.IDF[0:48, 0:48],
                            cstk + ["IDF"], [("ps", b)])
                self.cp("dve", SCT[:, c9 * 4:(c9 + 1) * 4, :], self.PS[:, b, 0:192].rearrange("p (c q) -> p c q", c=4),
                        [("ps", b)], SCTk)
        W3 = L + 3
        PQ = []
        for part in range(3):
            v_, k_ = self.sc("PQ%d" % part, nseq * W3 + 8, F32)
            PQ.append((v_[:, 0:nseq * W3].rearrange("p (s w) -> p s w", s=nseq), k_))
        YC, YCk = self.sc("YC", n, F32)
        YC3 = YC.rearrange("p (s l) -> p s l", s=nseq)
        QT, QTk = self.sc("QT", n)
        KTb, KTbk = self.sc("KTb", n)
        VTb, VTbk = self.sc("VTf", n, F32)
        KTf, KTfk = self.sc("KTf", n, F32)
        SZ, SZk = self.sc("SZ", n, F32)
        if sample:
            S0f, S0fk = self.sc("S0f", 16 * 128, F32, 1)
            S0b, S0bk = self.sc("S0b", 16 * 128, BF16, 3)
            S0f3 = S0f.rearrange("p (s v) -> p s v", s=16)
            S0b3 = S0b.rearrange("p (s v) -> p s v", s=16)
        nbuf = 1
        TB = []
        for i in range(nbuf):
            d = {}
            for nm, sz, dt in (("KBE", 128, F32), ("KDEC", 128, BF16), ("VB", 128, F32), ("E", 128, F32), ("L1", 128, F32),
                               ("Lb0", 128, F32), ("Rb0", 128, F32), ("Lb1", 128, F32), ("Rb1", 128, F32),
                               ("Ab", 128, BF16), ("ATb", 128, BF16), ("Xf", 128, F32), ("U", 128, F32),
                               ("WT", 128, BF16), ("QD", 128, BF16), ("GUN", 128, F32),
                               ("WM", nb * 128, BF16), ("QM", nb * 128, BF16), ("KDM", nb * 128, BF16),
                               ("VN", 128, BF16), ("O", 128, F32), ("ON", 128, F32)):
                pl = 0
                if sample and nm == "KDM":
                    pl = 3
                if sample and nm == "QM":
                    pl = 1
                d[nm] = self.sc("%s_%d" % (nm, i), sz, dt, pl)
            if not sample:
                d["ATM"] = self.sc("ATM_%d" % i, nb * 128, BF16)
            TB.append(d)

        for h in range(NH):
            for part in range(3):
                cidx = part * NH + h
                pq, pqk = PQ[part]
                wv, wk = self.wa_load(w_in[:, part * 1536 + h * 128:part * 1536 + (h + 1) * 128], 8192)
                if sample:
                    self.cp("pool", pq[:, :, 0:3], SCT[:, cidx, :].rearrange("p (s r) -> p s r", s=16), SCTk, pqk)
                else:
                    self.cp("pool", pq[:, 0, 0:3], CT[:, cidx, :], CTk, pqk)
                for bi, (bo, bn) in enumerate(blks):
                    b = self.psum()
                    for c in range(KC):
                        self.mm(self.PS[:, b, 0:bn], wv[:, c, :], hview[:, c, bo:bo + bn], c == 0, c == KC - 1, wk + hkeys, [("ps", b)])
                    if sample:
                        self.cp("act", pq[:, :, 3:3 + L], self.PS[:, b, 0:128].rearrange("p (s l) -> p s l", s=16), [("ps", b)], pqk)
                    else:
                        self.cp("act", pq[:, 0, 3 + bo:3 + bo + bn], self.PS[:, b, 0:bn], [("ps", b)], pqk)
                cw = lambda r_: CW[r_][0][:, cidx:cidx + 1]
                cwk_all = CW[0][1] + CW[1][1] + CW[2][1] + CW[3][1]
                self.ts("dve", YC3, pq[:, :, 0:L], cw(0), None, ALU.mult, None, pqk + cwk_all, YCk)
                for r_ in range(1, 4):
                    self.stt(YC3, pq[:, :, r_:r_ + L], cw(r_), YC3, ALU.mult, ALU.add, pqk + cwk_all + YCk, YCk)
                if sample:
                    self.cp("pool", SCT[:, cidx, :].rearrange("p (s r) -> p s r", s=16), pq[:, :, L:L + 3], pqk, SCTk)
                else:
                    self.cp("pool", CT[:, cidx, :], pq[:, 0, L:L + 3], pqk, CTk)
                self.af(YC, YC, AF.Silu, YCk, YCk)
                if part < 2:
                    sq = self.SQ[:, 0, 0:n]
                    self.tt("pool", sq, YC, YC, ALU.mult, YCk, [("SQ", 0)])
                    rn = self.STAT[:, 0, 0:n]
                    for bi, (bo, bn) in enumerate(blks):
                        b = self.psum()
                        self.mm(self.PS[:, b, 0:bn], self.ONF[:], sq[:, bo:bo + bn], True, True, [("SQ", 0), "ONF"], [("ps", b)])
                        self.af(rn[:, bo:bo + bn], self.PS[:, b, 0:bn], AF.Ln, [("ps", b), "EPSC"], [("STAT", 0)],
                                bias=self.EPSC[:, 0:1], scale=1.0)
                    self.af(rn, rn, AF.Exp, [("STAT", 0)], [("STAT", 0)], scale=-0.5)
                    if part == 0:
                        self.stt(QT, YC, 128.0 ** -0.5, rn, ALU.mult, ALU.mult, YCk + [("STAT", 0)], QTk)
                    else:
                        self.tt("dve", KTf, YC, rn, ALU.mult, YCk + [("STAT", 0)], KTfk)
                        self.cp("pool", KTb, KTf, KTfk, KTbk)
                else:
                    self.cp("dve", VTb, YC, YCk, VTbk)
            wv, wk = self.wa_load(w_in[:, 4608 + h * 128:4608 + (h + 1) * 128], 8192)
            for bi, (bo, bn) in enumerate(blks):
                b = self.psum()
                for c in range(KC):
                    self.mm(self.PS[:, b, 0:bn], wv[:, c, :], hview[:, c, bo:bo + bn], c == 0, c == KC - 1, wk + hkeys, [("ps", b)])
                self.af(SZ[:, bo:bo + bn], self.PS[:, b, 0:bn], AF.Silu, [("ps", b)], SZk)
            if sample:
                S.dma("sp", S0f3, I["sdelta"][j, self.sq0:self.sq0 + 16, h].rearrange("s k v -> k s v"), w=S0fk)
                S.dma("pool", S0b3, I["sdelta"][j, self.sq0:self.sq0 + 16, h].rearrange("s k v -> k s v"), w=S0bk)
            for t in range(nt):
                d = TB[t % nbuf]
                tc = slice(t * 128, (t + 1) * 128)
                col = lambda arr: arr[:, t, h:h + 1]
                KBE, KBEk = d["KBE"]; KDEC, KDECk = d["KDEC"]; VB, VBk = d["VB"]
                E, Ek = d["E"]; L1, L1k = d["L1"]; Ab, Abk = d["Ab"]; ATb, ATbk = d["ATb"]
                Xf, Xfk = d["Xf"]; Xb, Xbk = Xf, Xfk; U, Uk = d["U"]; WT, WTk = d["WT"]; QD, QDk = d["QD"]
                GUN, GUNk = d["GUN"]; DG, DGk = GUN, GUNk; WM, WMk = d["WM"]; QM, QMk = d["QM"]; KDM, KDMk = d["KDM"]
                VN, VNk = d["VN"]; Ot, Otk = d["O"]; ON, ONk = d["ON"]
                WM3 = WM.rearrange("p (s t) -> p s t", s=nb)
                QM3 = QM.rearrange("p (s t) -> p s t", s=nb)
                KDM3 = KDM.rearrange("p (s t) -> p s t", s=nb)
                b = self.psum()
                pb = self.PS[:, b, :]
                self.tr(pb[:, 0:128], KTf[:, tc], self.IDF[:], KTfk + ["IDF"], [("ps", b)])
                self.tr(pb[:, 128:256], VTb[:, tc], self.IDF[:], VTbk + ["IDF"], [("ps", b)])
                self.ts("dve", KBE, pb[:, 0:128], col(BE), None, ALU.mult, None, [("ps", b)] + BEk, KBEk)
                self.ts("dve", KDEC, pb[:, 0:128], col(EDEC), None, ALU.mult, None, [("ps", b)] + EDECk, KDECk)
                self.af(VB, pb[:, 128:256], AF.Copy, [("ps", b)] + BETAk, VBk, scale=col(BETA))
                self.ts("pool", GUN, UBL, col(NG), None, ALU.mult, None, UBLk + NGk, GUNk)
                b = self.psum()
                self.mm(self.PS[:, b, 0:128], self.ONF[:], GUN, True, False, GUNk + ["ONF"], [("ps", b)])
                self.mm(self.PS[:, b, 0:128], self.IDF[:], NEG, False, True, NEGk + ["IDF"], [("ps", b)])
                self.af(E, self.PS[:, b, 0:128], AF.Exp, [("ps", b)] + GCk, Ek, bias=col(GC), scale=1.0)
                bk = self.psum()
                self.mm(self.PS[:, bk, 0:128], KTb[:, tc], KTb[:, tc], True, True, KTbk, [("ps", bk)])
                self.mm(self.PS[:, bk, 128:256], QT[:, tc], KTb[:, tc], True, True, QTk + KTbk, [("ps", bk)])
                self.stt(L1, self.PS[:, bk, 0:128], col(BETA), E, ALU.mult, ALU.mult, [("ps", bk)] + BETAk + Ek, L1k)
                Lb, Lbk = d["Lb0"]
                Rb, Rbk = d["Rb0"]
                self.tt("pool", Lb, L1, STR, ALU.mult, L1k + STRk, Lbk)
                self.tt("dve", Ab, self.PS[:, bk, 128:256], E, ALU.mult, [("ps", bk)] + Ek, Abk)
                b = self.psum()
                self.tr(self.PS[:, b, 0:128], Lb, self.IDF[:], Lbk + ["IDF"], [("ps", b)])
                self.cp("act", Rb, self.PS[:, b, 0:128], [("ps", b)], Rbk)
                self.tt("dve", Xf, self.IDF[:], Rb, ALU.subtract, Rbk + ["IDF"], Xfk)
                b = self.psum()
                pb = self.psb(b)
                self.tr(pb[:, 0:128], Ab, self.IDB[:], Abk + ["IDB"], [("ps", b)])
                self.cp("act", ATb, pb[:, 0:128], [("ps", b)], ATbk)
                for lev in range(nlev):
                    L2, L2k = d["Lb%d" % ((lev + 1) % 2)]
                    R2, R2k = d["Rb%d" % ((lev + 1) % 2)]
                    b = self.psum()
                    self.mm(self.PS[:, b, 0:128], Rb, Lb, True, True, Rbk + Lbk, [("ps", b)])
                    if lev < nlev - 1:
                        self.mm(self.PS[:, b, 128:256], Lb, Rb, True, True, Rbk + Lbk, [("ps", b)])
                    self.cp("act", L2, self.PS[:, b, 0:128], [("ps", b)], L2k)
                    if lev < nlev - 1:
                        self.cp("dve", R2, self.PS[:, b, 128:256], [("ps", b)], R2k)
                    b2 = self.psum()
                    self.mm(self.PS[:, b2, 0:128], L2, Xb, True, True, L2k + Xbk, [("ps", b2)])
                    self.tt("dve", Xf, Xf, self.PS[:, b2, 0:128], ALU.add, Xfk + [("ps", b2)], Xfk)
                    Lb, Lbk, Rb, Rbk = L2, L2k, R2, R2k
                b = self.psum()
                self.mm(self.PS[:, b, 0:128], Xb, VB, True, True, Xbk + VBk, [("ps", b)])
                self.mm(self.PS[:, b, 128:256], KBE, Xb, True, True, Xbk + KBEk, [("ps", b)])
                self.cp("act", U, self.PS[:, b, 0:128], [("ps", b)], Uk)
                self.cp("dve", WT, self.PS[:, b, 128:256], [("ps", b)], WTk)
                self.ts("pool", DG, self.IDF[:], col(EGC), None, ALU.mult, None, EGCk + ["IDF"], DGk)
                b = self.psum()
                self.mm(self.PS[:, b, 0:128], self.ONF[:], DG, True, True, DGk + ["ONF"], [("ps", b)])
                self.tt("dve", QD, QT[:, tc], self.PS[:, b, 0:128], ALU.mult, QTk + [("ps", b)], QDk)
                self.tt("pool", WM3, WT.unsqueeze(1).broadcast_to([128, nb, 128]), COLM3, ALU.mult, WTk + COLMk, WMk)
                self.tt("pool", QM3, QD.unsqueeze(1).broadcast_to([128, nb, 128]), COLM3, ALU.mult, QDk + COLMk, QMk)
                self.tt("pool", KDM3, KDEC.unsqueeze(1).broadcast_to([128, nb, 128]),
                        ROWM.unsqueeze(2).broadcast_to([128, nb, 128]), ALU.mult, KDECk + ROWMk, KDMk)
                if not sample:
                    ATM, ATMk = d["ATM"]
                    ATM3 = ATM.rearrange("p (s t) -> p s t", s=nb)
                    self.tt("pool", ATM3, ATb.unsqueeze(1).broadcast_to([128, nb, 128]), COLM3, ALU.mult, ATbk + COLMk, ATMk)
                    for s in range(nb):
                        b = self.psum()
                        self.mm(self.PS[:, b, 0:128], WM3[:, s, :], Sb[:, h, :], True, True, WMk + Sbk(h), [("ps", b)])
                        self.tt("dve", VN, U, self.PS[:, b, 0:128], ALU.subtract, Uk + [("ps", b)], VNk)
                        bo_ = self.psum()
                        self.mm(self.PS[:, bo_, 0:128], QM3[:, s, :], Sb[:, h, :], True, False, QMk + Sbk(h), [("ps", bo_)])
                        self.mm(self.PS[:, bo_, 0:128], ATM3[:, s, :], VN, False, True, ATMk + VNk, [("ps", bo_)])
                        if s == 0:
                            self.cp("act", Ot, self.PS[:, bo_, 0:128], [("ps", bo_)], Otk)
                        else:
                            self.tt("dve", Ot, Ot, self.PS[:, bo_, 0:128], ALU.add, Otk + [("ps", bo_)], Otk)
                        bs = self.psum()
                        self.mm(self.PS[:, bs, 0:128], KDM3[:, s, :], VN, True, True, KDMk + VNk, [("ps", bs)])
                        self.stt(Sf[:, h, :], Sf[:, h, :], GLB[:, t, s, h:h + 1], self.PS[:, bs, 0:128], ALU.mult, ALU.add,
                                 Sfk(h) + GLBk + [("ps", bs)], Sfk(h))
                        self.cp("act", Sb[:, h, :], Sf[:, h, :], Sfk(h), Sbk(h))
                else:
                    b = self.psum()
                    for s in range(nb):
                        self.mm(self.PS[:, b, 0:128], WM3[:, s, :], S0b3[:, s, :], s == 0, s == nb - 1, WMk + S0bk, [("ps", b)])
                    self.tt("dve", VN, U, self.PS[:, b, 0:128], ALU.subtract, Uk + [("ps", b)], VNk)
                    bo_ = self.psum()
                    for s in range(nb):
                        self.mm(self.PS[:, bo_, 0:128], QM3[:, s, :], S0b3[:, s, :], s == 0, False, QMk + S0bk, [("ps", bo_)])
                    self.mm(self.PS[:, bo_, 0:128], ATb, VN, False, True, ATbk + VNk, [("ps", bo_)])
                    self.cp("act", Ot, self.PS[:, bo_, 0:128], [("ps", bo_)], Otk)
                    for s in range(nb):
                        bs = self.psum()
                        self.mm(self.PS[:, bs, 0:128], KDM3[:, s, :], VN, True, True, KDMk + VNk, [("ps", bs)])
                        self.stt(S0f3[:, s, :], S0f3[:, s, :], GLB[:, t, s, h:h + 1], self.PS[:, bs, 0:128], ALU.mult, ALU.add,
                                 S0fk + GLBk + [("ps", bs)], S0fk)
                    if commit:
                        S.dma("sp", O["delta_s"][j, self.sq0:self.sq0 + 16, h].rearrange("s k v -> k s v"), S0f3, r=S0fk)
                rs = self.SM[:, 0:1]
                self.tt("pool", L1, Ot, Ot, ALU.mult, Otk, L1k)
                S.add("dve", lambda hh, L1=L1: hh.tensor_reduce(out=self.SM[:, 0:1], in_=L1, axis=mybir.AxisListType.X, op=ALU.add),
                      r=L1k, w=["SM"])
                self.af(rs, rs, AF.Ln, ["SM", "EPSC"], ["SM"], bias=self.EPSC[:, 0:1], scale=1.0 / 128)
                self.af(rs, rs, AF.Exp, ["SM"], ["SM"], scale=-0.5)
                self.ts("dve", ON, Ot, rs, None, ALU.mult, None, Otk + ["SM"], ONk)
                b = self.psum()
                self.tr(self.PS[:, b, 0:128], ON, self.IDF[:], ONk + ["IDF"], [("ps", b)])
                self.stt(mixt[:, h, tc], self.PS[:, b, 0:128], DNG[:, 0:1], SZ[:, tc], ALU.mult, ALU.mult, [("ps", b)] + DNGk + SZk, mk(h))
        if commit and sample:
            cst2, cst2k = cst, cstk
            dst = O["conv_s"][j, self.sq0:self.sq0 + 16].rearrange("s r f -> (s r) f")
            for c9 in range(9):
                b = self.psum()
                for c in range(4):
                    self.tr(self.PS[0:48, b, c * 128:(c + 1) * 128], SCT[:, c9 * 4 + c, :], self.IDF[:], SCTk + ["IDF"], [("ps", b)])
                self.cp("dve", cst2[0:48, :], self.PS[0:48, b, :], [("ps", b)], cst2k)
                S.dma("sp", dst[:, c9 * 512:(c9 + 1) * 512], cst2[0:48, :], r=cst2k)
        if (not sample) and sbi == 1 and not self.last_pass:
            S.dma("sp", self.scrS[j], self.WD[:, 0, 2048:5120].bitcast(F32), r=allSf, w=[("scrS", j)])
            S.dma("sp", self.scrCT[j], self.WD[:, 0, 5120:5336].bitcast(F32), r=CTk, w=[("scrCT", j)])
        if commit and (not sample) and sbi == 1 and self.last_pass:
            S.dma("sp", O["delta_p"][j].rearrange("h k v -> k h v"), Sf, r=allSf)
            cst2, cst2k = YC, YCk
            for c9 in range(9):
                b = self.psum()
                for c in range(4):
                    self.tr(self.PS[0:3, b, c * 128:(c + 1) * 128], CT[:, c9 * 4 + c, :], self.IDF[:], CTk + ["IDF"], [("ps", b)])
                self.cp("dve", cst2[0:3, :], self.PS[0:3, b, :], [("ps", b)], cst2k)
                S.dma("sp", O["conv_p"][j][:, c9 * 512:(c9 + 1) * 512], cst2[0:3, :], r=cst2k)


_PROG = None


def get_prog():
    global _PROG
    if _PROG is None:
        _PROG = Prog()
    return _PROG


def kernel(**inp):
    prog = get_prog()
    return run_prog(prog, inp, 4)


def run_prog(prog, inp, ncores):
    consts = prog.consts
    f32 = lambda a: np.ascontiguousarray(a, dtype=np.float32)
    in_maps = []
    shared = {k: f32(inp[k]) for k in ("norm_gains", "w_ffn_gu", "w_ffn_dn", "w_in_a", "conv_w", "a_log", "dt_bias",
                                        "delta_norm_gain", "w_in_b", "gmlp_ln_gain", "gmlp_ln_bias", "w_spatial",
                                        "b_spatial", "mem_norm_gain", "w_mem_kv", "w_out")}
    for k, v in consts.items():
        shared["c_" + k] = f32(v)
    for c in range(ncores):
        b = c % 4
        m = dict(shared)
        m["xp"] = f32(inp["x_prompt"][b])
        m["xs"] = f32(inp["x_sample"][c * 32:(c + 1) * 32].reshape(2 * TS, D))
        m["mem"] = f32(inp["mem_prompt"][b])
        m["ck"] = f32(inp["cache_mem_k"][:, c * 32:(c + 1) * 32].reshape(DEPTH, 32, NMEM, 512))
        m["cv"] = f32(inp["cache_mem_v"][:, c * 32:(c + 1) * 32].reshape(DEPTH, 32, NMEM, 512))
        m["sdelta"] = f32(inp["state_delta"][:, c * 32:(c + 1) * 32])
        m["sconv"] = f32(inp["state_conv"][:, c * 32:(c + 1) * 32])
        in_maps.append(m)
    res = run_bass_kernel_spmd(prog.nc, in_maps, core_ids=list(range(ncores)))
    R = res.results
    if ncores < 4:
        return R
    y_prompt = np.stack([R[b]["yp"] for b in range(4)])
    y_sample = np.concatenate([R[c]["ys"].reshape(32, 8, D) for c in range(4)], 0)
    mem_k = np.stack([R[b]["memk"] for b in range(4)], 1).reshape(DEPTH, 4, NMEM, XH, 128)
    mem_v = np.stack([R[b]["memv"] for b in range(4)], 1).reshape(DEPTH, 4, NMEM, XH, 128)
    delta_p = np.stack([R[b]["delta_p"] for b in range(4)], 1)
    conv_p = np.stack([R[b]["conv_p"] for b in range(4)], 1)
    delta_s = np.concatenate([R[c]["delta_s"] for c in range(4)], 1)
    conv_s = np.concatenate([R[c]["conv_s"] for c in range(4)], 1)
    gv_s = np.concatenate([R[c]["gv_s"].reshape(2, 32, 8, 1536) for c in range(4)], 1)
    return tuple(np.ascontiguousarray(a, dtype=np.float32) for a in
                 (y_prompt, y_sample, mem_k, mem_v, delta_p, conv_p, delta_s, conv_s, gv_s))
```

```python
import contextlib
import numpy as np
import ml_dtypes
import concourse.bass as bass
import concourse.mybir as mybir
from concourse.bass_utils import run_bass_kernel_spmd

F32 = mybir.dt.float32
BF16 = mybir.dt.bfloat16
AF = mybir.ActivationFunctionType
ALU = mybir.AluOpType

D = 2048
KC = 16
FF = 5632
FC = 44
TP = 1024
TS = 128
T = TP + TS
NH = 12
XH = 4
NMEM = 256
DEPTH = 4
EPS = 1e-6
IN_A = 6680
IN_B = 3584


class Sched:
    ENGS = ("pe", "act", "dve", "pool", "sp")
    NDS = 6

    def __init__(self, nc, es):
        self.nc = nc
        self.ops = {e: [] for e in self.ENGS}
        self.res = {}
        self.waited = {e: {} for e in self.ENGS}
        self.needed = set()
        self.csem = {e: es.enter_context(nc.semaphore("c_" + e)) for e in ("pe", "act", "dve", "pool")}
        self.dsem = {q: [es.enter_context(nc.semaphore("d_%s%d" % (q, i))) for i in range(self.NDS)]
                     for q in ("sp", "pool", "act")}
        self.dcnt = {q: 0 for q in ("sp", "pool", "act")}
        self.dlast = {}

    def _dep(self, eng, ev, waits):
        if ev is None:
            return
        if ev[0] == "c":
            _, pe, idx = ev
            if pe == eng == "pe":
                return
            if pe == eng and idx == len(self.ops[eng]) - 0 - 1 and False:
                return
            key = ("c", pe)
            if self.waited[eng].get(key, -1) >= idx:
                return
            self.waited[eng][key] = idx
            self.needed.add((pe, idx))
            waits.append(ev)
        else:
            _, q, si, val = ev
            key = ("d", q, si)
            if self.waited[eng].get(key, -1) >= val:
                return
            self.waited[eng][key] = val
            waits.append(ev)

    def _track(self, eng, r, w, ev, waits):
        for k in r:
            st = self.res.get(k)
            if st is not None:
                self._dep(eng, st[0], waits)
                if isinstance(k, tuple) and k[0] == "ps":
                    for e2 in st[1]:
                        if e2[0] != "c" or e2[1] != eng:
                            self._dep(eng, e2, waits)
        for k in w:
            st = self.res.get(k)
            if st is not None:
                self._dep(eng, st[0], waits)
                for e2 in st[1]:
                    self._dep(eng, e2, waits)
        for k in r:
            st = self.res.setdefault(k, [None, []])
            st[1].append(ev)
            if len(st[1]) > 24:
                st[1] = st[1][-24:] if False else st[1]
        for k in w:
            self.res[k] = [ev, []]

    def barrier(self):
        bt = self.bar_tile
        self.add("dve", lambda h: h.memset(bt, 0.0), w=["PHASE"])

    def add(self, eng, fn, r=(), w=()):
        r = list(r) + ["PHASE"]
        idx = len(self.ops[eng])
        waits = []
        ev = ("c", eng, idx)
        self._track(eng, r, w, ev, waits)
        self.ops[eng].append(("c", fn, waits, None))

    def dma(self, q, out, in_, r=(), w=()):
        r = list(r) + ["PHASE"]
        n = self.dcnt[q]
        self.dcnt[q] = n + 1
        si = n % self.NDS
        val = 16 * (n // self.NDS + 1)
        waits = []
        if n >= self.NDS:
            self._dep(q, ("d", q, si, val - 16), waits)
        ev = ("d", q, si, val)
        self._track(q, r, w, ev, waits)
        self.dlast[(q, si)] = val
        self.ops[q].append(("d", (out, in_), waits, (q, si)))

    def emit(self, block):
        nc = self.nc
        cnt = {}
        for e in ("pe", "act", "dve", "pool"):
            c = 0
            for i in range(len(self.ops[e])):
                if (e, i) in self.needed:
                    c += 1
                    cnt[(e, i)] = c
        final_waits = [(self.dsem[q][si], v) for (q, si), v in self.dlast.items()]

        def run(eng_name, h):
            for i, (kind, fn, waits, dinfo) in enumerate(self.ops[eng_name]):
                for ev in waits:
                    if ev[0] == "c":
                        h.wait_ge(self.csem[ev[1]], cnt[(ev[1], ev[2])])
                    else:
                        h.wait_ge(self.dsem[ev[1]][ev[2]], ev[3])
                if kind == "c":
                    ins = fn(h)
                    if (eng_name, i) in self.needed:
                        ins.then_inc(self.csem[eng_name], 1)
                else:
                    out, in_ = fn
                    h.dma_start(out=out, in_=in_).then_inc(self.dsem[dinfo[0]][dinfo[1]], 16)
            if eng_name == "sp":
                for s, v in final_waits:
                    h.wait_ge(s, v)

        @block.tensor
        def _(h):
            run("pe", h)

        @block.scalar
        def _(h):
            run("act", h)

        @block.vector
        def _(h):
            run("dve", h)

        @block.gpsimd
        def _(h):
            run("pool", h)

        @block.sync
        def _(h):
            run("sp", h)


def make_consts():
    c = {}
    idx = np.arange(128)
    c["ident_f"] = np.eye(128, dtype=np.float32)
    c["ones_f"] = np.ones((128, 128), np.float32)
    for B in (64, 8):
        nb = 128 // B
        blk = idx // B
        same = blk[:, None] == blk[None, :]
        i = idx[:, None]
        j = idx[None, :]
        c["neg%d" % B] = np.where(same & (j <= i), 0.0, -30000.0).astype(np.float32)
        c["strict%d" % B] = (same & (j < i)).astype(np.float32)
        c["ublk%d" % B] = (same & (i <= j)).astype(np.float32)
        c["bblk%d" % B] = same.astype(np.float32)
        cm = np.zeros((128, nb, 128), np.float32)
        for s in range(nb):
            cm[:, s, s * B:(s + 1) * B] = 1.0
        c["colmask%d" % B] = cm.reshape(128, nb * 128)
        rm = np.zeros((128, nb), np.float32)
        rm[idx, blk] = 1.0
        c["rowmask%d" % B] = rm
    c["tril_t"] = (idx[:, None] <= idx[None, :]).astype(np.float32)
    m8 = np.zeros((128, 16, 8), np.float32)
    for p in range(128):
        s, jj = p // 8, p % 8
        m8[p, s, jj:] = 1.0
    c["mask8"] = m8.reshape(128, 128)
    rep = np.zeros((128, 128), np.float32)
    for p in range(128):
        rep[p % 8, p] = 1.0
    c["rep8"] = rep
    return c


SBS = [(0, 576), (576, 576)]
MSBS = [(0, 512), (512, 512), (1024, 128)]


def blocks_of(sb):
    off, n = sb
    return [(off, n // 2), (off + n // 2, n // 2)]


class Prog:
    def __init__(self, stage=9, nlayers=DEPTH, passes=("A", "B"), feat=("mix", "attn")):
        self.feat = set(feat)
        self.stage = stage
        self.nlayers = nlayers
        self.passes = passes
        nc = self.nc = bass.Bass("TRN2", target_bir_lowering=False)
        es = self.es = contextlib.ExitStack()
        self.S = Sched(nc, es)
        self.consts = make_consts()
        self.build()

    def din(self, name, shape, dt=F32):
        return self.nc.dram_tensor(name, list(shape), dt, kind="ExternalInput").ap()

    def dout(self, name, shape):
        return self.nc.dram_tensor(name, list(shape), F32, kind="ExternalOutput").ap()

    def sb(self, name, shape, dt):
        return self.es.enter_context(self.nc.sbuf_tensor(name, list(shape), dt))

    def build(self):
        nc, S = self.nc, self.S
        I = self.I = {}
        O = self.O = {}
        I["xp"] = self.din("xp", [2 * TP, D])
        I["xs"] = self.din("xs", [2 * TS, D])
        I["mem"] = self.din("mem", [NMEM, D])
        I["ck"] = self.din("ck", [DEPTH, 32, NMEM, 512])
        I["cv"] = self.din("cv", [DEPTH, 32, NMEM, 512])
        I["sdelta"] = self.din("sdelta", [2, 32, NH, 128, 128])
        I["sconv"] = self.din("sconv", [2, 32, 3, 4608])
        I["norm_gains"] = self.din("norm_gains", [DEPTH, 6, D])
        I["w_ffn_gu"] = self.din("w_ffn_gu", [DEPTH, 2, D, 2 * FF])
        I["w_ffn_dn"] = self.din("w_ffn_dn", [DEPTH, 2, FF, D])
        I["w_in_a"] = self.din("w_in_a", [2, D, IN_A])
        I["conv_w"] = self.din("conv_w", [2, 4, 4608])
        I["a_log"] = self.din("a_log", [2, NH])
        I["dt_bias"] = self.din("dt_bias", [2, NH])
        I["delta_norm_gain"] = self.din("delta_norm_gain", [2, 128])
        I["w_in_b"] = self.din("w_in_b", [2, D, IN_B])
        I["gmlp_ln_gain"] = self.din("gmlp_ln_gain", [2, 1536])
        I["gmlp_ln_bias"] = self.din("gmlp_ln_bias", [2, 1536])
        I["w_spatial"] = self.din("w_spatial", [2, NH, 128, 128])
        I["b_spatial"] = self.din("b_spatial", [2, NH, 128])
        I["mem_norm_gain"] = self.din("mem_norm_gain", [DEPTH, D])
        I["w_mem_kv"] = self.din("w_mem_kv", [DEPTH, D, 1024])
        I["w_out"] = self.din("w_out", [DEPTH, D, D])
        for k, v in self.consts.items():
            I["c_" + k] = self.din("c_" + k, v.shape)
        O["yp"] = self.dout("yp", [2 * TP, D])
        O["ys"] = self.dout("ys", [2 * TS, D])
        O["memk"] = self.dout("memk", [DEPTH, NMEM, 512])
        O["memv"] = self.dout("memv", [DEPTH, NMEM, 512])
        O["delta_p"] = self.dout("delta_p", [2, NH, 128, 128])
        O["conv_p"] = self.dout("conv_p", [2, 3, 4608])
        O["delta_s"] = self.dout("delta_s", [2, 32, NH, 128, 128])
        O["conv_s"] = self.dout("conv_s", [2, 32, 3, 4608])
        O["gv_s"] = self.dout("gv_s", [2, 2 * TS, 1536])

        self.XT = self.sb("XT", [128, KC, T], F32)
        self.R1 = self.sb("R1", [128, 19456], BF16)
        self.ARENA = self.sb("ARENA", [128, FC * 576], BF16)
        self.WD = self.sb("WD", [128, 2, FC * 128], BF16)
        self.GAIN = self.sb("GAIN", [128, DEPTH * 6 * KC], F32)
        self.MG = self.sb("MG", [128, DEPTH * KC], F32)
        self.IDF = self.sb("IDF", [128, 128], F32)
        self.IDB = self.sb("IDB", [128, 128], BF16)
        self.ONF = self.sb("ONF", [128, 128], F32)
        self.ONB = self.sb("ONB", [128, 128], BF16)
        self.STAT = self.sb("STAT", [128, 2, 576], F32)
        self.SQ = self.sb("SQ", [128, 2, 576], F32)
        self.EPSC = self.sb("EPSC", [128, 2], F32)
        self.PS = self.es.enter_context(nc.psum_tensor("PS", [128, 8, 512], F32))
        self.ONEC = self.sb("ONEC", [128, 2], F32)
        self.SM = self.sb("SM", [128, 8], F32)
        self.BAR = self.sb("BAR", [128, 2], F32)
        S.bar_tile = self.BAR[:, 0:1]
        m64 = []
        for nm, shp, dt in (("neg64", [128, 128], F32), ("strict64", [128, 128], F32), ("ublk64", [128, 128], F32),
                            ("bblk64", [128, 128], F32), ("colmask64", [128, 256], BF16), ("rowmask64", [128, 2], F32)):
            tl = self.sb("M_" + nm, shp, dt)
            S.dma("pool" if dt == BF16 else "sp", tl[:], I["c_" + nm], w=["M64"])
            m64.append(tl[:])
        self.M64 = m64
        S.add("dve", lambda h: h.memset(self.EPSC[:, 0:1], EPS), w=["EPSC"])
        S.add("dve", lambda h: h.memset(self.ONEC[:, 0:1], 1.0), w=["ONEC"])
        self.wa_i = 0
        self.ps_reserved = set()
        self.wd_i = 0
        self.ps_i = 0

        S.dma("sp", self.IDF[:], I["c_ident_f"], w=["IDF"])
        S.dma("sp", self.ONF[:], I["c_ones_f"], w=["ONF"])
        S.dma("pool", self.IDB[:], I["c_ident_f"], w=["IDB"])
        S.dma("pool", self.ONB[:], I["c_ones_f"], w=["ONB"])
        with nc.allow_non_contiguous_dma(reason="small gain vectors, feature-major"):
            pass
        self.load_featmajor(self.GAIN, I["norm_gains"].rearrange("l s (c p) -> (l s c) p", p=128), DEPTH * 6 * KC, "GAIN")
        self.load_featmajor(self.MG, I["mem_norm_gain"].rearrange("l (c p) -> (l c) p", p=128), DEPTH * KC, "MG")

        self.scrS = nc.dram_tensor("scrS", [2, 128, NH * 128], F32, kind="Internal").ap()
        self.scrCT = nc.dram_tensor("scrCT", [2, 128, 108], F32, kind="Internal").ap()
        for pi, pname in enumerate(self.passes):
            self.pname = pname
            self.first_pass = (pi == 0)
            self.last_pass = (pi == len(self.passes) - 1)
            self.tok0 = 0 if pname == "A" else TP
            self.sq0 = 0 if pname == "A" else 16
            self.has_sample = True
            self.Tp = T if self.has_sample else TP
            sbs = [(0, 576), (576, 576)] if self.has_sample else [(0, 512), (512, 512)]
            msbs = [(0, 512), (512, 512)] + ([(1024, 128)] if self.has_sample else [])
            if pi > 0:
                S.barrier()
            self.load_x()
            for layer in range(self.nlayers):
                for sbi, sb in enumerate(sbs):
                    self.ffn(layer, 0, sb)
                if self.stage >= 2:
                    for sbi, sb in enumerate(msbs):
                        self.mixer(layer, sbi, sb)
                for sbi, sb in enumerate(sbs):
                    self.ffn(layer, 1, sb)
            self.store_y()

        blk = self.es.enter_context(nc.Block())
        S.emit(blk)
        self.es.close()

    def psum(self):
        while True:
            b = self.ps_i % 8
            self.ps_i += 1
            if b not in self.ps_reserved:
                return b

    def load_featmajor(self, dst, src_rows, nrows, key):
        S = self.S
        done = 0
        while done < nrows:
            n = min(128, nrows - done)
            st = self.ARENA[:, 0:256].bitcast(F32)
            S.dma("sp", st[0:n, :], src_rows[done:done + n, :], w=[("AT", 0)])
            b = self.psum()
            S.add("pe", lambda h, b=b, n=n, st=st: h.transpose(self.PS[:, b, 0:n], st[0:n, :], self.IDF[0:n, 0:n]),
                  r=[("AT", 0), "IDF"], w=[("ps", b)])
            S.add("dve", lambda h, b=b, n=n, d0=done: h.tensor_copy(out=dst[:, d0:d0 + n], in_=self.PS[:, b, 0:n]),
                  r=[("ps", b)], w=[key])
            done += n

    def load_x(self):
        S, I = self.S, self.I
        for ti in range(self.Tp // 128):
            src = I["xp"][self.tok0 + ti * 128:self.tok0 + (ti + 1) * 128, :] if ti < 8 else I["xs"][self.sq0 * 8:self.sq0 * 8 + 128, :]
            st = self.WD[:, ti % 2, 0:4096].bitcast(F32)
            key = ("WD", ti % 2)
            S.dma("sp", st, src, w=[key])
            for c4 in range(4):
                b = self.psum()
                for c in range(4):
                    cc = c4 * 4 + c
                    S.add("pe", lambda h, b=b, c=c, cc=cc, st=st: h.transpose(
                        self.PS[:, b, c * 128:(c + 1) * 128], st[:, cc * 128:(cc + 1) * 128], self.IDF[:]),
                        r=[key, "IDF"], w=[("ps", b)])
                S.add("dve" if c4 % 2 == 0 else "act",
                      (lambda h, b=b, c4=c4, ti=ti: h.tensor_copy(
                          out=self.XT[:, c4 * 4:(c4 + 1) * 4, ti * 128:(ti + 1) * 128],
                          in_=self.PS[:, b, :].rearrange("p (c t) -> p c t", c=4))) if c4 % 2 == 0 else
                      (lambda h, b=b, c4=c4, ti=ti: h.activation(
                          out=self.XT[:, c4 * 4:(c4 + 1) * 4, ti * 128:(ti + 1) * 128],
                          in_=self.PS[:, b, :].rearrange("p (c t) -> p c t", c=4), func=AF.Copy)),
                      r=[("ps", b)], w=[("XT", ti)])

    def store_y(self):
        S, O = self.S, self.O
        for ti in range(self.Tp // 128):
            dst = O["yp"][self.tok0 + ti * 128:self.tok0 + (ti + 1) * 128, :] if ti < 8 else O["ys"][self.sq0 * 8:self.sq0 * 8 + 128, :]
            st = self.WD[:, ti % 2, 0:4096].bitcast(F32)
            key = ("WD", ti % 2)
            for c4 in range(4):
                b = self.psum()
                for c in range(4):
                    cc = c4 * 4 + c
                    S.add("pe", lambda h, b=b, c=c, cc=cc, ti=ti: h.transpose(
                        self.PS[:, b, c * 128:(c + 1) * 128], self.XT[:, cc, ti * 128:(ti + 1) * 128], self.IDF[:]),
                        r=[("XT", ti), "IDF"], w=[("ps", b)])
                S.add("dve", lambda h, b=b, c4=c4, st=st: h.tensor_copy(
                    out=st[:, c4 * 512:(c4 + 1) * 512], in_=self.PS[:, b, :]),
                    r=[("ps", b)], w=[key])
            S.dma("sp", dst, st, r=[key])

    def xt_keys(self, off, n):
        return [("XT", t) for t in range(off // 128, (off + n + 127) // 128)]

    def rstd_bcast(self, src_fn, nch, off, n, inv_d, slot, rkeys):
        S = self.S
        cb = [(0, n)] if n <= 512 else [(0, n // 2), (n // 2, n - n // 2)]
        banks = [self.psum() for _ in cb]
        for c in range(nch):
            sqt = self.SQ[:, c % 2, 0:n]
            S.add("pool" if c % 2 else "dve",
                  lambda h, c=c, sqt=sqt: h.tensor_tensor(out=sqt, in0=src_fn(c), in1=src_fn(c), op=ALU.mult),
                  r=rkeys, w=[("SQ", c % 2)])
            for (o, m), b in zip(cb, banks):
                S.add("pe", lambda h, c=c, b=b, sqt=sqt, o=o, m=m: h.matmul(
                    self.PS[:, b, 0:m], self.ONF[:], sqt[:, o:o + m], start=(c == 0), stop=(c == nch - 1)),
                    r=[("SQ", c % 2), "ONF"], w=[("ps", b)])
        st = self.STAT[:, slot, 0:n]
        for (o, m), b in zip(cb, banks):
            S.add("act", lambda h, b=b, o=o, m=m: h.activation(
                out=st[:, o:o + m], in_=self.PS[:, b, 0:m], func=AF.Ln, bias=self.EPSC[:, 0:1], scale=inv_d),
                r=[("ps", b), "EPSC"], w=[("STAT", slot)])
        S.add("act", lambda h: h.activation(out=st, in_=st, func=AF.Exp, scale=-0.5),
              r=[("STAT", slot)], w=[("STAT", slot)])
        return st

    def r1_keys(self, lo, hi):
        return [("R1", p) for p in range(lo // 2048, (hi + 2047) // 2048)]

    def ht_view(self, n):
        return self.R1[:, 0:KC * n].rearrange("p (c t) -> p c t", c=KC), ["HT"]

    def wa_load(self, src, base):
        nslots = (19456 - base) // 2048
        s = self.wa_i % nslots
        self.wa_i += 1
        lo = base + s * 2048
        view = self.R1[:, lo:lo + 2048].rearrange("p (c n) -> p c n", c=KC)
        keys = [("WA", lo)]
        self.S.dma("pool", view, src.rearrange("(c p) n -> p c n", p=128), r=["WAALL"], w=keys)
        return view, keys + ["WAALL"]

    def at_keys(self, lo, hi):
        return [("AT", j) for j in range(lo // 576, (hi + 575) // 576)]

    def prenorm(self, gidx, off, n, hview, hkeys):
        S = self.S
        xk = self.xt_keys(off, n)
        rstd = self.rstd_bcast(lambda c: self.XT[:, c, off:off + n], KC, off, n, 1.0 / D, 0, xk)
        for c in range(KC):
            S.add("dve", lambda h, c=c: h.scalar_tensor_tensor(
                out=hview[:, c, 0:n], in0=self.XT[:, c, off:off + n],
                scalar=self.GAIN[:, gidx * KC + c:gidx * KC + c + 1], in1=rstd, op0=ALU.mult, op1=ALU.mult),
                r=xk + [("STAT", 0), "GAIN"], w=hkeys)

    def postnorm_add(self, gidx, off, n, half):
        S = self.S
        outt = self.R1[:, 0:2 * KC * n].bitcast(F32).rearrange("p (c t) -> p c t", c=KC)
        okeys = ["HT", "WAALL"]
        xk = self.xt_keys(off, n)
        rstd = self.rstd_bcast(lambda c: outt[:, c, 0:n], KC, off, n, 1.0 / D, 1, okeys)
        if half != 1.0:
            S.add("pool", lambda h: h.tensor_scalar(out=rstd, in0=rstd, scalar1=half, scalar2=None, op0=ALU.mult),
                  r=[("STAT", 1)], w=[("STAT", 1)])
        for c in range(KC):
            S.add("dve", lambda h, c=c: h.scalar_tensor_tensor(
                out=outt[:, c, 0:n], in0=outt[:, c, 0:n],
                scalar=self.GAIN[:, gidx * KC + c:gidx * KC + c + 1], in1=rstd, op0=ALU.mult, op1=ALU.mult),
                r=okeys + [("STAT", 1), "GAIN"], w=okeys)
            S.add("pool", lambda h, c=c: h.tensor_tensor(
                out=self.XT[:, c, off:off + n], in0=self.XT[:, c, off:off + n], in1=outt[:, c, 0:n], op=ALU.add),
                r=okeys + xk, w=xk)

    def ffn(self, layer, which, sb):
        S, I = self.S, self.I
        off, n = sb
        S.barrier()
        blks = blocks_of((0, n))
        g0 = layer * 6 + (0 if which == 0 else 4)
        hview, hkeys = self.ht_view(n)
        self.prenorm(g0, off, n, hview, hkeys)
        wgu = I["w_ffn_gu"][layer, which]
        wdn = I["w_ffn_dn"][layer, which]
        actT = self.ARENA[:, 0:FC * n].rearrange("p (f t) -> p f t", f=FC)
        sil = self.SQ
        ring = [(self.R1[:, 9216:13312], [("WA", 9216), ]), (self.WD[:, 0, 0:4096], [("WD", 0)]),
                (self.R1[:, 13312:17408], [("WA", 13312)]), (self.WD[:, 1, 0:4096], [("WD", 1)])]

        def wload(src, ri):
            buf, keys = ring[ri % 4]
            view = buf.rearrange("p (c n) -> p c n", c=KC)
            S.dma("pool", view, src.rearrange("(c p) n -> p c n", p=128), r=["WAALL"], w=keys)
            return view, keys + ["WAALL"]

        for j2 in range(FC // 2):
            gv, gk = wload(wgu[:, j2 * 256:(j2 + 1) * 256], 2 * j2)
            uv, uk = wload(wgu[:, FF + j2 * 256:FF + (j2 + 1) * 256], 2 * j2 + 1)
            for sub in range(2):
                j = 2 * j2 + sub
                for bi, (bo, bn) in enumerate(blks):
                    bg, bu = self.psum(), self.psum()
                    for c in range(KC):
                        self.mm(self.PS[:, bg, 0:bn], gv[:, c, sub * 128:(sub + 1) * 128], hview[:, c, bo:bo + bn],
                                c == 0, c == KC - 1, gk + hkeys, [("ps", bg)])
                    for c in range(KC):
                        self.mm(self.PS[:, bu, 0:bn], uv[:, c, sub * 128:(sub + 1) * 128], hview[:, c, bo:bo + bn],
                                c == 0, c == KC - 1, uk + hkeys, [("ps", bu)])
                    self.af(sil[:, bi, 0:bn], self.PS[:, bg, 0:bn], AF.Silu, [("ps", bg)], [("SQ", bi)])
                    self.tt("dve", actT[:, j, bo:bo + bn], sil[:, bi, 0:bn], self.PS[:, bu, 0:bn], ALU.mult,
                            [("ps", bu), ("SQ", bi)], [("AT", j)])
        outt = self.R1[:, 0:2 * KC * n].bitcast(F32).rearrange("p (c t) -> p c t", c=KC)
        okeys = ["HT", "WAALL"]
        atk = [("AT", j) for j in range(FC)]
        HF = FC // 2
        for dcp in range(KC // 2):
            banks = [[self.psum() for _ in blks] for _ in range(2)]
            for half in range(2):
                s_ = self.wd_i % 2
                self.wd_i += 1
                wv = self.WD[:, s_, :].rearrange("p (f n) -> p f n", f=HF)
                S.dma("pool", wv, wdn[half * HF * 128:(half + 1) * HF * 128, dcp * 256:(dcp + 1) * 256].rearrange(
                    "(f p) n -> p f n", p=128), w=[("WD", s_)])
                for sub in range(2):
                    for bi, (bo, bn) in enumerate(blks):
                        b_ = banks[sub][bi]
                        for f in range(HF):
                            self.mm(self.PS[:, b_, 0:bn], wv[:, f, sub * 128:(sub + 1) * 128], actT[:, half * HF + f, bo:bo + bn],
                                    half == 0 and f == 0, half == 1 and f == HF - 1, [("WD", s_)] + atk, [("ps", b_)])
            for sub in range(2):
                for bi, (bo, bn) in enumerate(blks):
                    b_ = banks[sub][bi]
                    self.cp("act" if bi else "dve", outt[:, dcp * 2 + sub, bo:bo + bn], self.PS[:, b_, 0:bn], [("ps", b_)], okeys)
        self.postnorm_add(g0 + 1, off, n, 0.5)

    def ar(self, lo, n, dt=BF16):
        if dt == F32:
            return self.ARENA[:, lo:lo + 2 * n].bitcast(F32), self.at_keys(lo, lo + 2 * n)
        return self.ARENA[:, lo:lo + n], self.at_keys(lo, lo + n)

    def mem_kv(self, layer):
        S, I, O = self.S, self.I, self.O
        KT = self.WD[:, 0, 0:1024].rearrange("p (h n) -> p h n", h=4)
        V = self.WD[:, 0, 1024:2048].rearrange("p (c f) -> p c f", c=2)
        wdk = [("WD", 0)]
        st, stk = [], []
        for i in range(2):
            v, k = self.ar(10240 + i * 4096, 2048, F32)
            st.append(v)
            stk.append(k)
        mh, mhk = self.ar(18432, 4096)
        mh = mh.rearrange("p (c n) -> p c n", c=KC)
        rs = self.EPSC[:, 1:2]
        for i in range(2):
            S.dma("sp", st[i], I["mem"][i * 128:(i + 1) * 128, :], w=stk[i])
            junk = self.WD[:, 1, 0:4096].bitcast(F32)
            self.tt("pool", junk, st[i], st[i], ALU.mult, stk[i], [("WD", 1)])
            S.add("dve", lambda h, junk=junk: h.tensor_reduce(out=rs, in_=junk, axis=mybir.AxisListType.X, op=ALU.add),
                  r=[("WD", 1)], w=["EPSC2"])
            S.add("act", lambda h: h.activation(out=rs, in_=rs, func=AF.Ln, bias=self.EPSC[:, 0:1], scale=1.0 / D),
                  r=["EPSC2", "EPSC"], w=["EPSC2"])
            S.add("act", lambda h: h.activation(out=rs, in_=rs, func=AF.Exp, scale=-0.5), r=["EPSC2"], w=["EPSC2"])
            S.add("dve", lambda h, i=i: h.tensor_scalar(out=st[i], in0=st[i], scalar1=rs, scalar2=None, op0=ALU.mult),
                  r=stk[i] + ["EPSC2"], w=stk[i])
            for c4 in range(4):
                b = self.psum()
                for c in range(4):
                    cc = c4 * 4 + c
                    S.add("pe", lambda h, b=b, c=c, cc=cc, i=i: h.transpose(
                        self.PS[:, b, c * 128:(c + 1) * 128], st[i][:, cc * 128:(cc + 1) * 128], self.IDF[:]),
                        r=stk[i] + ["IDF"], w=[("ps", b)])
                for c in range(4):
                    cc = c4 * 4 + c
                    S.add("dve", lambda h, b=b, c=c, cc=cc, i=i: h.tensor_scalar(
                        out=mh[:, cc, i * 128:(i + 1) * 128], in0=self.PS[:, b, c * 128:(c + 1) * 128],
                        scalar1=self.MG[:, layer * KC + cc:layer * KC + cc + 1], scalar2=None, op0=ALU.mult),
                        r=[("ps", b), "MG"], w=mhk)
        kvf, kvfk = self.ar(10240, 2048, F32)
        kvf = kvf.rearrange("p (f n) -> p f n", f=8)
        for fc in range(8):
            wv, wk = self.wa_load(I["w_mem_kv"][layer][:, fc * 128:(fc + 1) * 128], 10240)
            b = self.psum()
            for c in range(KC):
                S.add("pe", lambda h, c=c, b=b, wv=wv: h.matmul(self.PS[:, b, 0:256], wv[:, c, :], mh[:, c, :],
                                                                 start=(c == 0), stop=(c == KC - 1)),
                      r=wk + mhk, w=[("ps", b)])
            S.add("dve", lambda h, b=b, fc=fc: h.tensor_copy(out=kvf[:, fc, :], in_=self.PS[:, b, 0:256]),
                  r=[("ps", b)], w=kvfk)
            if fc < 4:
                S.add("act", lambda h, b=b, fc=fc: h.activation(out=KT[:, fc, :], in_=self.PS[:, b, 0:256], func=AF.Copy),
                      r=[("ps", b)], w=wdk)
        tok, tokk = self.ar(14336, 1024, F32)
        for nc_ in range(2):
            for half in range(2):
                b = self.psum()
                for f in range(4):
                    S.add("pe", lambda h, b=b, f=f, half=half, nc_=nc_: h.transpose(
                        self.PS[:, b, f * 128:(f + 1) * 128], kvf[:, half * 4 + f, nc_ * 128:(nc_ + 1) * 128], self.IDF[:]),
                        r=kvfk + ["IDF"], w=[("ps", b)])
                S.add("dve", lambda h, b=b, half=half: h.tensor_copy(out=tok[:, half * 512:(half + 1) * 512], in_=self.PS[:, b, :]),
                      r=[("ps", b)], w=tokk)
                if half == 1:
                    S.add("act", lambda h, b=b, nc_=nc_: h.activation(out=V[:, nc_, :], in_=self.PS[:, b, :], func=AF.Copy),
                          r=[("ps", b)], w=wdk)
            if self.first_pass:
                S.dma("sp", O["memk"][layer, nc_ * 128:(nc_ + 1) * 128, :], tok[:, 0:512], r=tokk)
                S.dma("sp", O["memv"][layer, nc_ * 128:(nc_ + 1) * 128, :], tok[:, 512:1024], r=tokk)

    def mm(self, out, lhsT, rhs, start, stop, r, w):
        self.S.add("pe", lambda h: h.matmul(out, lhsT, rhs, start=start, stop=stop), r=r, w=w)

    def tr(self, out, in_, ident, r, w):
        self.S.add("pe", lambda h: h.transpose(out, in_, ident), r=r, w=w)

    def ts(self, eng, out, in0, s1, s2, op0, op1, r, w):
        if op1 is None:
            self.S.add(eng, lambda h: h.tensor_scalar(out=out, in0=in0, scalar1=s1, scalar2=None, op0=op0), r=r, w=w)
        else:
            self.S.add(eng, lambda h: h.tensor_scalar(out=out, in0=in0, scalar1=s1, scalar2=s2, op0=op0, op1=op1), r=r, w=w)

    def tt(self, eng, out, in0, in1, op, r, w):
        self.S.add(eng, lambda h: h.tensor_tensor(out=out, in0=in0, in1=in1, op=op), r=r, w=w)

    def stt(self, out, in0, scalar, in1, op0, op1, r, w):
        self.S.add("dve", lambda h: h.scalar_tensor_tensor(out=out, in0=in0, scalar=scalar, in1=in1, op0=op0, op1=op1), r=r, w=w)

    def af(self, out, in_, func, r, w, bias=None, scale=None):
        kw = {}
        if bias is not None:
            kw["bias"] = bias
        if scale is not None:
            kw["scale"] = scale
        self.S.add("act", lambda h: h.activation(out=out, in_=in_, func=func, **kw), r=r, w=w)

    def cp(self, eng, out, in_, r, w):
        if eng == "act":
            self.af(out, in_, AF.Copy, r, w)
        else:
            self.S.add(eng, lambda h: h.tensor_copy(out=out, in_=in_), r=r, w=w)

    def psb(self, b):
        return self.PS[:, b, :].bitcast(BF16)

    def sc_reset(self):
        self.sc_off = 8192
        self.sc2_off = 2048
        self.sc3_off = 18432
        self.sc4_off = 0

    def sc(self, name, n, dt=BF16, pool=0):
        ne = n * (2 if dt == F32 else 1)
        ne = (ne + 15) // 16 * 16
        if pool == 0:
            lo = self.sc_off
            self.sc_off += ne
            assert self.sc_off <= 25344, ("ARENA scratch overflow", name, self.sc_off)
            v = self.ARENA[:, lo:lo + ne]
        elif pool == 1:
            lo = self.sc2_off
            self.sc2_off += ne
            assert self.sc2_off <= 8192, ("R1 scratch overflow", name, self.sc2_off)
            v = self.R1[:, lo:lo + ne]
        elif pool == 2:
            lo = self.sc3_off
            self.sc3_off += ne
            assert self.sc3_off <= 19456, ("R1 tail overflow", name, self.sc3_off)
            v = self.R1[:, lo:lo + ne]
        else:
            lo = self.sc4_off
            self.sc4_off += ne
            assert self.sc4_off <= 5120, ("WD0 scratch overflow", name, self.sc4_off)
            v = self.WD[:, 0, lo:lo + ne]
        if dt == F32:
            v = v.bitcast(F32)
        return v[:, 0:n], [("sc", name)]

    def rows_to_featmajor(self, dst, dkeys, src_rows, nrows, stage, skeys):
        S = self.S
        S.dma("sp", stage[0:nrows, :], src_rows, w=skeys)
        b = self.psum()
        self.tr(self.PS[:, b, 0:nrows], stage[0:nrows, :], self.IDF[0:nrows, 0:nrows], skeys + ["IDF"], [("ps", b)])
        self.cp("dve", dst, self.PS[:, b, 0:nrows], [("ps", b)], dkeys)

    def mixer(self, layer, sbi, sb, commit=True):
        S, I, O = self.S, self.I, self.O
        off, n = sb
        nt = n // 128
        sample = (off >= TP)
        isA = (layer % 2 == 0)
        j = layer // 2
        S.barrier()
        self.sc_reset()
        if sbi == 0:
            self.mem_kv(layer)
            S.barrier()
            self.sc_reset()
        hview, hkeys = self.ht_view(n)
        self.prenorm(layer * 6 + 2, off, n, hview, hkeys)
        mixt = self.ARENA[:, 0:KC * n].rearrange("p (c t) -> p c t", c=KC)
        mk = lambda c: [("MIXT", c)]
        ctx = dict(layer=layer, j=j, off=off, n=n, nt=nt, sample=sample, hview=hview, hkeys=hkeys, mixt=mixt, mk=mk,
                   sbi=sbi, commit=commit)
        if "mix" in self.feat:
            if isA:
                self.delta_sb(ctx)
            else:
                self.gmlp_sb(ctx)
        else:
            for c in range(12):
                S.add("pool", lambda h, c=c: h.memset(mixt[:, c, :], 0.0), w=mk(c))
        if "attn" in self.feat:
            self.attn_sb(ctx)
        else:
            for c in range(12, 16):
                S.add("pool", lambda h, c=c: h.memset(mixt[:, c, :], 0.0), w=mk(c))
        self.outproj_sb(ctx)

    def outproj_sb(self, ctx):
        S, I = self.S, self.I
        layer, off, n, mixt, mk = ctx["layer"], ctx["off"], ctx["n"], ctx["mixt"], ctx["mk"]
        S.barrier()
        outt = self.R1[:, 0:2 * KC * n].bitcast(F32).rearrange("p (c t) -> p c t", c=KC)
        okeys = ["HT", "WAALL"]
        blks = blocks_of((0, n)) if n > 256 else [(0, n)]
        wo = I["w_out"][layer]
        allmk = [("MIXT", c) for c in range(KC)]
        for dc in range(KC):
            s = dc % 2
            lo = 1536 + s * 2048
            wv = self.WD[:, 1, lo:lo + 2048].rearrange("p (c n) -> p c n", c=KC)
            wk = [("WO", s)]
            S.dma("pool", wv, wo[:, dc * 128:(dc + 1) * 128].rearrange("(c p) n -> p c n", p=128), w=wk)
            for bi, (bo, bn) in enumerate(blks):
                b = self.psum()
                for c in range(KC):
                    self.mm(self.PS[:, b, 0:bn], wv[:, c, :], mixt[:, c, bo:bo + bn], c == 0, c == KC - 1,
                            wk + allmk, [("ps", b)])
                self.cp("act" if bi else "dve", outt[:, dc, bo:bo + bn], self.PS[:, b, 0:bn], [("ps", b)], okeys)
        self.postnorm_add(layer * 6 + 3, off, n, 1.0)

    def attn_sb(self, ctx):
        S, I = self.S, self.I
        layer, off, n, nt, sample = ctx["layer"], ctx["off"], ctx["n"], ctx["nt"], ctx["sample"]
        hview, hkeys, mixt, mk = ctx["hview"], ctx["hkeys"], ctx["mixt"], ctx["mk"]
        S.barrier()
        self.sc_reset()
        isA = (layer % 2 == 0)
        w_in = I["w_in_a"][layer // 2] if isA else I["w_in_b"][layer // 2]
        qcol0 = (IN_A - 512) if isA else (IN_B - 512)
        qT, qk = self.sc("qT", 4 * n)
        qT = qT.rearrange("p (h t) -> p h t", h=4)
        blks = blocks_of((0, n)) if n > 256 else [(0, n)]
        for hx in range(4):
            wv, wk = self.wa_load(w_in[:, qcol0 + hx * 128:qcol0 + (hx + 1) * 128], 8192)
            for bi, (bo, bn) in enumerate(blks):
                b = self.psum()
                for c in range(KC):
                    self.mm(self.PS[:, b, 0:bn], wv[:, c, :], hview[:, c, bo:bo + bn], c == 0, c == KC - 1,
                            wk + hkeys, [("ps", b)])
                self.cp("act", qT[:, hx, bo:bo + bn], self.PS[:, b, 0:bn], [("ps", b)], qk)
        P, Pk = self.sc("P", 512, F32)
        Pn, Pnk = self.sc("Pn", 512)
        PT, PTk = self.sc("PT", 512)
        sm, smk = self.sc("sm", 8, F32)
        P3 = P.rearrange("p (h n) -> p h n", h=2)
        Pn3 = Pn.rearrange("p (h n) -> p h n", h=2)
        scale = 128.0 ** -0.5

        def attend(col0, m, KT, V, kvk):
            for hp in range(2):
                b = self.psum()
                for hh in range(2):
                    self.mm(self.PS[0:m, b, hh * 256:(hh + 1) * 256], qT[:, 2 * hp + hh, col0:col0 + m], KT[:, 2 * hp + hh, :],
                            True, True, qk + kvk, [("ps", b)])
                ps3 = self.PS[0:m, b, :].rearrange("p (h n) -> p h n", h=2)
                S.add("dve", lambda h, ps3=ps3, m=m: h.tensor_reduce(out=sm[0:m, 0:2], in_=ps3, axis=mybir.AxisListType.X, op=ALU.max),
                      r=[("ps", b)], w=smk)
                self.tt("dve", P3[0:m], ps3, sm[0:m, 0:2].unsqueeze(2).broadcast_to([m, 2, 256]), ALU.subtract,
                        [("ps", b)] + smk, Pk)
                self.af(P[0:m], P[0:m], AF.Exp, Pk, Pk, scale=scale)
                S.add("dve", lambda h, m=m: h.tensor_reduce(out=sm[0:m, 2:4], in_=P3[0:m], axis=mybir.AxisListType.X, op=ALU.add),
                      r=Pk, w=smk)
                S.add("dve", lambda h, m=m: h.reciprocal(out=sm[0:m, 4:6], in_=sm[0:m, 2:4]), r=smk, w=smk)
                self.tt("dve", Pn3[0:m], P3[0:m], sm[0:m, 4:6].unsqueeze(2).broadcast_to([m, 2, 256]), ALU.mult,
                        Pk + smk, Pnk)
                b2 = self.psum()
                pb = self.psb(b2)
                for hh in range(2):
                    for ncnk in range(2):
                        q = hh * 2 + ncnk
                        self.tr(pb[:, q * m:(q + 1) * m], Pn3[0:m, hh, ncnk * 128:(ncnk + 1) * 128], self.IDB[0:m, 0:m],
                                Pnk + ["IDB"], [("ps", b2)])
                self.cp("act", PT[:, 0:4 * m], pb[:, 0:4 * m], [("ps", b2)], PTk)
                b3 = self.psum()
                for hh in range(2):
                    for ncnk in range(2):
                        q = hh * 2 + ncnk
                        hd = 2 * hp + hh
                        self.mm(self.PS[:, b3, hh * m:(hh + 1) * m], V[:, ncnk, hd * 128:(hd + 1) * 128], PT[:, q * m:(q + 1) * m],
                                ncnk == 0, ncnk == 1, PTk + kvk, [("ps", b3)])
                self.cp("dve", mixt[:, 12 + 2 * hp:12 + 2 * hp + 2, col0:col0 + m],
                        self.PS[:, b3, 0:2 * m].rearrange("p (h t) -> p h t", h=2), [("ps", b3)],
                        [("MIXT", 12 + 2 * hp), ("MIXT", 13 + 2 * hp)])

        if not sample:
            KT = self.WD[:, 0, 0:1024].rearrange("p (h n) -> p h n", h=4)
            V = self.WD[:, 0, 1024:2048].rearrange("p (c f) -> p c f", c=2)
            for t in range(nt):
                attend(t * 128, 128, KT, V, [("WD", 0)])
        else:
            bufs = []
            for i in range(2):
                kt_, ktk = self.sc("Ktok%d" % i, 1024)
                v_, vk = self.sc("Vs%d" % i, 1024)
                kT_, kTk = self.sc("KTs%d" % i, 1024)
                bufs.append((kt_, ktk, v_, vk, kT_, kTk))
            for s in range(16):
                kt_, ktk, v_, vk, kT_, kTk = bufs[s % 2]
                kt3 = kt_.rearrange("p (c f) -> p c f", c=2)
                v3 = v_.rearrange("p (c f) -> p c f", c=2)
                kT3 = kT_.rearrange("p (h n) -> p h n", h=4)
                S.dma("pool", kt3, I["ck"][layer, self.sq0 + s].rearrange("(c p) f -> p c f", p=128), w=ktk)
                S.dma("pool", v3, I["cv"][layer, self.sq0 + s].rearrange("(c p) f -> p c f", p=128), w=vk)
                b = self.psum()
                pb = self.psb(b)
                for hd in range(4):
                    for ncnk in range(2):
                        self.tr(pb[:, hd * 256 + ncnk * 128:hd * 256 + (ncnk + 1) * 128], kt3[:, ncnk, hd * 128:(hd + 1) * 128],
                                self.IDB[:], ktk + ["IDB"], [("ps", b)])
                self.cp("act", kT_, pb[:, 0:1024], [("ps", b)], kTk)
                attend(s * 8, 8, kT3, v3, kTk + vk)

    def gmlp_sb(self, ctx):
        S, I, O = self.S, self.I, self.O
        j, off, n, nt, sample = ctx["j"], ctx["off"], ctx["n"], ctx["nt"], ctx["sample"]
        hview, hkeys, mixt, mk = ctx["hview"], ctx["hkeys"], ctx["mixt"], ctx["mk"]
        w_in = I["w_in_b"][j]
        blks = blocks_of((0, n)) if n > 256 else [(0, n)]
        G = 12
        VG, VGk = self.sc("VG", G * n, F32)
        VG = VG.rearrange("p (g t) -> p g t", g=G)
        LG, LGk = self.sc("LG", 16, F32)
        LB, LBk = self.sc("LB", 16, F32)
        stg, stgk = self.sc("wsf", 128, F32)
        self.rows_to_featmajor(LG[:, 0:G], LGk, I["gmlp_ln_gain"][j].rearrange("(g p) -> g p", p=128), G, stg, stgk)
        self.rows_to_featmajor(LB[:, 0:G], LBk, I["gmlp_ln_bias"][j].rearrange("(g p) -> g p", p=128), G, stg, stgk)
        BROW, BRk = self.sc("BROW", 1536)
        S.dma("pool", BROW[0:1, :], I["b_spatial"][j:j + 1].rearrange("a g i -> a (g i)"), w=BRk)
        sq, sqk = self.sc("tmpA", n, F32)
        bsum = [self.psum() for _ in blks]
        bsq = [self.psum() for _ in blks]
        self.ps_reserved = set(bsum + bsq)
        for g in range(G):
            wv, wk = self.wa_load(w_in[:, 1536 + g * 128:1536 + (g + 1) * 128], 8192)
            for bi, (bo, bn) in enumerate(blks):
                b = self.psum()
                for c in range(KC):
                    self.mm(self.PS[:, b, 0:bn], wv[:, c, :], hview[:, c, bo:bo + bn], c == 0, c == KC - 1, wk + hkeys, [("ps", b)])
                self.af(VG[:, g, bo:bo + bn], self.PS[:, b, 0:bn], AF.Gelu, [("ps", b)], VGk)
            self.tt("pool", sq, VG[:, g, :], VG[:, g, :], ALU.mult, VGk, sqk)
            for bi, (bo, bn) in enumerate(blks):
                self.mm(self.PS[:, bsum[bi], 0:bn], self.ONF[:], VG[:, g, bo:bo + bn], g == 0, g == G - 1, VGk + ["ONF"], [("ps", bsum[bi])])
                self.mm(self.PS[:, bsq[bi], 0:bn], self.ONF[:], sq[:, bo:bo + bn], g == 0, g == G - 1, sqk + ["ONF"], [("ps", bsq[bi])])
        mean = self.STAT[:, 0, 0:n]
        rstd = self.STAT[:, 1, 0:n]
        for bi, (bo, bn) in enumerate(blks):
            self.ts("dve", mean[:, bo:bo + bn], self.PS[:, bsum[bi], 0:bn], 1.0 / 1536, None, ALU.mult, None, [("ps", bsum[bi])], [("STAT", 0)])
            self.tt("dve", sq[:, bo:bo + bn], mean[:, bo:bo + bn], mean[:, bo:bo + bn], ALU.mult, [("STAT", 0)], sqk)
            self.stt(rstd[:, bo:bo + bn], self.PS[:, bsq[bi], 0:bn], 1.0 / 1536, sq[:, bo:bo + bn], ALU.mult, ALU.subtract,
                     [("ps", bsq[bi])] + sqk, [("STAT", 1)])
        self.af(rstd, rstd, AF.Ln, [("STAT", 1), "EPSC"], [("STAT", 1)], bias=self.EPSC[:, 0:1], scale=1.0)
        self.af(rstd, rstd, AF.Exp, [("STAT", 1)], [("STAT", 1)], scale=-0.5)
        self.ps_reserved = set()
        vn, vnk = self.sc("vn", n, F32)
        vnb, vnbk = self.sc("vnb", n, BF16, 2)
        ug, ugk = sq, sqk
        wsf, wsfk = stg, stgk
        wsb, wsbk = self.sc("wsb", 128, BF16, 2)
        vtok, vtokk = self.sc("vtok", 128, BF16, 2)
        if sample:
            m8, m8k = self.sc("m8", 128, F32)
            rep, repk = self.sc("rep", 128, F32)
            S.dma("sp", m8, I["c_mask8"], w=m8k)
            S.dma("sp", rep, I["c_rep8"], w=repk)
            wst, wstk = self.sc("wst", 128, F32)
            wrep, wrepk = self.sc("wrep", 8, F32)
            brow8, br8k = self.sc("brow8", 128)
            gvst, gvstk = self.sc("gvst", 1536, F32)
        else:
            trl, trlk = self.sc("trl", 128, F32, 2)
            S.dma("sp", trl, I["c_tril_t"], w=trlk)
        for g in range(G):
            self.tt("dve", vn, VG[:, g, :], mean, ALU.subtract, VGk + [("STAT", 0)], vnk)
            self.tt("pool", vn, vn, rstd, ALU.mult, vnk + [("STAT", 1)], vnk)
            self.ts("dve", vn, vn, LG[:, g:g + 1], LB[:, g:g + 1], ALU.mult, ALU.add, vnk + LGk + LBk, vnk)
            self.cp("act", vnb, vn, vnk, vnbk)
            S.dma("sp", wsf, I["w_spatial"][j, g], w=wsfk)
            b = self.psum()
            self.tr(self.PS[:, b, 0:128], wsf, self.IDF[:], wsfk + ["IDF"], [("ps", b)])
            wv, wk = self.wa_load(w_in[:, g * 128:(g + 1) * 128], 8192)
            for bi, (bo, bn) in enumerate(blks):
                bu = self.psum()
                for c in range(KC):
                    self.mm(self.PS[:, bu, 0:bn], wv[:, c, :], hview[:, c, bo:bo + bn], c == 0, c == KC - 1, wk + hkeys, [("ps", bu)])
                self.af(ug[:, bo:bo + bn], self.PS[:, bu, 0:bn], AF.Gelu, [("ps", bu)], ugk)
            if not sample:
                self.tt("dve", wsb, self.PS[:, b, 0:128], trl, ALU.mult, [("ps", b)] + trlk, wsbk)
                for t in range(nt):
                    b2 = self.psum()
                    pb = self.psb(b2)
                    self.tr(pb[:, 0:128], vnb[:, t * 128:(t + 1) * 128], self.IDB[:], vnbk + ["IDB"], [("ps", b2)])
                    self.cp("act", vtok, pb[:, 0:128], [("ps", b2)], vtokk)
                    b3 = self.psum()
                    self.mm(self.PS[:, b3, 0:128], vtok, wsb, True, False, vtokk + wsbk, [("ps", b3)])
                    self.mm(self.PS[:, b3, 0:128], self.ONB[0:1, :], BROW[0:1, g * 128:(g + 1) * 128], False, True,
                            BRk + ["ONB"], [("ps", b3)])
                    self.tt("dve", mixt[:, g, t * 128:(t + 1) * 128], ug[:, t * 128:(t + 1) * 128], self.PS[:, b3, 0:128], ALU.mult,
                            ugk + [("ps", b3)], mk(g))
            else:
                self.cp("dve", wst, self.PS[:, b, 0:128], [("ps", b)], wstk)
                b4 = self.psum()
                self.mm(self.PS[:, b4, 0:8], rep, wst[:, 0:8], True, True, repk + wstk, [("ps", b4)])
                self.cp("dve", wrep, self.PS[:, b4, 0:8], [("ps", b4)], wrepk)
                self.tt("dve", wsb.rearrange("p (s i) -> p s i", s=16), wrep.unsqueeze(1).broadcast_to([128, 16, 8]),
                        m8.rearrange("p (s i) -> p s i", s=16), ALU.mult, wrepk + m8k, wsbk)
                self.cp("dve", brow8[0:1, :].rearrange("p (s i) -> p s i", s=16),
                        BROW[0:1, g * 128:g * 128 + 8].unsqueeze(1).broadcast_to([1, 16, 8]), BRk, br8k)
                b2 = self.psum()
                pb = self.psb(b2)
                self.tr(pb[:, 0:128], vnb[:, 0:128], self.IDB[:], vnbk + ["IDB"], [("ps", b2)])
                self.cp("act", vtok, pb[:, 0:128], [("ps", b2)], vtokk)
                b3 = self.psum()
                self.mm(self.PS[:, b3, 0:128], vtok, wsb, True, False, vtokk + wsbk, [("ps", b3)])
                self.mm(self.PS[:, b3, 0:128], self.ONB[0:1, :], brow8[0:1, :], False, True, br8k + ["ONB"], [("ps", b3)])
                self.tt("dve", mixt[:, g, 0:128], ug[:, 0:128], self.PS[:, b3, 0:128], ALU.mult, ugk + [("ps", b3)], mk(g))
                b5 = self.psum()
                self.tr(self.PS[:, b5, 0:128], vn[:, 0:128], self.IDF[:], vnk + ["IDF"], [("ps", b5)])
                self.cp("act", gvst[:, g * 128:(g + 1) * 128], self.PS[:, b5, 0:128], [("ps", b5)], gvstk)
        if sample:
            S.dma("sp", O["gv_s"][j, self.sq0 * 8:self.sq0 * 8 + 128, :], gvst, r=gvstk)

    def delta_sb(self, ctx):
        S, I, O = self.S, self.I, self.O
        j, off, n, nt, sample, sbi = ctx["j"], ctx["off"], ctx["n"], ctx["nt"], ctx["sample"], ctx["sbi"]
        hview, hkeys, mixt, mk, commit = ctx["hview"], ctx["hkeys"], ctx["mixt"], ctx["mk"], ctx["commit"]
        w_in = I["w_in_a"][j]
        B = 8 if sample else 64
        nb = 128 // B
        nlev = 2 if sample else 5
        nseq, L = (16, 8) if sample else (1, n)
        blks = blocks_of((0, n)) if n > 256 else [(0, n)]
        Sf = self.WD[:, 0, 2048:5120].bitcast(F32).rearrange("p (h v) -> p h v", h=NH)
        Sb = self.WD[:, 1, 0:1536].rearrange("p (h v) -> p h v", h=NH)
        CT = self.WD[:, 0, 5120:5336].bitcast(F32).rearrange("p (c r) -> p c r", c=36)
        Sfk = lambda h: [("Sf", h)]
        Sbk = lambda h: [("Sb", h)]
        CTk = ["CT"]
        allSf = [("Sf", h) for h in range(NH)]
        allSb = [("Sb", h) for h in range(NH)]
        if sbi == 0 and self.first_pass:
            S.add("pool", lambda h: h.memset(self.WD[:, 0, 2048:5120].bitcast(F32), 0.0), w=allSf)
            S.add("pool", lambda h: h.memset(self.WD[:, 1, 0:1536], 0.0), w=allSb)
            S.add("pool", lambda h: h.memset(self.WD[:, 0, 5120:5336].bitcast(F32), 0.0), w=CTk)
        elif sbi == 0:
            S.dma("sp", self.WD[:, 0, 2048:5120].bitcast(F32), self.scrS[j], r=[("scrS", j)], w=allSf)
            S.dma("pool", self.WD[:, 1, 0:1536], self.scrS[j], r=[("scrS", j)], w=allSb)
            S.dma("sp", self.WD[:, 0, 5120:5336].bitcast(F32), self.scrCT[j], r=[("scrCT", j)], w=CTk)
        if sample:
            NEG, NEGk = self.sc("NEG8", 128, F32)
            STR, STRk = self.sc("STR8", 128, F32)
            UBL, UBLk = self.sc("UBL8", 128, F32)
            BBL, BBLk = self.sc("BBL8", 128, F32)
            COLM, COLMk = self.sc("COLM8", nb * 128)
            ROWM, ROWMk = self.sc("ROWM8", nb, F32)
            for v_, k_, nm in ((NEG, NEGk, "neg8"), (STR, STRk, "strict8"), (UBL, UBLk, "ublk8"), (BBL, BBLk, "bblk8"),
                               (ROWM, ROWMk, "rowmask8")):
                S.dma("sp", v_, I["c_" + nm], w=k_)
            S.dma("pool", COLM, I["c_colmask8"], w=COLMk)
        else:
            NEG, STR, UBL, BBL, COLM, ROWM = self.M64
            NEGk = STRk = UBLk = BBLk = COLMk = ROWMk = ["M64"]
        COLM3 = COLM.rearrange("p (s t) -> p s t", s=nb)
        stg, stgk = self.sc("stg", 128, F32)
        CW = []
        for r_ in range(4):
            cw, cwk = self.sc("CW%d" % r_, 36, F32)
            self.rows_to_featmajor(cw, cwk, I["conv_w"][j, r_].rearrange("(c p) -> c p", p=128), 36, stg, stgk)
            CW.append((cw, cwk))
        DNG, DNGk = self.sc("DNG", 1, F32)
        self.rows_to_featmajor(DNG, DNGk, I["delta_norm_gain"][j:j + 1, :], 1, stg, stgk)
        DTB, DTBk = self.sc("DTB", NH, F32)
        NEGA, NEGAk = self.sc("NEGA", NH, F32)
        S.dma("sp", DTB, I["dt_bias"][j:j + 1, :].broadcast_to([128, NH]), w=DTBk)
        S.dma("sp", NEGA, I["a_log"][j:j + 1, :].broadcast_to([128, NH]), w=NEGAk)
        self.af(NEGA, NEGA, AF.Exp, NEGAk, NEGAk)
        self.ts("dve", NEGA, NEGA, -1.0, None, ALU.mult, None, NEGAk, NEGAk)
        Wab, Wabk = self.sc("Wab", KC * 24)
        Wab = Wab.rearrange("p (c n) -> p c n", c=KC)
        S.dma("pool", Wab, w_in[:, 6144:6168].rearrange("(c p) n -> p c n", p=128), w=Wabk)

        def scal(name):
            v_, k_ = self.sc(name, nt * NH, F32)
            return v_.rearrange("p (t h) -> p t h", t=nt), k_
        BETA, BETAk = scal("BETA")
        G, Gk = scal("G")
        NG, NGk = scal("NG")
        GC, GCk = scal("GC")
        EDEC, EDECk = scal("EDEC")
        EGC, EGCk = scal("EGC")
        BE, BEk = scal("BE")
        GLB, GLBk = self.sc("GLB", nt * nb * NH, F32)
        GLB = GLB.rearrange("p (t s h) -> p t s h", t=nt, s=nb)
        G2, G2k = self.sc("G2", nb * NH, F32)
        for t in range(nt):
            b = self.psum()
            for c in range(KC):
                self.mm(self.PS[:, b, 0:24], hview[:, c, t * 128:(t + 1) * 128], Wab[:, c, :], c == 0, c == KC - 1,
                        hkeys + Wabk, [("ps", b)])
            self.af(BETA[:, t, :], self.PS[:, b, 12:24], AF.Sigmoid, [("ps", b)], BETAk)
            self.tt("dve", G[:, t, :], self.PS[:, b, 0:12], DTB, ALU.add, [("ps", b)] + DTBk, Gk)
            self.af(G[:, t, :], G[:, t, :], AF.Exp, Gk, Gk)
            self.af(G[:, t, :], G[:, t, :], AF.Ln, Gk + ["ONEC"], Gk, bias=self.ONEC[:, 0:1], scale=1.0)
            self.tt("dve", G[:, t, :], G[:, t, :], NEGA, ALU.mult, Gk + NEGAk, Gk)
            self.ts("dve", NG[:, t, :], G[:, t, :], -1.0, None, ALU.mult, None, Gk, NGk)
            b1 = self.psum()
            self.mm(self.PS[:, b1, 0:12], UBL, G[:, t, :], True, True, UBLk + Gk, [("ps", b1)])
            self.mm(self.PS[:, b1, 16:28], BBL, G[:, t, :], True, True, BBLk + Gk, [("ps", b1)])
            self.cp("dve", GC[:, t, :], self.PS[:, b1, 0:12], [("ps", b1)], GCk)
            self.af(EGC[:, t, :], self.PS[:, b1, 0:12], AF.Exp, [("ps", b1)], EGCk)
            self.tt("dve", EDEC[:, t, :], self.PS[:, b1, 16:28], GC[:, t, :], ALU.subtract, [("ps", b1)] + GCk, EDECk)
            self.af(EDEC[:, t, :], EDEC[:, t, :], AF.Exp, EDECk, EDECk)
            self.tt("dve", BE[:, t, :], BETA[:, t, :], EGC[:, t, :], ALU.mult, BETAk + EGCk, BEk)
            self.tt("dve", G2.rearrange("p (s h) -> p s h", s=nb), G[:, t, :].unsqueeze(1).broadcast_to([128, nb, NH]),
                    ROWM.unsqueeze(2).broadcast_to([128, nb, NH]), ALU.mult, Gk + ROWMk, G2k)
            b2 = self.psum()
            self.mm(self.PS[:, b2, 0:nb * NH], self.ONF[:], G2, True, True, G2k + ["ONF"], [("ps", b2)])
            self.af(GLB[:, t].rearrange("p s h -> p (s h)"), self.PS[:, b2, 0:nb * NH], AF.Exp, [("ps", b2)], GLBk)
        if sample:
            SCT = self.WD[:, 1, 1536:1536 + 2 * 36 * 48].bitcast(F32).rearrange("p (c q) -> p c q", c=36)
            SCTk = ["SCT"]
            cst, cstk = self.sc("cst", 512, F32)
            src = I["sconv"][j, self.sq0:self.sq0 + 16].rearrange("s r f -> (s r) f")
            for c9 in range(9):
                S.dma("sp", cst[0:48, :], src[:, c9 * 512:(c9 + 1) * 512], w=cstk)
                b = self.psum()
                for c in range(4):
                    self.tr(self.PS[:, b, c * 48:(c + 1) * 48], cst[0:48, c * 128:(c + 1) * 128], self.IDF[0:48, 0:48],
                            cstk + ["IDF"], [("ps", b)])
                self.cp("dve", SCT[:, c9 * 4:(c9 + 1) * 4, :], self.PS[:, b, 0:192].rearrange("p (c q) -> p c q", c=4),
                        [("ps", b)], SCTk)
        W3 = L + 3
        PQ = []
        for part in range(3):
            v_, k_ = self.sc("PQ%d" % part, nseq * W3 + 8, F32)
            PQ.append((v_[:, 0:nseq * W3].rearrange("p (s w) -> p s w", s=nseq), k_))
        YC, YCk = self.sc("YC", n, F32)
        YC3 = YC.rearrange("p (s l) -> p s l", s=nseq)
        QT, QTk = self.sc("QT", n)
        KTb, KTbk = self.sc("KTb", n)
        VTb, VTbk = self.sc("VTf", n, F32)
        KTf, KTfk = self.sc("KTf", n, F32)
        SZ, SZk = self.sc("SZ", n, F32)
        if sample:
            S0f, S0fk = self.sc("S0f", 16 * 128, F32, 1)
            S0b, S0bk = self.sc("S0b", 16 * 128, BF16, 3)
            S0f3 = S0f.rearrange("p (s v) -> p s v", s=16)
            S0b3 = S0b.rearrange("p (s v) -> p s v", s=16)
        nbuf = 1
        TB = []
        for i in range(nbuf):
            d = {}
            for nm, sz, dt in (("KBE", 128, F32), ("KDEC", 128, BF16), ("VB", 128, F32), ("E", 128, F32), ("L1", 128, F32),
                               ("Lb0", 128, F32), ("Rb0", 128, F32), ("Lb1", 128, F32), ("Rb1", 128, F32),
                               ("Ab", 128, BF16), ("ATb", 128, BF16), ("Xf", 128, F32), ("U", 128, F32),
                               ("WT", 128, BF16), ("QD", 128, BF16), ("GUN", 128, F32),
                               ("WM", nb * 128, BF16), ("QM", nb * 128, BF16), ("KDM", nb * 128, BF16),
                               ("VN", 128, BF16), ("O", 128, F32), ("ON", 128, F32)):
                pl = 0
                if sample and nm == "KDM":
                    pl = 3
                if sample and nm == "QM":
                    pl = 1
                d[nm] = self.sc("%s_%d" % (nm, i), sz, dt, pl)
            if not sample:
                d["ATM"] = self.sc("ATM_%d" % i, nb * 128, BF16)
            TB.append(d)

        for h in range(NH):
            for part in range(3):
                cidx = part * NH + h
                pq, pqk = PQ[part]
                wv, wk = self.wa_load(w_in[:, part * 1536 + h * 128:part * 1536 + (h + 1) * 128], 8192)
                if sample:
                    self.cp("pool", pq[:, :, 0:3], SCT[:, cidx, :].rearrange("p (s r) -> p s r", s=16), SCTk, pqk)
                else:
                    self.cp("pool", pq[:, 0, 0:3], CT[:, cidx, :], CTk, pqk)
                for bi, (bo, bn) in enumerate(blks):
                    b = self.psum()
                    for c in range(KC):
                        self.mm(self.PS[:, b, 0:bn], wv[:, c, :], hview[:, c, bo:bo + bn], c == 0, c == KC - 1, wk + hkeys, [("ps", b)])
                    if sample:
                        self.cp("act", pq[:, :, 3:3 + L], self.PS[:, b, 0:128].rearrange("p (s l) -> p s l", s=16), [("ps", b)], pqk)
                    else:
                        self.cp("act", pq[:, 0, 3 + bo:3 + bo + bn], self.PS[:, b, 0:bn], [("ps", b)], pqk)
                cw = lambda r_: CW[r_][0][:, cidx:cidx + 1]
                cwk_all = CW[0][1] + CW[1][1] + CW[2][1] + CW[3][1]
                self.ts("dve", YC3, pq[:, :, 0:L], cw(0), None, ALU.mult, None, pqk + cwk_all, YCk)
                for r_ in range(1, 4):
                    self.stt(YC3, pq[:, :, r_:r_ + L], cw(r_), YC3, ALU.mult, ALU.add, pqk + cwk_all + YCk, YCk)
                if sample:
                    self.cp("pool", SCT[:, cidx, :].rearrange("p (s r) -> p s r", s=16), pq[:, :, L:L + 3], pqk, SCTk)
                else:
                    self.cp("pool", CT[:, cidx, :], pq[:, 0, L:L + 3], pqk, CTk)
                self.af(YC, YC, AF.Silu, YCk, YCk)
                if part < 2:
                    sq = self.SQ[:, 0, 0:n]
                    self.tt("pool", sq, YC, YC, ALU.mult, YCk, [("SQ", 0)])
                    rn = self.STAT[:, 0, 0:n]
                    for bi, (bo, bn) in enumerate(blks):
                        b = self.psum()
                        self.mm(self.PS[:, b, 0:bn], self.ONF[:], sq[:, bo:bo + bn], True, True, [("SQ", 0), "ONF"], [("ps", b)])
                        self.af(rn[:, bo:bo + bn], self.PS[:, b, 0:bn], AF.Ln, [("ps", b), "EPSC"], [("STAT", 0)],
                                bias=self.EPSC[:, 0:1], scale=1.0)
                    self.af(rn, rn, AF.Exp, [("STAT", 0)], [("STAT", 0)], scale=-0.5)
                    if part == 0:
                        self.stt(QT, YC, 128.0 ** -0.5, rn, ALU.mult, ALU.mult, YCk + [("STAT", 0)], QTk)
                    else:
                        self.tt("dve", KTf, YC, rn, ALU.mult, YCk + [("STAT", 0)], KTfk)
                        self.cp("pool", KTb, KTf, KTfk, KTbk)
                else:
                    self.cp("dve", VTb, YC, YCk, VTbk)
            wv, wk = self.wa_load(w_in[:, 4608 + h * 128:4608 + (h + 1) * 128], 8192)
            for bi, (bo, bn) in enumerate(blks):
                b = self.psum()
                for c in range(KC):
                    self.mm(self.PS[:, b, 0:bn], wv[:, c, :], hview[:, c, bo:bo + bn], c == 0, c == KC - 1, wk + hkeys, [("ps", b)])
                self.af(SZ[:, bo:bo + bn], self.PS[:, b, 0:bn], AF.Silu, [("ps", b)], SZk)
            if sample:
                S.dma("sp", S0f3, I["sdelta"][j, self.sq0:self.sq0 + 16, h].rearrange("s k v -> k s v"), w=S0fk)
                S.dma("pool", S0b3, I["sdelta"][j, self.sq0:self.sq0 + 16, h].rearrange("s k v -> k s v"), w=S0bk)
            for t in range(nt):
                d = TB[t % nbuf]
                tc = slice(t * 128, (t + 1) * 128)
                col = lambda arr: arr[:, t, h:h + 1]
                KBE, KBEk = d["KBE"]; KDEC, KDECk = d["KDEC"]; VB, VBk = d["VB"]
                E, Ek = d["E"]; L1, L1k = d["L1"]; Ab, Abk = d["Ab"]; ATb, ATbk = d["ATb"]
                Xf, Xfk = d["Xf"]; Xb, Xbk = Xf, Xfk; U, Uk = d["U"]; WT, WTk = d["WT"]; QD, QDk = d["QD"]
                GUN, GUNk = d["GUN"]; DG, DGk = GUN, GUNk; WM, WMk = d["WM"]; QM, QMk = d["QM"]; KDM, KDMk = d["KDM"]
                VN, VNk = d["VN"]; Ot, Otk = d["O"]; ON, ONk = d["ON"]
                WM3 = WM.rearrange("p (s t) -> p s t", s=nb)
                QM3 = QM.rearrange("p (s t) -> p s t", s=nb)
                KDM3 = KDM.rearrange("p (s t) -> p s t", s=nb)
                b = self.psum()
                pb = self.PS[:, b, :]
                self.tr(pb[:, 0:128], KTf[:, tc], self.IDF[:], KTfk + ["IDF"], [("ps", b)])
                self.tr(pb[:, 128:256], VTb[:, tc], self.IDF[:], VTbk + ["IDF"], [("ps", b)])
                self.ts("dve", KBE, pb[:, 0:128], col(BE), None, ALU.mult, None, [("ps", b)] + BEk, KBEk)
                self.ts("dve", KDEC, pb[:, 0:128], col(EDEC), None, ALU.mult, None, [("ps", b)] + EDECk, KDECk)
                self.af(VB, pb[:, 128:256], AF.Copy, [("ps", b)] + BETAk, VBk, scale=col(BETA))
                self.ts("pool", GUN, UBL, col(NG), None, ALU.mult, None, UBLk + NGk, GUNk)
                b = self.psum()
                self.mm(self.PS[:, b, 0:128], self.ONF[:], GUN, True, False, GUNk + ["ONF"], [("ps", b)])
                self.mm(self.PS[:, b, 0:128], self.IDF[:], NEG, False, True, NEGk + ["IDF"], [("ps", b)])
                self.af(E, self.PS[:, b, 0:128], AF.Exp, [("ps", b)] + GCk, Ek, bias=col(GC), scale=1.0)
                bk = self.psum()
                self.mm(self.PS[:, bk, 0:128], KTb[:, tc], KTb[:, tc], True, True, KTbk, [("ps", bk)])
                self.mm(self.PS[:, bk, 128:256], QT[:, tc], KTb[:, tc], True, True, QTk + KTbk, [("ps", bk)])
                self.stt(L1, self.PS[:, bk, 0:128], col(BETA), E, ALU.mult, ALU.mult, [("ps", bk)] + BETAk + Ek, L1k)
                Lb, Lbk = d["Lb0"]
                Rb, Rbk = d["Rb0"]
                self.tt("pool", Lb, L1, STR, ALU.mult, L1k + STRk, Lbk)
                self.tt("dve", Ab, self.PS[:, bk, 128:256], E, ALU.mult, [("ps", bk)] + Ek, Abk)
                b = self.psum()
                self.tr(self.PS[:, b, 0:128], Lb, self.IDF[:], Lbk + ["IDF"], [("ps", b)])
                self.cp("act", Rb, self.PS[:, b, 0:128], [("ps", b)], Rbk)
                self.tt("dve", Xf, self.IDF[:], Rb, ALU.subtract, Rbk + ["IDF"], Xfk)
                b = self.psum()
                pb = self.psb(b)
                self.tr(pb[:, 0:128], Ab, self.IDB[:], Abk + ["IDB"], [("ps", b)])
                self.cp("act", ATb, pb[:, 0:128], [("ps", b)], ATbk)
                for lev in range(nlev):
                    L2, L2k = d["Lb%d" % ((lev + 1) % 2)]
                    R2, R2k = d["Rb%d" % ((lev + 1) % 2)]
                    b = self.psum()
                    self.mm(self.PS[:, b, 0:128], Rb, Lb, True, True, Rbk + Lbk, [("ps", b)])
                    if lev < nlev - 1:
                        self.mm(self.PS[:, b, 128:256], Lb, Rb, True, True, Rbk + Lbk, [("ps", b)])
                    self.cp("act", L2, self.PS[:, b, 0:128], [("ps", b)], L2k)
                    if lev < nlev - 1:
                        self.cp("dve", R2, self.PS[:, b, 128:256], [("ps", b)], R2k)
                    b2 = self.psum()
                    self.mm(self.PS[:, b2, 0:128], L2, Xb, True, True, L2k + Xbk, [("ps", b2)])
                    self.tt("dve", Xf, Xf, self.PS[:, b2, 0:128], ALU.add, Xfk + [("ps", b2)], Xfk)
                    Lb, Lbk, Rb, Rbk = L2, L2k, R2, R2k
                b = self.psum()
                self.mm(self.PS[:, b, 0:128], Xb, VB, True, True, Xbk + VBk, [("ps", b)])
                self.mm(self.PS[:, b, 128:256], KBE, Xb, True, True, Xbk + KBEk, [("ps", b)])
                self.cp("act", U, self.PS[:, b, 0:128], [("ps", b)], Uk)
                self.cp("dve", WT, self.PS[:, b, 128:256], [("ps", b)], WTk)
                self.ts("pool", DG, self.IDF[:], col(EGC), None, ALU.mult, None, EGCk + ["IDF"], DGk)
                b = self.psum()
                self.mm(self.PS[:, b, 0:128], self.ONF[:], DG, True, True, DGk + ["ONF"], [("ps", b)])
                self.tt("dve", QD, QT[:, tc], self.PS[:, b, 0:128], ALU.mult, QTk + [("ps", b)], QDk)
                self.tt("pool", WM3, WT.unsqueeze(1).broadcast_to([128, nb, 128]), COLM3, ALU.mult, WTk + COLMk, WMk)
                self.tt("pool", QM3, QD.unsqueeze(1).broadcast_to([128, nb, 128]), COLM3, ALU.mult, QDk + COLMk, QMk)
                self.tt("pool", KDM3, KDEC.unsqueeze(1).broadcast_to([128, nb, 128]),
                        ROWM.unsqueeze(2).broadcast_to([128, nb, 128]), ALU.mult, KDECk + ROWMk, KDMk)
                if not sample:
                    ATM, ATMk = d["ATM"]
                    ATM3 = ATM.rearrange("p (s t) -> p s t", s=nb)
                    self.tt("pool", ATM3, ATb.unsqueeze(1).broadcast_to([128, nb, 128]), COLM3, ALU.mult, ATbk + COLMk, ATMk)
                    for s in range(nb):
                        b = self.psum()
                        self.mm(self.PS[:, b, 0:128], WM3[:, s, :], Sb[:, h, :], True, True, WMk + Sbk(h), [("ps", b)])
                        self.tt("dve", VN, U, self.PS[:, b, 0:128], ALU.subtract, Uk + [("ps", b)], VNk)
                        bo_ = self.psum()
                        self.mm(self.PS[:, bo_, 0:128], QM3[:, s, :], Sb[:, h, :], True, False, QMk + Sbk(h), [("ps", bo_)])
                        self.mm(self.PS[:, bo_, 0:128], ATM3[:, s, :], VN, False, True, ATMk + VNk, [("ps", bo_)])
                        if s == 0:
                            self.cp("act", Ot, self.PS[:, bo_, 0:128], [("ps", bo_)], Otk)
                        else:
                            self.tt("dve", Ot, Ot, self.PS[:, bo_, 0:128], ALU.add, Otk + [("ps", bo_)], Otk)
                        bs = self.psum()
                        self.mm(self.PS[:, bs, 0:128], KDM3[:, s, :], VN, True, True, KDMk + VNk, [("ps", bs)])
                        self.stt(Sf[:, h, :], Sf[:, h, :], GLB[:, t, s, h:h + 1], self.PS[:, bs, 0:128], ALU.mult, ALU.add,
                                 Sfk(h) + GLBk + [("ps", bs)], Sfk(h))
                        self.cp("act", Sb[:, h, :], Sf[:, h, :], Sfk(h), Sbk(h))
                else:
                    b = self.psum()
                    for s in range(nb):
                        self.mm(self.PS[:, b, 0:128], WM3[:, s, :], S0b3[:, s, :], s == 0, s == nb - 1, WMk + S0bk, [("ps", b)])
                    self.tt("dve", VN, U, self.PS[:, b, 0:128], ALU.subtract, Uk + [("ps", b)], VNk)
                    bo_ = self.psum()
                    for s in range(nb):
                        self.mm(self.PS[:, bo_, 0:128], QM3[:, s, :], S0b3[:, s, :], s == 0, False, QMk + S0bk, [("ps", bo_)])
                    self.mm(self.PS[:, bo_, 0:128], ATb, VN, False, True, ATbk + VNk, [("ps", bo_)])
                    self.cp("act", Ot, self.PS[:, bo_, 0:128], [("ps", bo_)], Otk)
                    for s in range(nb):
                        bs = self.psum()
                        self.mm(self.PS[:, bs, 0:128], KDM3[:, s, :], VN, True, True, KDMk + VNk, [("ps", bs)])
                        self.stt(S0f3[:, s, :], S0f3[:, s, :], GLB[:, t, s, h:h + 1], self.PS[:, bs, 0:128], ALU.mult, ALU.add,
                                 S0fk + GLBk + [("ps", bs)], S0fk)
                    if commit:
                        S.dma("sp", O["delta_s"][j, self.sq0:self.sq0 + 16, h].rearrange("s k v -> k s v"), S0f3, r=S0fk)
                rs = self.SM[:, 0:1]
                self.tt("pool", L1, Ot, Ot, ALU.mult, Otk, L1k)
                S.add("dve", lambda hh, L1=L1: hh.tensor_reduce(out=self.SM[:, 0:1], in_=L1, axis=mybir.AxisListType.X, op=ALU.add),
                      r=L1k, w=["SM"])
                self.af(rs, rs, AF.Ln, ["SM", "EPSC"], ["SM"], bias=self.EPSC[:, 0:1], scale=1.0 / 128)
                self.af(rs, rs, AF.Exp, ["SM"], ["SM"], scale=-0.5)
                self.ts("dve", ON, Ot, rs, None, ALU.mult, None, Otk + ["SM"], ONk)
                b = self.psum()
                self.tr(self.PS[:, b, 0:128], ON, self.IDF[:], ONk + ["IDF"], [("ps", b)])
                self.stt(mixt[:, h, tc], self.PS[:, b, 0:128], DNG[:, 0:1], SZ[:, tc], ALU.mult, ALU.mult, [("ps", b)] + DNGk + SZk, mk(h))
        if commit and sample:
            cst2, cst2k = cst, cstk
            dst = O["conv_s"][j, self.sq0:self.sq0 + 16].rearrange("s r f -> (s r) f")
            for c9 in range(9):
                b = self.psum()
                for c in range(4):
                    self.tr(self.PS[0:48, b, c * 128:(c + 1) * 128], SCT[:, c9 * 4 + c, :], self.IDF[:], SCTk + ["IDF"], [("ps", b)])
                self.cp("dve", cst2[0:48, :], self.PS[0:48, b, :], [("ps", b)], cst2k)
                S.dma("sp", dst[:, c9 * 512:(c9 + 1) * 512], cst2[0:48, :], r=cst2k)
        if (not sample) and sbi == 1 and not self.last_pass:
            S.dma("sp", self.scrS[j], self.WD[:, 0, 2048:5120].bitcast(F32), r=allSf, w=[("scrS", j)])
            S.dma("sp", self.scrCT[j], self.WD[:, 0, 5120:5336].bitcast(F32), r=CTk, w=[("scrCT", j)])
        if commit and (not sample) and sbi == 1 and self.last_pass:
            S.dma("sp", O["delta_p"][j].rearrange("h k v -> k h v"), Sf, r=allSf)
            cst2, cst2k = YC, YCk
            for c9 in range(9):
                b = self.psum()
                for c in range(4):
                    self.tr(self.PS[0:3, b, c * 128:(c + 1) * 128], CT[:, c9 * 4 + c, :], self.IDF[:], CTk + ["IDF"], [("ps", b)])
                self.cp("dve", cst2[0:3, :], self.PS[0:3, b, :], [("ps", b)], cst2k)
                S.dma("sp", O["conv_p"][j][:, c9 * 512:(c9 + 1) * 512], cst2[0:3, :], r=cst2k)


_PROG = None


def get_prog():
    global _PROG
    if _PROG is None:
        _PROG = Prog()
    return _PROG


def kernel(**inp):
    prog = get_prog()
    return run_prog(prog, inp, 4)


def run_prog(prog, inp, ncores):
    consts = prog.consts
    f32 = lambda a: np.ascontiguousarray(a, dtype=np.float32)
    in_maps = []
    shared = {k: f32(inp[k]) for k in ("norm_gains", "w_ffn_gu", "w_ffn_dn", "w_in_a", "conv_w", "a_log", "dt_bias",
                                        "delta_norm_gain", "w_in_b", "gmlp_ln_gain", "gmlp_ln_bias", "w_spatial",
                                        "b_spatial", "mem_norm_gain", "w_mem_kv", "w_out")}
    for k, v in consts.items():
        shared["c_" + k] = f32(v)
    for c in range(ncores):
        b = c % 4
        m = dict(shared)
        m["xp"] = f32(inp["x_prompt"][b])
        m["xs"] = f32(inp["x_sample"][c * 32:(c + 1) * 32].reshape(2 * TS, D))
        m["mem"] = f32(inp["mem_prompt"][b])
        m["ck"] = f32(inp["cache_mem_k"][:, c * 32:(c + 1) * 32].reshape(DEPTH, 32, NMEM, 512))
        m["cv"] = f32(inp["cache_mem_v"][:, c * 32:(c + 1) * 32].reshape(DEPTH, 32, NMEM, 512))
        m["sdelta"] = f32(inp["state_delta"][:, c * 32:(c + 1) * 32])
        m["sconv"] = f32(inp["state_conv"][:, c * 32:(c + 1) * 32])
        in_maps.append(m)
    res = run_bass_kernel_spmd(prog.nc, in_maps, core_ids=list(range(ncores)))
    R = res.results
    if ncores < 4:
        return R
    y_prompt = np.stack([R[b]["yp"] for b in range(4)])
    y_sample = np.concatenate([R[c]["ys"].reshape(32, 8, D) for c in range(4)], 0)
    mem_k = np.stack([R[b]["memk"] for b in range(4)], 1).reshape(DEPTH, 4, NMEM, XH, 128)
    mem_v = np.stack([R[b]["memv"] for b in range(4)], 1).reshape(DEPTH, 4, NMEM, XH, 128)
    delta_p = np.stack([R[b]["delta_p"] for b in range(4)], 1)
    conv_p = np.stack([R[b]["conv_p"] for b in range(4)], 1)
    delta_s = np.concatenate([R[c]["delta_s"] for c in range(4)], 1)
    conv_s = np.concatenate([R[c]["conv_s"] for c in range(4)], 1)
    gv_s = np.concatenate([R[c]["gv_s"].reshape(2, 32, 8, 1536) for c in range(4)], 1)
    return tuple(np.ascontiguousarray(a, dtype=np.float32) for a in
                 (y_prompt, y_sample, mem_k, mem_v, delta_p, conv_p, delta_s, conv_s, gv_s))
```

```python
import contextlib
import numpy as np
import ml_dtypes
import concourse.bass as bass
import concourse.mybir as mybir
from concourse.bass_utils import run_bass_kernel_spmd

F32 = mybir.dt.float32
BF16 = mybir.dt.bfloat16
AF = mybir.ActivationFunctionType
ALU = mybir.AluOpType

D = 2048
KC = 16
FF = 5632
FC = 44
TP = 1024
TS = 128
T = TP + TS
NH = 12
XH = 4
NMEM = 256
DEPTH = 4
EPS = 1e-6
IN_A = 6680
IN_B = 3584


class Sched:
    ENGS = ("pe", "act", "dve", "pool", "sp")
    NDS = 6

    def __init__(self, nc, es):
        self.nc = nc
        self.ops = {e: [] for e in self.ENGS}
        self.res = {}
        self.waited = {e: {} for e in self.ENGS}
        self.needed = set()
        self.csem = {e: es.enter_context(nc.semaphore("c_" + e)) for e in ("pe", "act", "dve", "pool")}
        self.dsem = {q: [es.enter_context(nc.semaphore("d_%s%d" % (q, i))) for i in range(self.NDS)]
                     for q in ("sp", "pool", "act")}
        self.dcnt = {q: 0 for q in ("sp", "pool", "act")}
        self.dlast = {}

    def _dep(self, eng, ev, waits):
        if ev is None:
            return
        if ev[0] == "c":
            _, pe, idx = ev
            if pe == eng == "pe":
                return
            if pe == eng and idx == len(self.ops[eng]) - 0 - 1 and False:
                return
            key = ("c", pe)
            if self.waited[eng].get(key, -1) >= idx:
                return
            self.waited[eng][key] = idx
            self.needed.add((pe, idx))
            waits.append(ev)
        else:
            _, q, si, val = ev
            key = ("d", q, si)
            if self.waited[eng].get(key, -1) >= val:
                return
            self.waited[eng][key] = val
            waits.append(ev)

    def _track(self, eng, r, w, ev, waits):
        for k in r:
            st = self.res.get(k)
            if st is not None:
                self._dep(eng, st[0], waits)
                if isinstance(k, tuple) and k[0] == "ps":
                    for e2 in st[1]:
                        if e2[0] != "c" or e2[1] != eng:
                            self._dep(eng, e2, waits)
        for k in w:
            st = self.res.get(k)
            if st is not None:
                self._dep(eng, st[0], waits)
                for e2 in st[1]:
                    self._dep(eng, e2, waits)
        for k in r:
            st = self.res.setdefault(k, [None, []])
            st[1].append(ev)
            if len(st[1]) > 24:
                st[1] = st[1][-24:] if False else st[1]
        for k in w:
            self.res[k] = [ev, []]

    def barrier(self):
        bt = self.bar_tile
        self.add("dve", lambda h: h.memset(bt, 0.0), w=["PHASE"])

    def add(self, eng, fn, r=(), w=()):
        r = list(r) + ["PHASE"]
        idx = len(self.ops[eng])
        waits = []
        ev = ("c", eng, idx)
        self._track(eng, r, w, ev, waits)
        self.ops[eng].append(("c", fn, waits, None))

    def dma(self, q, out, in_, r=(), w=()):
        r = list(r) + ["PHASE"]
        n = self.dcnt[q]
        self.dcnt[q] = n + 1
        si = n % self.NDS
        val = 16 * (n // self.NDS + 1)
        waits = []
        if n >= self.NDS:
            self._dep(q, ("d", q, si, val - 16), waits)
        ev = ("d", q, si, val)
        self._track(q, r, w, ev, waits)
        self.dlast[(q, si)] = val
        self.ops[q].append(("d", (out, in_), waits, (q, si)))

    def emit(self, block):
        nc = self.nc
        cnt = {}
        for e in ("pe", "act", "dve", "pool"):
            c = 0
            for i in range(len(self.ops[e])):
                if (e, i) in self.needed:
                    c += 1
                    cnt[(e, i)] = c
        final_waits = [(self.dsem[q][si], v) for (q, si), v in self.dlast.items()]

        def run(eng_name, h):
            for i, (kind, fn, waits, dinfo) in enumerate(self.ops[eng_name]):
                for ev in waits:
                    if ev[0] == "c":
                        h.wait_ge(self.csem[ev[1]], cnt[(ev[1], ev[2])])
                    else:
                        h.wait_ge(self.dsem[ev[1]][ev[2]], ev[3])
                if kind == "c":
                    ins = fn(h)
                    if (eng_name, i) in self.needed:
                        ins.then_inc(self.csem[eng_name], 1)
                else:
                    out, in_ = fn
                    h.dma_start(out=out, in_=in_).then_inc(self.dsem[dinfo[0]][dinfo[1]], 16)
            if eng_name == "sp":
                for s, v in final_waits:
                    h.wait_ge(s, v)

        @block.tensor
        def _(h):
            run("pe", h)

        @block.scalar
        def _(h):
            run("act", h)

        @block.vector
        def _(h):
            run("dve", h)

        @block.gpsimd
        def _(h):
            run("pool", h)

        @block.sync
        def _(h):
            run("sp", h)


def make_consts():
    c = {}
    idx = np.arange(128)
    c["ident_f"] = np.eye(128, dtype=np.float32)
    c["ones_f"] = np.ones((128, 128), np.float32)
    for B in (64, 8):
        nb = 128 // B
        blk = idx // B
        same = blk[:, None] == blk[None, :]
        i = idx[:, None]
        j = idx[None, :]
        c["neg%d" % B] = np.where(same & (j <= i), 0.0, -30000.0).astype(np.float32)
        c["strict%d" % B] = (same & (j < i)).astype(np.float32)
        c["ublk%d" % B] = (same & (i <= j)).astype(np.float32)
        c["bblk%d" % B] = same.astype(np.float32)
        cm = np.zeros((128, nb, 128), np.float32)
        for s in range(nb):
            cm[:, s, s * B:(s + 1) * B] = 1.0
        c["colmask%d" % B] = cm.reshape(128, nb * 128)
        rm = np.zeros((128, nb), np.float32)
        rm[idx, blk] = 1.0
        c["rowmask%d" % B] = rm
    c["tril_t"] = (idx[:, None] <= idx[None, :]).astype(np.float32)
    m8 = np.zeros((128, 16, 8), np.float32)
    for p in range(128):
        s, jj = p // 8, p % 8
        m8[p, s, jj:] = 1.0
    c["mask8"] = m8.reshape(128, 128)
    rep = np.zeros((128, 128), np.float32)
    for p in range(128):
        rep[p % 8, p] = 1.0
    c["rep8"] = rep
    return c


SBS = [(0, 576), (576, 576)]
MSBS = [(0, 512), (512, 512), (1024, 128)]


def blocks_of(sb):
    off, n = sb
    return [(off, n // 2), (off + n // 2, n // 2)]


class Prog:
    def __init__(self, stage=9, nlayers=DEPTH, passes=("A", "B"), feat=("mix", "attn")):
        self.feat = set(feat)
        self.stage = stage
        self.nlayers = nlayers
        self.passes = passes
        nc = self.nc = bass.Bass("TRN2", target_bir_lowering=False)
        es = self.es = contextlib.ExitStack()
        self.S = Sched(nc, es)
        self.consts = make_consts()
        self.build()

    def din(self, name, shape, dt=F32):
        return self.nc.dram_tensor(name, list(shape), dt, kind="ExternalInput").ap()

    def dout(self, name, shape):
        return self.nc.dram_tensor(name, list(shape), F32, kind="ExternalOutput").ap()

    def sb(self, name, shape, dt):
        return self.es.enter_context(self.nc.sbuf_tensor(name, list(shape), dt))

    def build(self):
        nc, S = self.nc, self.S
        I = self.I = {}
        O = self.O = {}
        I["xp"] = self.din("xp", [2 * TP, D])
        I["xs"] = self.din("xs", [2 * TS, D])
        I["mem"] = self.din("mem", [NMEM, D])
        I["ck"] = self.din("ck", [DEPTH, 32, NMEM, 512])
        I["cv"] = self.din("cv", [DEPTH, 32, NMEM, 512])
        I["sdelta"] = self.din("sdelta", [2, 32, NH, 128, 128])
        I["sconv"] = self.din("sconv", [2, 32, 3, 4608])
        I["norm_gains"] = self.din("norm_gains", [DEPTH, 6, D])
        I["w_ffn_gu"] = self.din("w_ffn_gu", [DEPTH, 2, D, 2 * FF])
        I["w_ffn_dn"] = self.din("w_ffn_dn", [DEPTH, 2, FF, D])
        I["w_in_a"] = self.din("w_in_a", [2, D, IN_A])
        I["conv_w"] = self.din("conv_w", [2, 4, 4608])
        I["a_log"] = self.din("a_log", [2, NH])
        I["dt_bias"] = self.din("dt_bias", [2, NH])
        I["delta_norm_gain"] = self.din("delta_norm_gain", [2, 128])
        I["w_in_b"] = self.din("w_in_b", [2, D, IN_B])
        I["gmlp_ln_gain"] = self.din("gmlp_ln_gain", [2, 1536])
        I["gmlp_ln_bias"] = self.din("gmlp_ln_bias", [2, 1536])
        I["w_spatial"] = self.din("w_spatial", [2, NH, 128, 128])
        I["b_spatial"] = self.din("b_spatial", [2, NH, 128])
        I["mem_norm_gain"] = self.din("mem_norm_gain", [DEPTH, D])
        I["w_mem_kv"] = self.din("w_mem_kv", [DEPTH, D, 1024])
        I["w_out"] = self.din("w_out", [DEPTH, D, D])
        for k, v in self.consts.items():
            I["c_" + k] = self.din("c_" + k, v.shape)
        O["yp"] = self.dout("yp", [2 * TP, D])
        O["ys"] = self.dout("ys", [2 * TS, D])
        O["memk"] = self.dout("memk", [DEPTH, NMEM, 512])
        O["memv"] = self.dout("memv", [DEPTH, NMEM, 512])
        O["delta_p"] = self.dout("delta_p", [2, NH, 128, 128])
        O["conv_p"] = self.dout("conv_p", [2, 3, 4608])
        O["delta_s"] = self.dout("delta_s", [2, 32, NH, 128, 128])
        O["conv_s"] = self.dout("conv_s", [2, 32, 3, 4608])
        O["gv_s"] = self.dout("gv_s", [2, 2 * TS, 1536])

        self.XT = self.sb("XT", [128, KC, T], F32)
        self.R1 = self.sb("R1", [128, 19456], BF16)
        self.ARENA = self.sb("ARENA", [128, FC * 576], BF16)
        self.WD = self.sb("WD", [128, 2, FC * 128], BF16)
        self.GAIN = self.sb("GAIN", [128, DEPTH * 6 * KC], F32)
        self.MG = self.sb("MG", [128, DEPTH * KC], F32)
        self.IDF = self.sb("IDF", [128, 128], F32)
        self.IDB = self.sb("IDB", [128, 128], BF16)
        self.ONF = self.sb("ONF", [128, 128], F32)
        self.ONB = self.sb("ONB", [128, 128], BF16)
        self.STAT = self.sb("STAT", [128, 2, 576], F32)
        self.SQ = self.sb("SQ", [128, 2, 576], F32)
        self.EPSC = self.sb("EPSC", [128, 2], F32)
        self.PS = self.es.enter_context(nc.psum_tensor("PS", [128, 8, 512], F32))
        self.ONEC = self.sb("ONEC", [128, 2], F32)
        self.SM = self.sb("SM", [128, 8], F32)
        self.BAR = self.sb("BAR", [128, 2], F32)
        S.bar_tile = self.BAR[:, 0:1]
        m64 = []
        for nm, shp, dt in (("neg64", [128, 128], F32), ("strict64", [128, 128], F32), ("ublk64", [128, 128], F32),
                            ("bblk64", [128, 128], F32), ("colmask64", [128, 256], BF16), ("rowmask64", [128, 2], F32)):
            tl = self.sb("M_" + nm, shp, dt)
            S.dma("pool" if dt == BF16 else "sp", tl[:], I["c_" + nm], w=["M64"])
            m64.append(tl[:])
        self.M64 = m64
        S.add("dve", lambda h: h.memset(self.EPSC[:, 0:1], EPS), w=["EPSC"])
        S.add("dve", lambda h: h.memset(self.ONEC[:, 0:1], 1.0), w=["ONEC"])
        self.wa_i = 0
        self.ps_reserved = set()
        self.wd_i = 0
        self.ps_i = 0

        S.dma("sp", self.IDF[:], I["c_ident_f"], w=["IDF"])
        S.dma("sp", self.ONF[:], I["c_ones_f"], w=["ONF"])
        S.dma("pool", self.IDB[:], I["c_ident_f"], w=["IDB"])
        S.dma("pool", self.ONB[:], I["c_ones_f"], w=["ONB"])
        with nc.allow_non_contiguous_dma(reason="small gain vectors, feature-major"):
            pass
        self.load_featmajor(self.GAIN, I["norm_gains"].rearrange("l s (c p) -> (l s c) p", p=128), DEPTH * 6 * KC, "GAIN")
        self.load_featmajor(self.MG, I["mem_norm_gain"].rearrange("l (c p) -> (l c) p", p=128), DEPTH * KC, "MG")

        self.scrS = nc.dram_tensor("scrS", [2, 128, NH * 128], F32, kind="Internal").ap()
        self.scrCT = nc.dram_tensor("scrCT", [2, 128, 108], F32, kind="Internal").ap()
        for pi, pname in enumerate(self.passes):
            self.pname = pname
            self.first_pass = (pi == 0)
            self.last_pass = (pi == len(self.passes) - 1)
            self.tok0 = 0 if pname == "A" else TP
            self.sq0 = 0 if pname == "A" else 16
            self.has_sample = True
            self.Tp = T if self.has_sample else TP
            sbs = [(0, 576), (576, 576)] if self.has_sample else [(0, 512), (512, 512)]
            msbs = [(0, 512), (512, 512)] + ([(1024, 128)] if self.has_sample else [])
            if pi > 0:
                S.barrier()
            self.load_x()
            for layer in range(self.nlayers):
                for sbi, sb in enumerate(sbs):
                    self.ffn(layer, 0, sb)
                if self.stage >= 2:
                    for sbi, sb in enumerate(msbs):
                        self.mixer(layer, sbi, sb)
                for sbi, sb in enumerate(sbs):
                    self.ffn(layer, 1, sb)
            self.store_y()

        blk = self.es.enter_context(nc.Block())
        S.emit(blk)
        self.es.close()

    def psum(self):
        while True:
            b = self.ps_i % 8
            self.ps_i += 1
            if b not in self.ps_reserved:
                return b

    def load_featmajor(self, dst, src_rows, nrows, key):
        S = self.S
        done = 0
        while done < nrows:
            n = min(128, nrows - done)
            st = self.ARENA[:, 0:256].bitcast(F32)
            S.dma("sp", st[0:n, :], src_rows[done:done + n, :], w=[("AT", 0)])
            b = self.psum()
            S.add("pe", lambda h, b=b, n=n, st=st: h.transpose(self.PS[:, b, 0:n], st[0:n, :], self.IDF[0:n, 0:n]),
                  r=[("AT", 0), "IDF"], w=[("ps", b)])
            S.add("dve", lambda h, b=b, n=n, d0=done: h.tensor_copy(out=dst[:, d0:d0 + n], in_=self.PS[:, b, 0:n]),
                  r=[("ps", b)], w=[key])
            done += n

    def load_x(self):
        S, I = self.S, self.I
        for ti in range(self.Tp // 128):
            src = I["xp"][self.tok0 + ti * 128:self.tok0 + (ti + 1) * 128, :] if ti < 8 else I["xs"][self.sq0 * 8:self.sq0 * 8 + 128, :]
            st = self.WD[:, ti % 2, 0:4096].bitcast(F32)
            key = ("WD", ti % 2)
            S.dma("sp", st, src, w=[key])
            for c4 in range(4):
                b = self.psum()
                for c in range(4):
                    cc = c4 * 4 + c
                    S.add("pe", lambda h, b=b, c=c, cc=cc, st=st: h.transpose(
                        self.PS[:, b, c * 128:(c + 1) * 128], st[:, cc * 128:(cc + 1) * 128], self.IDF[:]),
                        r=[key, "IDF"], w=[("ps", b)])
                S.add("dve" if c4 % 2 == 0 else "act",
                      (lambda h, b=b, c4=c4, ti=ti: h.tensor_copy(
                          out=self.XT[:, c4 * 4:(c4 + 1) * 4, ti * 128:(ti + 1) * 128],
                          in_=self.PS[:, b, :].rearrange("p (c t) -> p c t", c=4))) if c4 % 2 == 0 else
                      (lambda h, b=b, c4=c4, ti=ti: h.activation(
                          out=self.XT[:, c4 * 4:(c4 + 1) * 4, ti * 128:(ti + 1) * 128],
                          in_=self.PS[:, b, :].rearrange("p (c t) -> p c t", c=4), func=AF.Copy)),
                      r=[("ps", b)], w=[("XT", ti)])

    def store_y(self):
        S, O = self.S, self.O
        for ti in range(self.Tp // 128):
            dst = O["yp"][self.tok0 + ti * 128:self.tok0 + (ti + 1) * 128, :] if ti < 8 else O["ys"][self.sq0 * 8:self.sq0 * 8 + 128, :]
            st = self.WD[:, ti % 2, 0:4096].bitcast(F32)
            key = ("WD", ti % 2)
            for c4 in range(4):
                b = self.psum()
                for c in range(4):
                    cc = c4 * 4 + c
                    S.add("pe", lambda h, b=b, c=c, cc=cc, ti=ti: h.transpose(
                        self.PS[:, b, c * 128:(c + 1) * 128], self.XT[:, cc, ti * 128:(ti + 1) * 128], self.IDF[:]),
                        r=[("XT", ti), "IDF"], w=[("ps", b)])
                S.add("dve", lambda h, b=b, c4=c4, st=st: h.tensor_copy(
                    out=st[:, c4 * 512:(c4 + 1) * 512], in_=self.PS[:, b, :]),
                    r=[("ps", b)], w=[key])
            S.dma("sp", dst, st, r=[key])

    def xt_keys(self, off, n):
        return [("XT", t) for t in range(off // 128, (off + n + 127) // 128)]

    def rstd_bcast(self, src_fn, nch, off, n, inv_d, slot, rkeys):
        S = self.S
        cb = [(0, n)] if n <= 512 else [(0, n // 2), (n // 2, n - n // 2)]
        banks = [self.psum() for _ in cb]
        for c in range(nch):
            sqt = self.SQ[:, c % 2, 0:n]
            S.add("pool" if c % 2 else "dve",
                  lambda h, c=c, sqt=sqt: h.tensor_tensor(out=sqt, in0=src_fn(c), in1=src_fn(c), op=ALU.mult),
                  r=rkeys, w=[("SQ", c % 2)])
            for (o, m), b in zip(cb, banks):
                S.add("pe", lambda h, c=c, b=b, sqt=sqt, o=o, m=m: h.matmul(
                    self.PS[:, b, 0:m], self.ONF[:], sqt[:, o:o + m], start=(c == 0), stop=(c == nch - 1)),
                    r=[("SQ", c % 2), "ONF"], w=[("ps", b)])
        st = self.STAT[:, slot, 0:n]
        for (o, m), b in zip(cb, banks):
            S.add("act", lambda h, b=b, o=o, m=m: h.activation(
                out=st[:, o:o + m], in_=self.PS[:, b, 0:m], func=AF.Ln, bias=self.EPSC[:, 0:1], scale=inv_d),
                r=[("ps", b), "EPSC"], w=[("STAT", slot)])
        S.add("act", lambda h: h.activation(out=st, in_=st, func=AF.Exp, scale=-0.5),
              r=[("STAT", slot)], w=[("STAT", slot)])
        return st

    def r1_keys(self, lo, hi):
        return [("R1", p) for p in range(lo // 2048, (hi + 2047) // 2048)]

    def ht_view(self, n):
        return self.R1[:, 0:KC * n].rearrange("p (c t) -> p c t", c=KC), [("HT", c) for c in range(KC)]

    def wa_load(self, src, base):
        nslots = (19456 - base) // 2048
        s = self.wa_i % nslots
        self.wa_i += 1
        lo = base + s * 2048
        view = self.R1[:, lo:lo + 2048].rearrange("p (c n) -> p c n", c=KC)
        keys = [("WA", lo)]
        self.S.dma("pool", view, src.rearrange("(c p) n -> p c n", p=128), r=["WAALL"], w=keys)
        return view, keys + ["WAALL"]

    def at_keys(self, lo, hi):
        return [("AT", j) for j in range(lo // 576, (hi + 575) // 576)]

    def prenorm(self, gidx, off, n, hview, hkeys):
        S = self.S
        xk = self.xt_keys(off, n)
        rstd = self.rstd_bcast(lambda c: self.XT[:, c, off:off + n], KC, off, n, 1.0 / D, 0, xk)
        for c in range(KC):
            S.add("dve", lambda h, c=c: h.scalar_tensor_tensor(
                out=hview[:, c, 0:n], in0=self.XT[:, c, off:off + n],
                scalar=self.GAIN[:, gidx * KC + c:gidx * KC + c + 1], in1=rstd, op0=ALU.mult, op1=ALU.mult),
                r=xk + [("STAT", 0), "GAIN"], w=[hkeys[c]])

    def postnorm_add(self, gidx, off, n, half):
        S = self.S
        outt = self.R1[:, 0:2 * KC * n].bitcast(F32).rearrange("p (c t) -> p c t", c=KC)
        okeys = [("HT", c) for c in range(KC)] + ["WAALL"]
        xk = self.xt_keys(off, n)
        rstd = self.rstd_bcast(lambda c: outt[:, c, 0:n], KC, off, n, 1.0 / D, 1, okeys)
        if half != 1.0:
            S.add("pool", lambda h: h.tensor_scalar(out=rstd, in0=rstd, scalar1=half, scalar2=None, op0=ALU.mult),
                  r=[("STAT", 1)], w=[("STAT", 1)])
        for c in range(KC):
            S.add("dve", lambda h, c=c: h.scalar_tensor_tensor(
                out=outt[:, c, 0:n], in0=outt[:, c, 0:n],
                scalar=self.GAIN[:, gidx * KC + c:gidx * KC + c + 1], in1=rstd, op0=ALU.mult, op1=ALU.mult),
                r=okeys + [("STAT", 1), "GAIN"], w=okeys)
            S.add("pool", lambda h, c=c: h.tensor_tensor(
                out=self.XT[:, c, off:off + n], in0=self.XT[:, c, off:off + n], in1=outt[:, c, 0:n], op=ALU.add),
                r=okeys + xk, w=xk)

    def ffn(self, layer, which, sb):
        S, I = self.S, self.I
        off, n = sb
        S.barrier()
        blks = blocks_of((0, n))
        g0 = layer * 6 + (0 if which == 0 else 4)
        hview, hkeys = self.ht_view(n)
        self.prenorm(g0, off, n, hview, hkeys)
        wgu = I["w_ffn_gu"][layer, which]
        wdn = I["w_ffn_dn"][layer, which]
        actT = self.ARENA[:, 0:FC * n].rearrange("p (f t) -> p f t", f=FC)
        sil = self.SQ
        ring = [(self.R1[:, 9216:13312], [("WA", 9216), ]), (self.WD[:, 0, 0:4096], [("WD", 0)]),
                (self.R1[:, 13312:17408], [("WA", 13312)]), (self.WD[:, 1, 0:4096], [("WD", 1)])]

        def wload(src, ri):
            buf, keys = ring[ri % 4]
            view = buf.rearrange("p (c n) -> p c n", c=KC)
            S.dma("pool", view, src.rearrange("(c p) n -> p c n", p=128), r=["WAALL"], w=keys)
            return view, keys + ["WAALL"]

        for j2 in range(FC // 2):
            gv, gk = wload(wgu[:, j2 * 256:(j2 + 1) * 256], 2 * j2)
            uv, uk = wload(wgu[:, FF + j2 * 256:FF + (j2 + 1) * 256], 2 * j2 + 1)
            for sub in range(2):
                j = 2 * j2 + sub
                for bi, (bo, bn) in enumerate(blks):
                    bg, bu = self.psum(), self.psum()
                    for c in range(KC):
                        self.mm(self.PS[:, bg, 0:bn], gv[:, c, sub * 128:(sub + 1) * 128], hview[:, c, bo:bo + bn],
                                c == 0, c == KC - 1, gk + [hkeys[c]], [("ps", bg)])
                    for c in range(KC):
                        self.mm(self.PS[:, bu, 0:bn], uv[:, c, sub * 128:(sub + 1) * 128], hview[:, c, bo:bo + bn],
                                c == 0, c == KC - 1, uk + [hkeys[c]], [("ps", bu)])
                    self.af(sil[:, bi, 0:bn], self.PS[:, bg, 0:bn], AF.Silu, [("ps", bg)], [("SQ", bi)])
                    self.tt("dve", actT[:, j, bo:bo + bn], sil[:, bi, 0:bn], self.PS[:, bu, 0:bn], ALU.mult,
                            [("ps", bu), ("SQ", bi)], [("AT", j)])
        outt = self.R1[:, 0:2 * KC * n].bitcast(F32).rearrange("p (c t) -> p c t", c=KC)
        okeys = [("HT", c) for c in range(KC)] + ["WAALL"]
        atk = [("AT", j) for j in range(FC)]
        HF = FC // 2
        for dcp in range(KC // 2):
            banks = [[self.psum() for _ in blks] for _ in range(2)]
            for half in range(2):
                s_ = self.wd_i % 2
                self.wd_i += 1
                wv = self.WD[:, s_, :].rearrange("p (f n) -> p f n", f=HF)
                S.dma("pool", wv, wdn[half * HF * 128:(half + 1) * HF * 128, dcp * 256:(dcp + 1) * 256].rearrange(
                    "(f p) n -> p f n", p=128), w=[("WD", s_)])
                for sub in range(2):
                    for bi, (bo, bn) in enumerate(blks):
                        b_ = banks[sub][bi]
                        for f in range(HF):
                            self.mm(self.PS[:, b_, 0:bn], wv[:, f, sub * 128:(sub + 1) * 128], actT[:, half * HF + f, bo:bo + bn],
                                    half == 0 and f == 0, half == 1 and f == HF - 1, [("WD", s_)] + atk, [("ps", b_)])
            for sub in range(2):
                for bi, (bo, bn) in enumerate(blks):
                    b_ = banks[sub][bi]
                    self.cp("act" if bi else "dve", outt[:, dcp * 2 + sub, bo:bo + bn], self.PS[:, b_, 0:bn], [("ps", b_)], okeys)
        self.postnorm_add(g0 + 1, off, n, 0.5)

    def ar(self, lo, n, dt=BF16):
        if dt == F32:
            return self.ARENA[:, lo:lo + 2 * n].bitcast(F32), self.at_keys(lo, lo + 2 * n)
        return self.ARENA[:, lo:lo + n], self.at_keys(lo, lo + n)

    def mem_kv(self, layer):
        S, I, O = self.S, self.I, self.O
        KT = self.WD[:, 0, 0:1024].rearrange("p (h n) -> p h n", h=4)
        V = self.WD[:, 0, 1024:2048].rearrange("p (c f) -> p c f", c=2)
        wdk = [("WD", 0)]
        st, stk = [], []
        for i in range(2):
            v, k = self.ar(10240 + i * 4096, 2048, F32)
            st.append(v)
            stk.append(k)
        mh, mhk = self.ar(18432, 4096)
        mh = mh.rearrange("p (c n) -> p c n", c=KC)
        rs = self.EPSC[:, 1:2]
        for i in range(2):
            S.dma("sp", st[i], I["mem"][i * 128:(i + 1) * 128, :], w=stk[i])
            junk = self.WD[:, 1, 0:4096].bitcast(F32)
            self.tt("pool", junk, st[i], st[i], ALU.mult, stk[i], [("WD", 1)])
            S.add("dve", lambda h, junk=junk: h.tensor_reduce(out=rs, in_=junk, axis=mybir.AxisListType.X, op=ALU.add),
                  r=[("WD", 1)], w=["EPSC2"])
            S.add("act", lambda h: h.activation(out=rs, in_=rs, func=AF.Ln, bias=self.EPSC[:, 0:1], scale=1.0 / D),
                  r=["EPSC2", "EPSC"], w=["EPSC2"])
            S.add("act", lambda h: h.activation(out=rs, in_=rs, func=AF.Exp, scale=-0.5), r=["EPSC2"], w=["EPSC2"])
            S.add("dve", lambda h, i=i: h.tensor_scalar(out=st[i], in0=st[i], scalar1=rs, scalar2=None, op0=ALU.mult),
                  r=stk[i] + ["EPSC2"], w=stk[i])
            for c4 in range(4):
                b = self.psum()
                for c in range(4):
                    cc = c4 * 4 + c
                    S.add("pe", lambda h, b=b, c=c, cc=cc, i=i: h.transpose(
                        self.PS[:, b, c * 128:(c + 1) * 128], st[i][:, cc * 128:(cc + 1) * 128], self.IDF[:]),
                        r=stk[i] + ["IDF"], w=[("ps", b)])
                for c in range(4):
                    cc = c4 * 4 + c
                    S.add("dve", lambda h, b=b, c=c, cc=cc, i=i: h.tensor_scalar(
                        out=mh[:, cc, i * 128:(i + 1) * 128], in0=self.PS[:, b, c * 128:(c + 1) * 128],
                        scalar1=self.MG[:, layer * KC + cc:layer * KC + cc + 1], scalar2=None, op0=ALU.mult),
                        r=[("ps", b), "MG"], w=mhk)
        kvf, kvfk = self.ar(10240, 2048, F32)
        kvf = kvf.rearrange("p (f n) -> p f n", f=8)
        for fc in range(8):
            wv, wk = self.wa_load(I["w_mem_kv"][layer][:, fc * 128:(fc + 1) * 128], 10240)
            b = self.psum()
            for c in range(KC):
                S.add("pe", lambda h, c=c, b=b, wv=wv: h.matmul(self.PS[:, b, 0:256], wv[:, c, :], mh[:, c, :],
                                                                 start=(c == 0), stop=(c == KC - 1)),
                      r=wk + mhk, w=[("ps", b)])
            S.add("dve", lambda h, b=b, fc=fc: h.tensor_copy(out=kvf[:, fc, :], in_=self.PS[:, b, 0:256]),
                  r=[("ps", b)], w=kvfk)
            if fc < 4:
                S.add("act", lambda h, b=b, fc=fc: h.activation(out=KT[:, fc, :], in_=self.PS[:, b, 0:256], func=AF.Copy),
                      r=[("ps", b)], w=wdk)
        tok, tokk = self.ar(14336, 1024, F32)
        for nc_ in range(2):
            for half in range(2):
                b = self.psum()
                for f in range(4):
                    S.add("pe", lambda h, b=b, f=f, half=half, nc_=nc_: h.transpose(
                        self.PS[:, b, f * 128:(f + 1) * 128], kvf[:, half * 4 + f, nc_ * 128:(nc_ + 1) * 128], self.IDF[:]),
                        r=kvfk + ["IDF"], w=[("ps", b)])
                S.add("dve", lambda h, b=b, half=half: h.tensor_copy(out=tok[:, half * 512:(half + 1) * 512], in_=self.PS[:, b, :]),
                      r=[("ps", b)], w=tokk)
                if half == 1:
                    S.add("act", lambda h, b=b, nc_=nc_: h.activation(out=V[:, nc_, :], in_=self.PS[:, b, :], func=AF.Copy),
                          r=[("ps", b)], w=wdk)
            if self.first_pass:
                S.dma("sp", O["memk"][layer, nc_ * 128:(nc_ + 1) * 128, :], tok[:, 0:512], r=tokk)
                S.dma("sp", O["memv"][layer, nc_ * 128:(nc_ + 1) * 128, :], tok[:, 512:1024], r=tokk)

    def mm(self, out, lhsT, rhs, start, stop, r, w):
        self.S.add("pe", lambda h: h.matmul(out, lhsT, rhs, start=start, stop=stop), r=r, w=w)

    def tr(self, out, in_, ident, r, w):
        self.S.add("pe", lambda h: h.transpose(out, in_, ident), r=r, w=w)

    def ts(self, eng, out, in0, s1, s2, op0, op1, r, w):
        if op1 is None:
            self.S.add(eng, lambda h: h.tensor_scalar(out=out, in0=in0, scalar1=s1, scalar2=None, op0=op0), r=r, w=w)
        else:
            self.S.add(eng, lambda h: h.tensor_scalar(out=out, in0=in0, scalar1=s1, scalar2=s2, op0=op0, op1=op1), r=r, w=w)

    def tt(self, eng, out, in0, in1, op, r, w):
        self.S.add(eng, lambda h: h.tensor_tensor(out=out, in0=in0, in1=in1, op=op), r=r, w=w)

    def stt(self, out, in0, scalar, in1, op0, op1, r, w):
        self.S.add("dve", lambda h: h.scalar_tensor_tensor(out=out, in0=in0, scalar=scalar, in1=in1, op0=op0, op1=op1), r=r, w=w)

    def af(self, out, in_, func, r, w, bias=None, scale=None):
        kw = {}
        if bias is not None:
            kw["bias"] = bias
        if scale is not None:
            kw["scale"] = scale
        self.S.add("act", lambda h: h.activation(out=out, in_=in_, func=func, **kw), r=r, w=w)

    def cp(self, eng, out, in_, r, w):
        if eng == "act":
            self.af(out, in_, AF.Copy, r, w)
        else:
            self.S.add(eng, lambda h: h.tensor_copy(out=out, in_=in_), r=r, w=w)

    def psb(self, b):
        return self.PS[:, b, :].bitcast(BF16)

    def sc_reset(self):
        self.sc_off = 8192
        self.sc2_off = 2048
        self.sc3_off = 18432
        self.sc4_off = 0

    def sc(self, name, n, dt=BF16, pool=0):
        ne = n * (2 if dt == F32 else 1)
        ne = (ne + 15) // 16 * 16
        if pool == 0:
            lo = self.sc_off
            self.sc_off += ne
            assert self.sc_off <= 25344, ("ARENA scratch overflow", name, self.sc_off)
            v = self.ARENA[:, lo:lo + ne]
        elif pool == 1:
            lo = self.sc2_off
            self.sc2_off += ne
            assert self.sc2_off <= 8192, ("R1 scratch overflow", name, self.sc2_off)
            v = self.R1[:, lo:lo + ne]
        elif pool == 2:
            lo = self.sc3_off
            self.sc3_off += ne
            assert self.sc3_off <= 19456, ("R1 tail overflow", name, self.sc3_off)
            v = self.R1[:, lo:lo + ne]
        else:
            lo = self.sc4_off
            self.sc4_off += ne
            assert self.sc4_off <= 5120, ("WD0 scratch overflow", name, self.sc4_off)
            v = self.WD[:, 0, lo:lo + ne]
        if dt == F32:
            v = v.bitcast(F32)
        return v[:, 0:n], [("sc", name)]

    def rows_to_featmajor(self, dst, dkeys, src_rows, nrows, stage, skeys):
        S = self.S
        S.dma("sp", stage[0:nrows, :], src_rows, w=skeys)
        b = self.psum()
        self.tr(self.PS[:, b, 0:nrows], stage[0:nrows, :], self.IDF[0:nrows, 0:nrows], skeys + ["IDF"], [("ps", b)])
        self.cp("dve", dst, self.PS[:, b, 0:nrows], [("ps", b)], dkeys)

    def mixer(self, layer, sbi, sb, commit=True):
        S, I, O = self.S, self.I, self.O
        off, n = sb
        nt = n // 128
        sample = (off >= TP)
        isA = (layer % 2 == 0)
        j = layer // 2
        S.barrier()
        self.sc_reset()
        if sbi == 0:
            self.mem_kv(layer)
            S.barrier()
            self.sc_reset()
        hview, hkeys = self.ht_view(n)
        self.prenorm(layer * 6 + 2, off, n, hview, hkeys)
        mixt = self.ARENA[:, 0:KC * n].rearrange("p (c t) -> p c t", c=KC)
        mk = lambda c: [("MIXT", c)]
        ctx = dict(layer=layer, j=j, off=off, n=n, nt=nt, sample=sample, hview=hview, hkeys=hkeys, mixt=mixt, mk=mk,
                   sbi=sbi, commit=commit)
        if "mix" in self.feat:
            if isA:
                self.delta_sb(ctx)
            else:
                self.gmlp_sb(ctx)
        else:
            for c in range(12):
                S.add("pool", lambda h, c=c: h.memset(mixt[:, c, :], 0.0), w=mk(c))
        if "attn" in self.feat:
            self.attn_sb(ctx)
        else:
            for c in range(12, 16):
                S.add("pool", lambda h, c=c: h.memset(mixt[:, c, :], 0.0), w=mk(c))
        self.outproj_sb(ctx)

    def outproj_sb(self, ctx):
        S, I = self.S, self.I
        layer, off, n, mixt, mk = ctx["layer"], ctx["off"], ctx["n"], ctx["mixt"], ctx["mk"]
        S.barrier()
        outt = self.R1[:, 0:2 * KC * n].bitcast(F32).rearrange("p (c t) -> p c t", c=KC)
        okeys = [("HT", c) for c in range(KC)] + ["WAALL"]
        blks = blocks_of((0, n)) if n > 256 else [(0, n)]
        wo = I["w_out"][layer]
        allmk = [("MIXT", c) for c in range(KC)]
        for dc in range(KC):
            s = dc % 2
            lo = 1536 + s * 2048
            wv = self.WD[:, 1, lo:lo + 2048].rearrange("p (c n) -> p c n", c=KC)
            wk = [("WO", s)]
            S.dma("pool", wv, wo[:, dc * 128:(dc + 1) * 128].rearrange("(c p) n -> p c n", p=128), w=wk)
            for bi, (bo, bn) in enumerate(blks):
                b = self.psum()
                for c in range(KC):
                    self.mm(self.PS[:, b, 0:bn], wv[:, c, :], mixt[:, c, bo:bo + bn], c == 0, c == KC - 1,
                            wk + allmk, [("ps", b)])
                self.cp("act" if bi else "dve", outt[:, dc, bo:bo + bn], self.PS[:, b, 0:bn], [("ps", b)], okeys)
        self.postnorm_add(layer * 6 + 3, off, n, 1.0)

    def attn_sb(self, ctx):
        S, I = self.S, self.I
        layer, off, n, nt, sample = ctx["layer"], ctx["off"], ctx["n"], ctx["nt"], ctx["sample"]
        hview, hkeys, mixt, mk = ctx["hview"], ctx["hkeys"], ctx["mixt"], ctx["mk"]
        S.barrier()
        self.sc_reset()
        isA = (layer % 2 == 0)
        w_in = I["w_in_a"][layer // 2] if isA else I["w_in_b"][layer // 2]
        qcol0 = (IN_A - 512) if isA else (IN_B - 512)
        qT, qk = self.sc("qT", 4 * n)
        qT = qT.rearrange("p (h t) -> p h t", h=4)
        blks = blocks_of((0, n)) if n > 256 else [(0, n)]
        for hx in range(4):
            wv, wk = self.wa_load(w_in[:, qcol0 + hx * 128:qcol0 + (hx + 1) * 128], 8192)
            for bi, (bo, bn) in enumerate(blks):
                b = self.psum()
                for c in range(KC):
                    self.mm(self.PS[:, b, 0:bn], wv[:, c, :], hview[:, c, bo:bo + bn], c == 0, c == KC - 1,
                            wk + hkeys, [("ps", b)])
                self.cp("act", qT[:, hx, bo:bo + bn], self.PS[:, b, 0:bn], [("ps", b)], qk)
        P, Pk = self.sc("P", 512, F32)
        Pn, Pnk = self.sc("Pn", 512)
        PT, PTk = self.sc("PT", 512)
        sm, smk = self.sc("sm", 8, F32)
        P3 = P.rearrange("p (h n) -> p h n", h=2)
        Pn3 = Pn.rearrange("p (h n) -> p h n", h=2)
        scale = 128.0 ** -0.5

        def attend(col0, m, KT, V, kvk):
            for hp in range(2):
                b = self.psum()
                for hh in range(2):
                    self.mm(self.PS[0:m, b, hh * 256:(hh + 1) * 256], qT[:, 2 * hp + hh, col0:col0 + m], KT[:, 2 * hp + hh, :],
                            True, True, qk + kvk, [("ps", b)])
                ps3 = self.PS[0:m, b, :].rearrange("p (h n) -> p h n", h=2)
                S.add("dve", lambda h, ps3=ps3, m=m: h.tensor_reduce(out=sm[0:m, 0:2], in_=ps3, axis=mybir.AxisListType.X, op=ALU.max),
                      r=[("ps", b)], w=smk)
                self.tt("dve", P3[0:m], ps3, sm[0:m, 0:2].unsqueeze(2).broadcast_to([m, 2, 256]), ALU.subtract,
                        [("ps", b)] + smk, Pk)
                self.af(P[0:m], P[0:m], AF.Exp, Pk, Pk, scale=scale)
                S.add("dve", lambda h, m=m: h.tensor_reduce(out=sm[0:m, 2:4], in_=P3[0:m], axis=mybir.AxisListType.X, op=ALU.add),
                      r=Pk, w=smk)
                S.add("dve", lambda h, m=m: h.reciprocal(out=sm[0:m, 4:6], in_=sm[0:m, 2:4]), r=smk, w=smk)
                self.tt("dve", Pn3[0:m], P3[0:m], sm[0:m, 4:6].unsqueeze(2).broadcast_to([m, 2, 256]), ALU.mult,
                        Pk + smk, Pnk)
                b2 = self.psum()
                pb = self.psb(b2)
                for hh in range(2):
                    for ncnk in range(2):
                        q = hh * 2 + ncnk
                        self.tr(pb[:, q * m:(q + 1) * m], Pn3[0:m, hh, ncnk * 128:(ncnk + 1) * 128], self.IDB[0:m, 0:m],
                                Pnk + ["IDB"], [("ps", b2)])
                self.cp("act", PT[:, 0:4 * m], pb[:, 0:4 * m], [("ps", b2)], PTk)
                b3 = self.psum()
                for hh in range(2):
                    for ncnk in range(2):
                        q = hh * 2 + ncnk
                        hd = 2 * hp + hh
                        self.mm(self.PS[:, b3, hh * m:(hh + 1) * m], V[:, ncnk, hd * 128:(hd + 1) * 128], PT[:, q * m:(q + 1) * m],
                                ncnk == 0, ncnk == 1, PTk + kvk, [("ps", b3)])
                self.cp("dve", mixt[:, 12 + 2 * hp:12 + 2 * hp + 2, col0:col0 + m],
                        self.PS[:, b3, 0:2 * m].rearrange("p (h t) -> p h t", h=2), [("ps", b3)],
                        [("MIXT", 12 + 2 * hp), ("MIXT", 13 + 2 * hp)])

        if not sample:
            KT = self.WD[:, 0, 0:1024].rearrange("p (h n) -> p h n", h=4)
            V = self.WD[:, 0, 1024:2048].rearrange("p (c f) -> p c f", c=2)
            for t in range(nt):
                attend(t * 128, 128, KT, V, [("WD", 0)])
        else:
            bufs = []
            for i in range(2):
                kt_, ktk = self.sc("Ktok%d" % i, 1024)
                v_, vk = self.sc("Vs%d" % i, 1024)
                kT_, kTk = self.sc("KTs%d" % i, 1024)
                bufs.append((kt_, ktk, v_, vk, kT_, kTk))
            for s in range(16):
                kt_, ktk, v_, vk, kT_, kTk = bufs[s % 2]
                kt3 = kt_.rearrange("p (c f) -> p c f", c=2)
                v3 = v_.rearrange("p (c f) -> p c f", c=2)
                kT3 = kT_.rearrange("p (h n) -> p h n", h=4)
                S.dma("pool", kt3, I["ck"][layer, self.sq0 + s].rearrange("(c p) f -> p c f", p=128), w=ktk)
                S.dma("pool", v3, I["cv"][layer, self.sq0 + s].rearrange("(c p) f -> p c f", p=128), w=vk)
                b = self.psum()
                pb = self.psb(b)
                for hd in range(4):
                    for ncnk in range(2):
                        self.tr(pb[:, hd * 256 + ncnk * 128:hd * 256 + (ncnk + 1) * 128], kt3[:, ncnk, hd * 128:(hd + 1) * 128],
                                self.IDB[:], ktk + ["IDB"], [("ps", b)])
                self.cp("act", kT_, pb[:, 0:1024], [("ps", b)], kTk)
                attend(s * 8, 8, kT3, v3, kTk + vk)

    def gmlp_sb(self, ctx):
        S, I, O = self.S, self.I, self.O
        j, off, n, nt, sample = ctx["j"], ctx["off"], ctx["n"], ctx["nt"], ctx["sample"]
        hview, hkeys, mixt, mk = ctx["hview"], ctx["hkeys"], ctx["mixt"], ctx["mk"]
        w_in = I["w_in_b"][j]
        blks = blocks_of((0, n)) if n > 256 else [(0, n)]
        G = 12
        VG, VGk = self.sc("VG", G * n, F32)
        VG = VG.rearrange("p (g t) -> p g t", g=G)
        LG, LGk = self.sc("LG", 16, F32)
        LB, LBk = self.sc("LB", 16, F32)
        stg, stgk = self.sc("wsf", 128, F32)
        self.rows_to_featmajor(LG[:, 0:G], LGk, I["gmlp_ln_gain"][j].rearrange("(g p) -> g p", p=128), G, stg, stgk)
        self.rows_to_featmajor(LB[:, 0:G], LBk, I["gmlp_ln_bias"][j].rearrange("(g p) -> g p", p=128), G, stg, stgk)
        BROW, BRk = self.sc("BROW", 1536)
        S.dma("pool", BROW[0:1, :], I["b_spatial"][j:j + 1].rearrange("a g i -> a (g i)"), w=BRk)
        sq, sqk = self.sc("tmpA", n, F32)
        bsum = [self.psum() for _ in blks]
        bsq = [self.psum() for _ in blks]
        self.ps_reserved = set(bsum + bsq)
        for g in range(G):
            wv, wk = self.wa_load(w_in[:, 1536 + g * 128:1536 + (g + 1) * 128], 8192)
            for bi, (bo, bn) in enumerate(blks):
                b = self.psum()
                for c in range(KC):
                    self.mm(self.PS[:, b, 0:bn], wv[:, c, :], hview[:, c, bo:bo + bn], c == 0, c == KC - 1, wk + hkeys, [("ps", b)])
                self.af(VG[:, g, bo:bo + bn], self.PS[:, b, 0:bn], AF.Gelu, [("ps", b)], VGk)
            self.tt("pool", sq, VG[:, g, :], VG[:, g, :], ALU.mult, VGk, sqk)
            for bi, (bo, bn) in enumerate(blks):
                self.mm(self.PS[:, bsum[bi], 0:bn], self.ONF[:], VG[:, g, bo:bo + bn], g == 0, g == G - 1, VGk + ["ONF"], [("ps", bsum[bi])])
                self.mm(self.PS[:, bsq[bi], 0:bn], self.ONF[:], sq[:, bo:bo + bn], g == 0, g == G - 1, sqk + ["ONF"], [("ps", bsq[bi])])
        mean = self.STAT[:, 0, 0:n]
        rstd = self.STAT[:, 1, 0:n]
        for bi, (bo, bn) in enumerate(blks):
            self.ts("dve", mean[:, bo:bo + bn], self.PS[:, bsum[bi], 0:bn], 1.0 / 1536, None, ALU.mult, None, [("ps", bsum[bi])], [("STAT", 0)])
            self.tt("dve", sq[:, bo:bo + bn], mean[:, bo:bo + bn], mean[:, bo:bo + bn], ALU.mult, [("STAT", 0)], sqk)
            self.stt(rstd[:, bo:bo + bn], self.PS[:, bsq[bi], 0:bn], 1.0 / 1536, sq[:, bo:bo + bn], ALU.mult, ALU.subtract,
                     [("ps", bsq[bi])] + sqk, [("STAT", 1)])
        self.af(rstd, rstd, AF.Ln, [("STAT", 1), "EPSC"], [("STAT", 1)], bias=self.EPSC[:, 0:1], scale=1.0)
        self.af(rstd, rstd, AF.Exp, [("STAT", 1)], [("STAT", 1)], scale=-0.5)
        self.ps_reserved = set()
        vn, vnk = self.sc("vn", n, F32)
        vnb, vnbk = self.sc("vnb", n, BF16, 2)
        ug, ugk = sq, sqk
        wsf, wsfk = stg, stgk
        wsb, wsbk = self.sc("wsb", 128, BF16, 2)
        vtok, vtokk = self.sc("vtok", 128, BF16, 2)
        if sample:
            m8, m8k = self.sc("m8", 128, F32)
            rep, repk = self.sc("rep", 128, F32)
            S.dma("sp", m8, I["c_mask8"], w=m8k)
            S.dma("sp", rep, I["c_rep8"], w=repk)
            wst, wstk = self.sc("wst", 128, F32)
            wrep, wrepk = self.sc("wrep", 8, F32)
            brow8, br8k = self.sc("brow8", 128)
            gvst, gvstk = self.sc("gvst", 1536, F32)
        else:
            trl, trlk = self.sc("trl", 128, F32, 2)
            S.dma("sp", trl, I["c_tril_t"], w=trlk)
        for g in range(G):
            self.tt("dve", vn, VG[:, g, :], mean, ALU.subtract, VGk + [("STAT", 0)], vnk)
            self.tt("pool", vn, vn, rstd, ALU.mult, vnk + [("STAT", 1)], vnk)
            self.ts("dve", vn, vn, LG[:, g:g + 1], LB[:, g:g + 1], ALU.mult, ALU.add, vnk + LGk + LBk, vnk)
            self.cp("act", vnb, vn, vnk, vnbk)
            S.dma("sp", wsf, I["w_spatial"][j, g], w=wsfk)
            b = self.psum()
            self.tr(self.PS[:, b, 0:128], wsf, self.IDF[:], wsfk + ["IDF"], [("ps", b)])
            wv, wk = self.wa_load(w_in[:, g * 128:(g + 1) * 128], 8192)
            for bi, (bo, bn) in enumerate(blks):
                bu = self.psum()
                for c in range(KC):
                    self.mm(self.PS[:, bu, 0:bn], wv[:, c, :], hview[:, c, bo:bo + bn], c == 0, c == KC - 1, wk + hkeys, [("ps", bu)])
                self.af(ug[:, bo:bo + bn], self.PS[:, bu, 0:bn], AF.Gelu, [("ps", bu)], ugk)
            if not sample:
                self.tt("dve", wsb, self.PS[:, b, 0:128], trl, ALU.mult, [("ps", b)] + trlk, wsbk)
                for t in range(nt):
                    b2 = self.psum()
                    pb = self.psb(b2)
                    self.tr(pb[:, 0:128], vnb[:, t * 128:(t + 1) * 128], self.IDB[:], vnbk + ["IDB"], [("ps", b2)])
                    self.cp("act", vtok, pb[:, 0:128], [("ps", b2)], vtokk)
                    b3 = self.psum()
                    self.mm(self.PS[:, b3, 0:128], vtok, wsb, True, False, vtokk + wsbk, [("ps", b3)])
                    self.mm(self.PS[:, b3, 0:128], self.ONB[0:1, :], BROW[0:1, g * 128:(g + 1) * 128], False, True,
                            BRk + ["ONB"], [("ps", b3)])
                    self.tt("dve", mixt[:, g, t * 128:(t + 1) * 128], ug[:, t * 128:(t + 1) * 128], self.PS[:, b3, 0:128], ALU.mult,
                            ugk + [("ps", b3)], mk(g))
            else:
                self.cp("dve", wst, self.PS[:, b, 0:128], [("ps", b)], wstk)
                b4 = self.psum()
                self.mm(self.PS[:, b4, 0:8], rep, wst[:, 0:8], True, True, repk + wstk, [("ps", b4)])
                self.cp("dve", wrep, self.PS[:, b4, 0:8], [("ps", b4)], wrepk)
                self.tt("dve", wsb.rearrange("p (s i) -> p s i", s=16), wrep.unsqueeze(1).broadcast_to([128, 16, 8]),
                        m8.rearrange("p (s i) -> p s i", s=16), ALU.mult, wrepk + m8k, wsbk)
                self.cp("dve", brow8[0:1, :].rearrange("p (s i) -> p s i", s=16),
                        BROW[0:1, g * 128:g * 128 + 8].unsqueeze(1).broadcast_to([1, 16, 8]), BRk, br8k)
                b2 = self.psum()
                pb = self.psb(b2)
                self.tr(pb[:, 0:128], vnb[:, 0:128], self.IDB[:], vnbk + ["IDB"], [("ps", b2)])
                self.cp("act", vtok, pb[:, 0:128], [("ps", b2)], vtokk)
                b3 = self.psum()
                self.mm(self.PS[:, b3, 0:128], vtok, wsb, True, False, vtokk + wsbk, [("ps", b3)])
                self.mm(self.PS[:, b3, 0:128], self.ONB[0:1, :], brow8[0:1, :], False, True, br8k + ["ONB"], [("ps", b3)])
                self.tt("dve", mixt[:, g, 0:128], ug[:, 0:128], self.PS[:, b3, 0:128], ALU.mult, ugk + [("ps", b3)], mk(g))
                b5 = self.psum()
                self.tr(self.PS[:, b5, 0:128], vn[:, 0:128], self.IDF[:], vnk + ["IDF"], [("ps", b5)])
                self.cp("act", gvst[:, g * 128:(g + 1) * 128], self.PS[:, b5, 0:128], [("ps", b5)], gvstk)
        if sample:
            S.dma("sp", O["gv_s"][j, self.sq0 * 8:self.sq0 * 8 + 128, :], gvst, r=gvstk)

    def delta_sb(self, ctx):
        S, I, O = self.S, self.I, self.O
        j, off, n, nt, sample, sbi = ctx["j"], ctx["off"], ctx["n"], ctx["nt"], ctx["sample"], ctx["sbi"]
        hview, hkeys, mixt, mk, commit = ctx["hview"], ctx["hkeys"], ctx["mixt"], ctx["mk"], ctx["commit"]
        w_in = I["w_in_a"][j]
        B = 8 if sample else 64
        nb = 128 // B
        nlev = 2 if sample else 5
        nseq, L = (16, 8) if sample else (1, n)
        blks = blocks_of((0, n)) if n > 256 else [(0, n)]
        Sf = self.WD[:, 0, 2048:5120].bitcast(F32).rearrange("p (h v) -> p h v", h=NH)
        Sb = self.WD[:, 1, 0:1536].rearrange("p (h v) -> p h v", h=NH)
        CT = self.WD[:, 0, 5120:5336].bitcast(F32).rearrange("p (c r) -> p c r", c=36)
        Sfk = lambda h: [("Sf", h)]
        Sbk = lambda h: [("Sb", h)]
        CTk = ["CT"]
        allSf = [("Sf", h) for h in range(NH)]
        allSb = [("Sb", h) for h in range(NH)]
        if sbi == 0 and self.first_pass:
            S.add("pool", lambda h: h.memset(self.WD[:, 0, 2048:5120].bitcast(F32), 0.0), w=allSf)
            S.add("pool", lambda h: h.memset(self.WD[:, 1, 0:1536], 0.0), w=allSb)
            S.add("pool", lambda h: h.memset(self.WD[:, 0, 5120:5336].bitcast(F32), 0.0), w=CTk)
        elif sbi == 0:
            S.dma("sp", self.WD[:, 0, 2048:5120].bitcast(F32), self.scrS[j], r=[("scrS", j)], w=allSf)
            S.dma("pool", self.WD[:, 1, 0:1536], self.scrS[j], r=[("scrS", j)], w=allSb)
            S.dma("sp", self.WD[:, 0, 5120:5336].bitcast(F32), self.scrCT[j], r=[("scrCT", j)], w=CTk)
        if sample:
            NEG, NEGk = self.sc("NEG8", 128, F32)
            STR, STRk = self.sc("STR8", 128, F32)
            UBL, UBLk = self.sc("UBL8", 128, F32)
            BBL, BBLk = self.sc("BBL8", 128, F32)
            COLM, COLMk = self.sc("COLM8", nb * 128)
            ROWM, ROWMk = self.sc("ROWM8", nb, F32)
            for v_, k_, nm in ((NEG, NEGk, "neg8"), (STR, STRk, "strict8"), (UBL, UBLk, "ublk8"), (BBL, BBLk, "bblk8"),
                               (ROWM, ROWMk, "rowmask8")):
                S.dma("sp", v_, I["c_" + nm], w=k_)
            S.dma("pool", COLM, I["c_colmask8"], w=COLMk)
        else:
            NEG, STR, UBL, BBL, COLM, ROWM = self.M64
            NEGk = STRk = UBLk = BBLk = COLMk = ROWMk = ["M64"]
        COLM3 = COLM.rearrange("p (s t) -> p s t", s=nb)
        stg, stgk = self.sc("stg", 128, F32)
        CW = []
        for r_ in range(4):
            cw, cwk = self.sc("CW%d" % r_, 36, F32)
            self.rows_to_featmajor(cw, cwk, I["conv_w"][j, r_].rearrange("(c p) -> c p", p=128), 36, stg, stgk)
            CW.append((cw, cwk))
        DNG, DNGk = self.sc("DNG", 1, F32)
        self.rows_to_featmajor(DNG, DNGk, I["delta_norm_gain"][j:j + 1, :], 1, stg, stgk)
        DTB, DTBk = self.sc("DTB", NH, F32)
        NEGA, NEGAk = self.sc("NEGA", NH, F32)
        S.dma("sp", DTB, I["dt_bias"][j:j + 1, :].broadcast_to([128, NH]), w=DTBk)
        S.dma("sp", NEGA, I["a_log"][j:j + 1, :].broadcast_to([128, NH]), w=NEGAk)
        self.af(NEGA, NEGA, AF.Exp, NEGAk, NEGAk)
        self.ts("dve", NEGA, NEGA, -1.0, None, ALU.mult, None, NEGAk, NEGAk)
        Wab, Wabk = self.sc("Wab", KC * 24)
        Wab = Wab.rearrange("p (c n) -> p c n", c=KC)
        S.dma("pool", Wab, w_in[:, 6144:6168].rearrange("(c p) n -> p c n", p=128), w=Wabk)

        def scal(name):
            v_, k_ = self.sc(name, nt * NH, F32)
            return v_.rearrange("p (t h) -> p t h", t=nt), k_
        BETA, BETAk = scal("BETA")
        G, Gk = scal("G")
        NG, NGk = scal("NG")
        GC, GCk = scal("GC")
        EDEC, EDECk = scal("EDEC")
        EGC, EGCk = scal("EGC")
        BE, BEk = scal("BE")
        GLB, GLBk = self.sc("GLB", nt * nb * NH, F32)
        GLB = GLB.rearrange("p (t s h) -> p t s h", t=nt, s=nb)
        G2, G2k = self.sc("G2", nb * NH, F32)
        for t in range(nt):
            b = self.psum()
            for c in range(KC):
                self.mm(self.PS[:, b, 0:24], hview[:, c, t * 128:(t + 1) * 128], Wab[:, c, :], c == 0, c == KC - 1,
                        hkeys + Wabk, [("ps", b)])
            self.af(BETA[:, t, :], self.PS[:, b, 12:24], AF.Sigmoid, [("ps", b)], BETAk)
            self.tt("dve", G[:, t, :], self.PS[:, b, 0:12], DTB, ALU.add, [("ps", b)] + DTBk, Gk)
            self.af(G[:, t, :], G[:, t, :], AF.Exp, Gk, Gk)
            self.af(G[:, t, :], G[:, t, :], AF.Ln, Gk + ["ONEC"], Gk, bias=self.ONEC[:, 0:1], scale=1.0)
            self.tt("dve", G[:, t, :], G[:, t, :], NEGA, ALU.mult, Gk + NEGAk, Gk)
            self.ts("dve", NG[:, t, :], G[:, t, :], -1.0, None, ALU.mult, None, Gk, NGk)
            b1 = self.psum()
            self.mm(self.PS[:, b1, 0:12], UBL, G[:, t, :], True, True, UBLk + Gk, [("ps", b1)])
            self.mm(self.PS[:, b1, 16:28], BBL, G[:, t, :], True, True, BBLk + Gk, [("ps", b1)])
            self.cp("dve", GC[:, t, :], self.PS[:, b1, 0:12], [("ps", b1)], GCk)
            self.af(EGC[:, t, :], self.PS[:, b1, 0:12], AF.Exp, [("ps", b1)], EGCk)
            self.tt("dve", EDEC[:, t, :], self.PS[:, b1, 16:28], GC[:, t, :], ALU.subtract, [("ps", b1)] + GCk, EDECk)
            self.af(EDEC[:, t, :], EDEC[:, t, :], AF.Exp, EDECk, EDECk)
            self.tt("dve", BE[:, t, :], BETA[:, t, :], EGC[:, t, :], ALU.mult, BETAk + EGCk, BEk)
            self.tt("dve", G2.rearrange("p (s h) -> p s h", s=nb), G[:, t, :].unsqueeze(1).broadcast_to([128, nb, NH]),
                    ROWM.unsqueeze(2).broadcast_to([128, nb, NH]), ALU.mult, Gk + ROWMk, G2k)
            b2 = self.psum()
            self.mm(self.PS[:, b2, 0:nb * NH], self.ONF[:], G2, True, True, G2k + ["ONF"], [("ps", b2)])
            self.af(GLB[:, t].rearrange("p s h -> p (s h)"), self.PS[:, b2, 0:nb * NH], AF.Exp, [("ps", b2)], GLBk)
        if sample:
            SCT = self.WD[:, 1, 1536:1536 + 2 * 36 * 48].bitcast(F32).rearrange("p (c q) -> p c q", c=36)
            SCTk = ["SCT"]
            cst, cstk = self.sc("cst", 512, F32)
            src = I["sconv"][j, self.sq0:self.sq0 + 16].rearrange("s r f -> (s r) f")
            for c9 in range(9):
                S.dma("sp", cst[0:48, :], src[:, c9 * 512:(c9 + 1) * 512], w=cstk)
                b = self.psum()
                for c in range(4):
                    self.tr(self.PS[:, b, c * 48:(c + 1) * 48], cst[0:48, c * 128:(c + 1) * 128], self.IDF[0:48, 0:48],
                            cstk + ["IDF"], [("ps", b)])
                self.cp("dve", SCT[:, c9 * 4:(c9 + 1) * 4, :], self.PS[:, b, 0:192].rearrange("p (c q) -> p c q", c=4),
                        [("ps", b)], SCTk)
        W3 = L + 3
        PQ = []
        for part in range(3):
            v_, k_ = self.sc("PQ%d" % part, nseq * W3 + 8, F32)
            PQ.append((v_[:, 0:nseq * W3].rearrange("p (s w) -> p s w", s=nseq), k_))
        YC, YCk = self.sc("YC", n, F32)
        YC3 = YC.rearrange("p (s l) -> p s l", s=nseq)
        QT, QTk = self.sc("QT", n)
        KTb, KTbk = self.sc("KTb", n)
        VTb, VTbk = self.sc("VTf", n, F32)
        KTf, KTfk = self.sc("KTf", n, F32)
        SZ, SZk = self.sc("SZ", n, F32)
        if sample:
            S0f, S0fk = self.sc("S0f", 16 * 128, F32, 1)
            S0bufs = [self.sc("S0b", 16 * 128, BF16, 3), self.sc("S0b1", 16 * 128, BF16, 0)]
            S0f3 = S0f.rearrange("p (s v) -> p s v", s=16)

            def load_s0b(hh):
                v_, k_ = S0bufs[hh % 2]
                S.dma("pool", v_.rearrange("p (s v) -> p s v", s=16),
                      I["sdelta"][j, self.sq0:self.sq0 + 16, hh].rearrange("s k v -> k s v"), w=k_)
        nbuf = 1
        TB = []
        for i in range(nbuf):
            d = {}
            for nm, sz, dt in (("KBE", 128, F32), ("KDEC", 128, BF16), ("VB", 128, F32), ("E", 128, F32), ("L1", 128, F32),
                               ("Lb0", 128, F32), ("Rb0", 128, F32), ("Lb1", 128, F32), ("Rb1", 128, F32),
                               ("Ab", 128, BF16), ("ATb", 128, BF16), ("Xf", 128, F32), ("U", 128, F32),
                               ("WT", 128, BF16), ("QD", 128, BF16), ("GUN", 128, F32),
                               ("WM", nb * 128, BF16), ("QM", nb * 128, BF16), ("KDM", nb * 128, BF16),
                               ("VN", 128, BF16), ("O", 128, F32), ("ON", 128, F32)):
                pl = 0
                if sample and nm == "KDM":
                    pl = 3
                if sample and nm == "QM":
                    pl = 1
                d[nm] = self.sc("%s_%d" % (nm, i), sz, dt, pl)
            if not sample:
                d["ATM"] = self.sc("ATM_%d" % i, nb * 128, BF16)
            TB.append(d)

        if sample:
            load_s0b(0)
        for h in range(NH):
            if sample:
                if h + 1 < NH:
                    load_s0b(h + 1)
                S0b, S0bk = S0bufs[h % 2]
                S0b3 = S0b.rearrange("p (s v) -> p s v", s=16)
            for part in range(3):
                cidx = part * NH + h
                pq, pqk = PQ[part]
                wv, wk = self.wa_load(w_in[:, part * 1536 + h * 128:part * 1536 + (h + 1) * 128], 8192)
                if sample:
                    self.cp("pool", pq[:, :, 0:3], SCT[:, cidx, :].rearrange("p (s r) -> p s r", s=16), SCTk, pqk)
                else:
                    self.cp("pool", pq[:, 0, 0:3], CT[:, cidx, :], CTk, pqk)
                for bi, (bo, bn) in enumerate(blks):
                    b = self.psum()
                    for c in range(KC):
                        self.mm(self.PS[:, b, 0:bn], wv[:, c, :], hview[:, c, bo:bo + bn], c == 0, c == KC - 1, wk + hkeys, [("ps", b)])
                    if sample:
                        self.cp("act", pq[:, :, 3:3 + L], self.PS[:, b, 0:128].rearrange("p (s l) -> p s l", s=16), [("ps", b)], pqk)
                    else:
                        self.cp("act", pq[:, 0, 3 + bo:3 + bo + bn], self.PS[:, b, 0:bn], [("ps", b)], pqk)
                cw = lambda r_: CW[r_][0][:, cidx:cidx + 1]
                cwk_all = CW[0][1] + CW[1][1] + CW[2][1] + CW[3][1]
                self.ts("dve", YC3, pq[:, :, 0:L], cw(0), None, ALU.mult, None, pqk + cwk_all, YCk)
                for r_ in range(1, 4):
                    self.stt(YC3, pq[:, :, r_:r_ + L], cw(r_), YC3, ALU.mult, ALU.add, pqk + cwk_all + YCk, YCk)
                if sample:
                    self.cp("pool", SCT[:, cidx, :].rearrange("p (s r) -> p s r", s=16), pq[:, :, L:L + 3], pqk, SCTk)
                else:
                    self.cp("pool", CT[:, cidx, :], pq[:, 0, L:L + 3], pqk, CTk)
                self.af(YC, YC, AF.Silu, YCk, YCk)
                if part < 2:
                    sq = self.SQ[:, 0, 0:n]
                    self.tt("pool", sq, YC, YC, ALU.mult, YCk, [("SQ", 0)])
                    rn = self.STAT[:, 0, 0:n]
                    for bi, (bo, bn) in enumerate(blks):
                        b = self.psum()
                        self.mm(self.PS[:, b, 0:bn], self.ONF[:], sq[:, bo:bo + bn], True, True, [("SQ", 0), "ONF"], [("ps", b)])
                        self.af(rn[:, bo:bo + bn], self.PS[:, b, 0:bn], AF.Ln, [("ps", b), "EPSC"], [("STAT", 0)],
                                bias=self.EPSC[:, 0:1], scale=1.0)
                    self.af(rn, rn, AF.Exp, [("STAT", 0)], [("STAT", 0)], scale=-0.5)
                    if part == 0:
                        self.stt(QT, YC, 128.0 ** -0.5, rn, ALU.mult, ALU.mult, YCk + [("STAT", 0)], QTk)
                    else:
                        self.tt("dve", KTf, YC, rn, ALU.mult, YCk + [("STAT", 0)], KTfk)
                        self.cp("pool", KTb, KTf, KTfk, KTbk)
                else:
                    self.cp("dve", VTb, YC, YCk, VTbk)
            wv, wk = self.wa_load(w_in[:, 4608 + h * 128:4608 + (h + 1) * 128], 8192)
            for bi, (bo, bn) in enumerate(blks):
                b = self.psum()
                for c in range(KC):
                    self.mm(self.PS[:, b, 0:bn], wv[:, c, :], hview[:, c, bo:bo + bn], c == 0, c == KC - 1, wk + hkeys, [("ps", b)])
                self.af(SZ[:, bo:bo + bn], self.PS[:, b, 0:bn], AF.Silu, [("ps", b)], SZk)
            if sample:
                S.dma("sp", S0f3, I["sdelta"][j, self.sq0:self.sq0 + 16, h].rearrange("s k v -> k s v"), w=S0fk)
            for t in range(nt):
                d = TB[t % nbuf]
                tc = slice(t * 128, (t + 1) * 128)
                col = lambda arr: arr[:, t, h:h + 1]
                KBE, KBEk = d["KBE"]; KDEC, KDECk = d["KDEC"]; VB, VBk = d["VB"]
                E, Ek = d["E"]; L1, L1k = d["L1"]; Ab, Abk = d["Ab"]; ATb, ATbk = d["ATb"]
                Xf, Xfk = d["Xf"]; Xb, Xbk = Xf, Xfk; U, Uk = d["U"]; WT, WTk = d["WT"]; QD, QDk = d["QD"]
                GUN, GUNk = d["GUN"]; DG, DGk = GUN, GUNk; WM, WMk = d["WM"]; QM, QMk = d["QM"]; KDM, KDMk = d["KDM"]
                VN, VNk = d["VN"]; Ot, Otk = d["O"]; ON, ONk = d["ON"]
                WM3 = WM.rearrange("p (s t) -> p s t", s=nb)
                QM3 = QM.rearrange("p (s t) -> p s t", s=nb)
                KDM3 = KDM.rearrange("p (s t) -> p s t", s=nb)
                b = self.psum()
                pb = self.PS[:, b, :]
                self.tr(pb[:, 0:128], KTf[:, tc], self.IDF[:], KTfk + ["IDF"], [("ps", b)])
                self.tr(pb[:, 128:256], VTb[:, tc], self.IDF[:], VTbk + ["IDF"], [("ps", b)])
                self.ts("dve", KBE, pb[:, 0:128], col(BE), None, ALU.mult, None, [("ps", b)] + BEk, KBEk)
                self.ts("dve", KDEC, pb[:, 0:128], col(EDEC), None, ALU.mult, None, [("ps", b)] + EDECk, KDECk)
                self.af(VB, pb[:, 128:256], AF.Copy, [("ps", b)] + BETAk, VBk, scale=col(BETA))
                self.ts("pool", GUN, UBL, col(NG), None, ALU.mult, None, UBLk + NGk, GUNk)
                b = self.psum()
                self.mm(self.PS[:, b, 0:128], self.ONF[:], GUN, True, False, GUNk + ["ONF"], [("ps", b)])
                self.mm(self.PS[:, b, 0:128], self.IDF[:], NEG, False, True, NEGk + ["IDF"], [("ps", b)])
                self.af(E, self.PS[:, b, 0:128], AF.Exp, [("ps", b)] + GCk, Ek, bias=col(GC), scale=1.0)
                bk = self.psum()
                self.mm(self.PS[:, bk, 0:128], KTb[:, tc], KTb[:, tc], True, True, KTbk, [("ps", bk)])
                self.mm(self.PS[:, bk, 128:256], QT[:, tc], KTb[:, tc], True, True, QTk + KTbk, [("ps", bk)])
                self.stt(L1, self.PS[:, bk, 0:128], col(BETA), E, ALU.mult, ALU.mult, [("ps", bk)] + BETAk + Ek, L1k)
                Lb, Lbk = d["Lb0"]
                Rb, Rbk = d["Rb0"]
                self.tt("pool", Lb, L1, STR, ALU.mult, L1k + STRk, Lbk)
                self.tt("dve", Ab, self.PS[:, bk, 128:256], E, ALU.mult, [("ps", bk)] + Ek, Abk)
                b = self.psum()
                self.tr(self.PS[:, b, 0:128], Lb, self.IDF[:], Lbk + ["IDF"], [("ps", b)])
                self.cp("act", Rb, self.PS[:, b, 0:128], [("ps", b)], Rbk)
                self.tt("dve", Xf, self.IDF[:], Rb, ALU.subtract, Rbk + ["IDF"], Xfk)
                b = self.psum()
                pb = self.psb(b)
                self.tr(pb[:, 0:128], Ab, self.IDB[:], Abk + ["IDB"], [("ps", b)])
                self.cp("act", ATb, pb[:, 0:128], [("ps", b)], ATbk)
                for lev in range(nlev):
                    L2, L2k = d["Lb%d" % ((lev + 1) % 2)]
                    R2, R2k = d["Rb%d" % ((lev + 1) % 2)]
                    b = self.psum()
                    self.mm(self.PS[:, b, 0:128], Rb, Lb, True, True, Rbk + Lbk, [("ps", b)])
                    if lev < nlev - 1:
                        self.mm(self.PS[:, b, 128:256], Lb, Rb, True, True, Rbk + Lbk, [("ps", b)])
                    self.cp("act", L2, self.PS[:, b, 0:128], [("ps", b)], L2k)
                    if lev < nlev - 1:
                        self.cp("dve", R2, self.PS[:, b, 128:256], [("ps", b)], R2k)
                    b2 = self.psum()
                    self.mm(self.PS[:, b2, 0:128], L2, Xb, True, True, L2k + Xbk, [("ps", b2)])
                    self.tt("dve", Xf, Xf, self.PS[:, b2, 0:128], ALU.add, Xfk + [("ps", b2)], Xfk)
                    Lb, Lbk, Rb, Rbk = L2, L2k, R2, R2k
                b = self.psum()
                self.mm(self.PS[:, b, 0:128], Xb, VB, True, True, Xbk + VBk, [("ps", b)])
                self.mm(self.PS[:, b, 128:256], KBE, Xb, True, True, Xbk + KBEk, [("ps", b)])
                self.cp("act", U, self.PS[:, b, 0:128], [("ps", b)], Uk)
                self.cp("dve", WT, self.PS[:, b, 128:256], [("ps", b)], WTk)
                self.ts("pool", DG, self.IDF[:], col(EGC), None, ALU.mult, None, EGCk + ["IDF"], DGk)
                b = self.psum()
                self.mm(self.PS[:, b, 0:128], self.ONF[:], DG, True, True, DGk + ["ONF"], [("ps", b)])
                self.tt("dve", QD, QT[:, tc], self.PS[:, b, 0:128], ALU.mult, QTk + [("ps", b)], QDk)
                self.tt("pool", WM3, WT.unsqueeze(1).broadcast_to([128, nb, 128]), COLM3, ALU.mult, WTk + COLMk, WMk)
                self.tt("pool", QM3, QD.unsqueeze(1).broadcast_to([128, nb, 128]), COLM3, ALU.mult, QDk + COLMk, QMk)
                self.tt("pool", KDM3, KDEC.unsqueeze(1).broadcast_to([128, nb, 128]),
                        ROWM.unsqueeze(2).broadcast_to([128, nb, 128]), ALU.mult, KDECk + ROWMk, KDMk)
                if not sample:
                    ATM, ATMk = d["ATM"]
                    ATM3 = ATM.rearrange("p (s t) -> p s t", s=nb)
                    self.tt("pool", ATM3, ATb.unsqueeze(1).broadcast_to([128, nb, 128]), COLM3, ALU.mult, ATbk + COLMk, ATMk)
                    for s in range(nb):
                        b = self.psum()
                        self.mm(self.PS[:, b, 0:128], WM3[:, s, :], Sb[:, h, :], True, True, WMk + Sbk(h), [("ps", b)])
                        self.tt("dve", VN, U, self.PS[:, b, 0:128], ALU.subtract, Uk + [("ps", b)], VNk)
                        bo_ = self.psum()
                        self.mm(self.PS[:, bo_, 0:128], QM3[:, s, :], Sb[:, h, :], True, False, QMk + Sbk(h), [("ps", bo_)])
                        self.mm(self.PS[:, bo_, 0:128], ATM3[:, s, :], VN, False, True, ATMk + VNk, [("ps", bo_)])
                        if s == 0:
                            self.cp("act", Ot, self.PS[:, bo_, 0:128], [("ps", bo_)], Otk)
                        else:
                            self.tt("dve", Ot, Ot, self.PS[:, bo_, 0:128], ALU.add, Otk + [("ps", bo_)], Otk)
                        bs = self.psum()
                        self.mm(self.PS[:, bs, 0:128], KDM3[:, s, :], VN, True, True, KDMk + VNk, [("ps", bs)])
                        self.stt(Sf[:, h, :], Sf[:, h, :], GLB[:, t, s, h:h + 1], self.PS[:, bs, 0:128], ALU.mult, ALU.add,
                                 Sfk(h) + GLBk + [("ps", bs)], Sfk(h))
                        self.cp("act", Sb[:, h, :], Sf[:, h, :], Sfk(h), Sbk(h))
                else:
                    b = self.psum()
                    for s in range(nb):
                        self.mm(self.PS[:, b, 0:128], WM3[:, s, :], S0b3[:, s, :], s == 0, s == nb - 1, WMk + S0bk, [("ps", b)])
                    self.tt("dve", VN, U, self.PS[:, b, 0:128], ALU.subtract, Uk + [("ps", b)], VNk)
                    bo_ = self.psum()
                    for s in range(nb):
                        self.mm(self.PS[:, bo_, 0:128], QM3[:, s, :], S0b3[:, s, :], s == 0, False, QMk + S0bk, [("ps", bo_)])
                    self.mm(self.PS[:, bo_, 0:128], ATb, VN, False, True, ATbk + VNk, [("ps", bo_)])
                    self.cp("act", Ot, self.PS[:, bo_, 0:128], [("ps", bo_)], Otk)
                    for s in range(nb):
                        bs = self.psum()
                        self.mm(self.PS[:, bs, 0:128], KDM3[:, s, :], VN, True, True, KDMk + VNk, [("ps", bs)])
                        self.stt(S0f3[:, s, :], S0f3[:, s, :], GLB[:, t, s, h:h + 1], self.PS[:, bs, 0:128], ALU.mult, ALU.add,
                                 S0fk + GLBk + [("ps", bs)], S0fk)
                    if commit:
                        S.dma("sp", O["delta_s"][j, self.sq0:self.sq0 + 16, h].rearrange("s k v -> k s v"), S0f3, r=S0fk)
                rs = self.SM[:, 0:1]
                self.tt("pool", L1, Ot, Ot, ALU.mult, Otk, L1k)
                S.add("dve", lambda hh, L1=L1: hh.tensor_reduce(out=self.SM[:, 0:1], in_=L1, axis=mybir.AxisListType.X, op=ALU.add),
                      r=L1k, w=["SM"])
                self.af(rs, rs, AF.Ln, ["SM", "EPSC"], ["SM"], bias=self.EPSC[:, 0:1], scale=1.0 / 128)
                self.af(rs, rs, AF.Exp, ["SM"], ["SM"], scale=-0.5)
                self.ts("dve", ON, Ot, rs, None, ALU.mult, None, Otk + ["SM"], ONk)
                b = self.psum()
                self.tr(self.PS[:, b, 0:128], ON, self.IDF[:], ONk + ["IDF"], [("ps", b)])
                self.stt(mixt[:, h, tc], self.PS[:, b, 0:128], DNG[:, 0:1], SZ[:, tc], ALU.mult, ALU.mult, [("ps", b)] + DNGk + SZk, mk(h))
        if commit and sample:
            cst2, cst2k = cst, cstk
            dst = O["conv_s"][j, self.sq0:self.sq0 + 16].rearrange("s r f -> (s r) f")
            for c9 in range(9):
                b = self.psum()
                for c in range(4):
                    self.tr(self.PS[0:48, b, c * 128:(c + 1) * 128], SCT[:, c9 * 4 + c, :], self.IDF[:], SCTk + ["IDF"], [("ps", b)])
                self.cp("dve", cst2[0:48, :], self.PS[0:48, b, :], [("ps", b)], cst2k)
                S.dma("sp", dst[:, c9 * 512:(c9 + 1) * 512], cst2[0:48, :], r=cst2k)
        if (not sample) and sbi == 1 and not self.last_pass:
            S.dma("sp", self.scrS[j], self.WD[:, 0, 2048:5120].bitcast(F32), r=allSf, w=[("scrS", j)])
            S.dma("sp", self.scrCT[j], self.WD[:, 0, 5120:5336].bitcast(F32), r=CTk, w=[("scrCT", j)])
        if commit and (not sample) and sbi == 1 and self.last_pass:
            S.dma("sp", O["delta_p"][j].rearrange("h k v -> k h v"), Sf, r=allSf)
            cst2, cst2k = YC, YCk
            for c9 in range(9):
                b = self.psum()
                for c in range(4):
                    self.tr(self.PS[0:3, b, c * 128:(c + 1) * 128], CT[:, c9 * 4 + c, :], self.IDF[:], CTk + ["IDF"], [("ps", b)])
                self.cp("dve", cst2[0:3, :], self.PS[0:3, b, :], [("ps", b)], cst2k)
                S.dma("sp", O["conv_p"][j][:, c9 * 512:(c9 + 1) * 512], cst2[0:3, :], r=cst2k)


_PROG = None


def get_prog():
    global _PROG
    if _PROG is None:
        _PROG = Prog()
    return _PROG


def kernel(**inp):
    prog = get_prog()
    return run_prog(prog, inp, 4)


def run_prog(prog, inp, ncores):
    consts = prog.consts
    f32 = lambda a: np.ascontiguousarray(a, dtype=np.float32)
    in_maps = []
    shared = {k: f32(inp[k]) for k in ("norm_gains", "w_ffn_gu", "w_ffn_dn", "w_in_a", "conv_w", "a_log", "dt_bias",
                                        "delta_norm_gain", "w_in_b", "gmlp_ln_gain", "gmlp_ln_bias", "w_spatial",
                                        "b_spatial", "mem_norm_gain", "w_mem_kv", "w_out")}
    for k, v in consts.items():
        shared["c_" + k] = f32(v)
    for c in range(ncores):
        b = c % 4
        m = dict(shared)
        m["xp"] = f32(inp["x_prompt"][b])
        m["xs"] = f32(inp["x_sample"][c * 32:(c + 1) * 32].reshape(2 * TS, D))
        m["mem"] = f32(inp["mem_prompt"][b])
        m["ck"] = f32(inp["cache_mem_k"][:, c * 32:(c + 1) * 32].reshape(DEPTH, 32, NMEM, 512))
        m["cv"] = f32(inp["cache_mem_v"][:, c * 32:(c + 1) * 32].reshape(DEPTH, 32, NMEM, 512))
        m["sdelta"] = f32(inp["state_delta"][:, c * 32:(c + 1) * 32])
        m["sconv"] = f32(inp["state_conv"][:, c * 32:(c + 1) * 32])
        in_maps.append(m)
    res = run_bass_kernel_spmd(prog.nc, in_maps, core_ids=list(range(ncores)))
    R = res.results
    if ncores < 4:
        return R
    y_prompt = np.stack([R[b]["yp"] for b in range(4)])
    y_sample = np.concatenate([R[c]["ys"].reshape(32, 8, D) for c in range(4)], 0)
    mem_k = np.stack([R[b]["memk"] for b in range(4)], 1).reshape(DEPTH, 4, NMEM, XH, 128)
    mem_v = np.stack([R[b]["memv"] for b in range(4)], 1).reshape(DEPTH, 4, NMEM, XH, 128)
    delta_p = np.stack([R[b]["delta_p"] for b in range(4)], 1)
    conv_p = np.stack([R[b]["conv_p"] for b in range(4)], 1)
    delta_s = np.concatenate([R[c]["delta_s"] for c in range(4)], 1)
    conv_s = np.concatenate([R[c]["conv_s"] for c in range(4)], 1)
    gv_s = np.concatenate([R[c]["gv_s"].reshape(2, 32, 8, 1536) for c in range(4)], 1)
    return tuple(np.ascontiguousarray(a, dtype=np.float32) for a in
                 (y_prompt, y_sample, mem_k, mem_v, delta_p, conv_p, delta_s, conv_s, gv_s))
```

```python
import contextlib
import numpy as np
import ml_dtypes
import concourse.bass as bass
import concourse.mybir as mybir
from concourse.bass_utils import run_bass_kernel_spmd

F32 = mybir.dt.float32
BF16 = mybir.dt.bfloat16
AF = mybir.ActivationFunctionType
ALU = mybir.AluOpType

D = 2048
KC = 16
FF = 5632
FC = 44
TP = 1024
TS = 128
T = TP + TS
NH = 12
XH = 4
NMEM = 256
DEPTH = 4
EPS = 1e-6
IN_A = 6680
IN_B = 3584


class Sched:
    ENGS = ("pe", "act", "dve", "pool", "sp")
    NDS = 6

    def __init__(self, nc, es):
        self.nc = nc
        self.ops = {e: [] for e in self.ENGS}
        self.res = {}
        self.waited = {e: {} for e in self.ENGS}
        self.needed = set()
        self.csem = {e: es.enter_context(nc.semaphore("c_" + e)) for e in ("pe", "act", "dve", "pool")}
        self.dsem = {q: [es.enter_context(nc.semaphore("d_%s%d" % (q, i))) for i in range(self.NDS)]
                     for q in ("sp", "pool", "act")}
        self.dcnt = {q: 0 for q in ("sp", "pool", "act")}
        self.dlast = {}

    def _dep(self, eng, ev, waits):
        if ev is None:
            return
        if ev[0] == "c":
            _, pe, idx = ev
            if pe == eng == "pe":
                return
            if pe == eng and idx == len(self.ops[eng]) - 0 - 1 and False:
                return
            key = ("c", pe)
            if self.waited[eng].get(key, -1) >= idx:
                return
            self.waited[eng][key] = idx
            self.needed.add((pe, idx))
            waits.append(ev)
        else:
            _, q, si, val = ev
            key = ("d", q, si)
            if self.waited[eng].get(key, -1) >= val:
                return
            self.waited[eng][key] = val
            waits.append(ev)

    def _track(self, eng, r, w, ev, waits):
        for k in r:
            st = self.res.get(k)
            if st is not None:
                self._dep(eng, st[0], waits)
                if isinstance(k, tuple) and k[0] == "ps":
                    for e2 in st[1]:
                        if e2[0] != "c" or e2[1] != eng:
                            self._dep(eng, e2, waits)
        for k in w:
            st = self.res.get(k)
            if st is not None:
                self._dep(eng, st[0], waits)
                for e2 in st[1]:
                    self._dep(eng, e2, waits)
        for k in r:
            st = self.res.setdefault(k, [None, []])
            st[1].append(ev)
            if len(st[1]) > 24:
                st[1] = st[1][-24:] if False else st[1]
        for k in w:
            self.res[k] = [ev, []]

    def barrier(self):
        bt = self.bar_tile
        self.add("dve", lambda h: h.memset(bt, 0.0), w=["PHASE"])

    def add(self, eng, fn, r=(), w=()):
        r = list(r) + ["PHASE"]
        idx = len(self.ops[eng])
        waits = []
        ev = ("c", eng, idx)
        self._track(eng, r, w, ev, waits)
        self.ops[eng].append(("c", fn, waits, None))

    def dma(self, q, out, in_, r=(), w=()):
        r = list(r) + ["PHASE"]
        n = self.dcnt[q]
        self.dcnt[q] = n + 1
        si = n % self.NDS
        val = 16 * (n // self.NDS + 1)
        waits = []
        if n >= self.NDS:
            self._dep(q, ("d", q, si, val - 16), waits)
        ev = ("d", q, si, val)
        self._track(q, r, w, ev, waits)
        self.dlast[(q, si)] = val
        self.ops[q].append(("d", (out, in_), waits, (q, si)))

    def emit(self, block):
        nc = self.nc
        cnt = {}
        for e in ("pe", "act", "dve", "pool"):
            c = 0
            for i in range(len(self.ops[e])):
                if (e, i) in self.needed:
                    c += 1
                    cnt[(e, i)] = c
        final_waits = [(self.dsem[q][si], v) for (q, si), v in self.dlast.items()]

        def run(eng_name, h):
            for i, (kind, fn, waits, dinfo) in enumerate(self.ops[eng_name]):
                for ev in waits:
                    if ev[0] == "c":
                        h.wait_ge(self.csem[ev[1]], cnt[(ev[1], ev[2])])
                    else:
                        h.wait_ge(self.dsem[ev[1]][ev[2]], ev[3])
                if kind == "c":
                    ins = fn(h)
                    if (eng_name, i) in self.needed:
                        ins.then_inc(self.csem[eng_name], 1)
                else:
                    out, in_ = fn
                    h.dma_start(out=out, in_=in_).then_inc(self.dsem[dinfo[0]][dinfo[1]], 16)
            if eng_name == "sp":
                for s, v in final_waits:
                    h.wait_ge(s, v)

        @block.tensor
        def _(h):
            run("pe", h)

        @block.scalar
        def _(h):
            run("act", h)

        @block.vector
        def _(h):
            run("dve", h)

        @block.gpsimd
        def _(h):
            run("pool", h)

        @block.sync
        def _(h):
            run("sp", h)


def make_consts():
    c = {}
    idx = np.arange(128)
    c["ident_f"] = np.eye(128, dtype=np.float32)
    c["ones_f"] = np.ones((128, 128), np.float32)
    for B in (64, 8):
        nb = 128 // B
        blk = idx // B
        same = blk[:, None] == blk[None, :]
        i = idx[:, None]
        j = idx[None, :]
        c["neg%d" % B] = np.where(same & (j <= i), 0.0, -30000.0).astype(np.float32)
        c["strict%d" % B] = (same & (j < i)).astype(np.float32)
        c["ublk%d" % B] = (same & (i <= j)).astype(np.float32)
        c["bblk%d" % B] = same.astype(np.float32)
        cm = np.zeros((128, nb, 128), np.float32)
        for s in range(nb):
            cm[:, s, s * B:(s + 1) * B] = 1.0
        c["colmask%d" % B] = cm.reshape(128, nb * 128)
        rm = np.zeros((128, nb), np.float32)
        rm[idx, blk] = 1.0
        c["rowmask%d" % B] = rm
    c["tril_t"] = (idx[:, None] <= idx[None, :]).astype(np.float32)
    m8 = np.zeros((128, 16, 8), np.float32)
    for p in range(128):
        s, jj = p // 8, p % 8
        m8[p, s, jj:] = 1.0
    c["mask8"] = m8.reshape(128, 128)
    rep = np.zeros((128, 128), np.float32)
    for p in range(128):
        rep[p % 8, p] = 1.0
    c["rep8"] = rep
    return c


SBS = [(0, 576), (576, 576)]
MSBS = [(0, 512), (512, 512), (1024, 128)]


def blocks_of(sb):
    off, n = sb
    return [(off, n // 2), (off + n // 2, n // 2)]


class Prog:
    def __init__(self, stage=9, nlayers=DEPTH, passes=("A", "B"), feat=("mix", "attn")):
        self.feat = set(feat)
        self.stage = stage
        self.nlayers = nlayers
        self.passes = passes
        nc = self.nc = bass.Bass("TRN2", target_bir_lowering=False)
        es = self.es = contextlib.ExitStack()
        self.S = Sched(nc, es)
        self.consts = make_consts()
        self.build()

    def din(self, name, shape, dt=F32):
        return self.nc.dram_tensor(name, list(shape), dt, kind="ExternalInput").ap()

    def dout(self, name, shape):
        return self.nc.dram_tensor(name, list(shape), F32, kind="ExternalOutput").ap()

    def sb(self, name, shape, dt):
        return self.es.enter_context(self.nc.sbuf_tensor(name, list(shape), dt))

    def build(self):
        nc, S = self.nc, self.S
        I = self.I = {}
        O = self.O = {}
        I["xp"] = self.din("xp", [2 * TP, D])
        I["xs"] = self.din("xs", [2 * TS, D])
        I["mem"] = self.din("mem", [NMEM, D])
        I["ck"] = self.din("ck", [DEPTH, 32, NMEM, 512])
        I["cv"] = self.din("cv", [DEPTH, 32, NMEM, 512])
        I["sdelta"] = self.din("sdelta", [2, 32, NH, 128, 128])
        I["sconv"] = self.din("sconv", [2, 32, 3, 4608])
        I["norm_gains"] = self.din("norm_gains", [DEPTH, 6, D])
        I["w_ffn_gu"] = self.din("w_ffn_gu", [DEPTH, 2, D, 2 * FF])
        I["w_ffn_dn"] = self.din("w_ffn_dn", [DEPTH, 2, FF, D])
        I["w_in_a"] = self.din("w_in_a", [2, D, IN_A])
        I["conv_w"] = self.din("conv_w", [2, 4, 4608])
        I["a_log"] = self.din("a_log", [2, NH])
        I["dt_bias"] = self.din("dt_bias", [2, NH])
        I["delta_norm_gain"] = self.din("delta_norm_gain", [2, 128])
        I["w_in_b"] = self.din("w_in_b", [2, D, IN_B])
        I["gmlp_ln_gain"] = self.din("gmlp_ln_gain", [2, 1536])
        I["gmlp_ln_bias"] = self.din("gmlp_ln_bias", [2, 1536])
        I["w_spatial"] = self.din("w_spatial", [2, NH, 128, 128])
        I["b_spatial"] = self.din("b_spatial", [2, NH, 128])
        I["mem_norm_gain"] = self.din("mem_norm_gain", [DEPTH, D])
        I["w_mem_kv"] = self.din("w_mem_kv", [DEPTH, D, 1024])
        I["w_out"] = self.din("w_out", [DEPTH, D, D])
        for k, v in self.consts.items():
            I["c_" + k] = self.din("c_" + k, v.shape)
        O["yp"] = self.dout("yp", [2 * TP, D])
        O["ys"] = self.dout("ys", [2 * TS, D])
        O["memk"] = self.dout("memk", [DEPTH, NMEM, 512])
        O["memv"] = self.dout("memv", [DEPTH, NMEM, 512])
        O["delta_p"] = self.dout("delta_p", [2, NH, 128, 128])
        O["conv_p"] = self.dout("conv_p", [2, 3, 4608])
        O["delta_s"] = self.dout("delta_s", [2, 32, NH, 128, 128])
        O["conv_s"] = self.dout("conv_s", [2, 32, 3, 4608])
        O["gv_s"] = self.dout("gv_s", [2, 2 * TS, 1536])

        self.XT = self.sb("XT", [128, KC, T], F32)
        self.R1 = self.sb("R1", [128, 19456], BF16)
        self.ARENA = self.sb("ARENA", [128, FC * 576], BF16)
        self.WD = self.sb("WD", [128, 2, FC * 128], BF16)
        self.GAIN = self.sb("GAIN", [128, DEPTH * 6 * KC], F32)
        self.MG = self.sb("MG", [128, DEPTH * KC], F32)
        self.IDF = self.sb("IDF", [128, 128], F32)
        self.IDB = self.sb("IDB", [128, 128], BF16)
        self.ONF = self.sb("ONF", [128, 128], F32)
        self.ONB = self.sb("ONB", [128, 128], BF16)
        self.STAT = self.sb("STAT", [128, 2, 576], F32)
        self.SQ = self.sb("SQ", [128, 2, 576], F32)
        self.EPSC = self.sb("EPSC", [128, 2], F32)
        self.PS = self.es.enter_context(nc.psum_tensor("PS", [128, 8, 512], F32))
        self.ONEC = self.sb("ONEC", [128, 2], F32)
        self.SM = self.sb("SM", [128, 8], F32)
        self.BAR = self.sb("BAR", [128, 2], F32)
        S.bar_tile = self.BAR[:, 0:1]
        m64 = []
        for nm, shp, dt in (("neg64", [128, 128], F32), ("strict64", [128, 128], F32), ("ublk64", [128, 128], F32),
                            ("bblk64", [128, 128], F32), ("colmask64", [128, 256], BF16), ("rowmask64", [128, 2], F32)):
            tl = self.sb("M_" + nm, shp, dt)
            S.dma("pool" if dt == BF16 else "sp", tl[:], I["c_" + nm], w=["M64"])
            m64.append(tl[:])
        self.M64 = m64
        S.add("dve", lambda h: h.memset(self.EPSC[:, 0:1], EPS), w=["EPSC"])
        S.add("dve", lambda h: h.memset(self.ONEC[:, 0:1], 1.0), w=["ONEC"])
        self.wa_i = 0
        self.ps_reserved = set()
        self.wd_i = 0
        self.ps_i = 0

        S.dma("sp", self.IDF[:], I["c_ident_f"], w=["IDF"])
        S.dma("sp", self.ONF[:], I["c_ones_f"], w=["ONF"])
        S.dma("pool", self.IDB[:], I["c_ident_f"], w=["IDB"])
        S.dma("pool", self.ONB[:], I["c_ones_f"], w=["ONB"])
        with nc.allow_non_contiguous_dma(reason="small gain vectors, feature-major"):
            pass
        self.load_featmajor(self.GAIN, I["norm_gains"].rearrange("l s (c p) -> (l s c) p", p=128), DEPTH * 6 * KC, "GAIN")
        self.load_featmajor(self.MG, I["mem_norm_gain"].rearrange("l (c p) -> (l c) p", p=128), DEPTH * KC, "MG")

        self.scrS = nc.dram_tensor("scrS", [2, 128, NH * 128], F32, kind="Internal").ap()
        self.scrCT = nc.dram_tensor("scrCT", [2, 128, 108], F32, kind="Internal").ap()
        for pi, pname in enumerate(self.passes):
            self.pname = pname
            self.first_pass = (pi == 0)
            self.last_pass = (pi == len(self.passes) - 1)
            self.tok0 = 0 if pname == "A" else TP
            self.sq0 = 0 if pname == "A" else 16
            self.has_sample = True
            self.Tp = T if self.has_sample else TP
            sbs = [(0, 576), (576, 576)] if self.has_sample else [(0, 512), (512, 512)]
            msbs = [(0, 512), (512, 512)] + ([(1024, 128)] if self.has_sample else [])
            if pi > 0:
                S.barrier()
            self.load_x()
            for layer in range(self.nlayers):
                for sbi, sb in enumerate(sbs):
                    self.ffn(layer, 0, sb)
                if self.stage >= 2:
                    for sbi, sb in enumerate(msbs):
                        self.mixer(layer, sbi, sb)
                for sbi, sb in enumerate(sbs):
                    self.ffn(layer, 1, sb)
            self.store_y()

        blk = self.es.enter_context(nc.Block())
        S.emit(blk)
        self.es.close()

    def psum(self):
        while True:
            b = self.ps_i % 8
            self.ps_i += 1
            if b not in self.ps_reserved:
                return b

    def load_featmajor(self, dst, src_rows, nrows, key):
        S = self.S
        done = 0
        while done < nrows:
            n = min(128, nrows - done)
            st = self.ARENA[:, 0:256].bitcast(F32)
            S.dma("sp", st[0:n, :], src_rows[done:done + n, :], w=[("AT", 0)])
            b = self.psum()
            S.add("pe", lambda h, b=b, n=n, st=st: h.transpose(self.PS[:, b, 0:n], st[0:n, :], self.IDF[0:n, 0:n]),
                  r=[("AT", 0), "IDF"], w=[("ps", b)])
            S.add("dve", lambda h, b=b, n=n, d0=done: h.tensor_copy(out=dst[:, d0:d0 + n], in_=self.PS[:, b, 0:n]),
                  r=[("ps", b)], w=[key])
            done += n

    def load_x(self):
        S, I = self.S, self.I
        for ti in range(self.Tp // 128):
            src = I["xp"][self.tok0 + ti * 128:self.tok0 + (ti + 1) * 128, :] if ti < 8 else I["xs"][self.sq0 * 8:self.sq0 * 8 + 128, :]
            st = self.WD[:, ti % 2, 0:4096].bitcast(F32)
            key = ("WD", ti % 2)
            S.dma("sp", st, src, w=[key])
            for c4 in range(4):
                b = self.psum()
                for c in range(4):
                    cc = c4 * 4 + c
                    S.add("pe", lambda h, b=b, c=c, cc=cc, st=st: h.transpose(
                        self.PS[:, b, c * 128:(c + 1) * 128], st[:, cc * 128:(cc + 1) * 128], self.IDF[:]),
                        r=[key, "IDF"], w=[("ps", b)])
                S.add("dve" if c4 % 2 == 0 else "act",
                      (lambda h, b=b, c4=c4, ti=ti: h.tensor_copy(
                          out=self.XT[:, c4 * 4:(c4 + 1) * 4, ti * 128:(ti + 1) * 128],
                          in_=self.PS[:, b, :].rearrange("p (c t) -> p c t", c=4))) if c4 % 2 == 0 else
                      (lambda h, b=b, c4=c4, ti=ti: h.activation(
                          out=self.XT[:, c4 * 4:(c4 + 1) * 4, ti * 128:(ti + 1) * 128],
                          in_=self.PS[:, b, :].rearrange("p (c t) -> p c t", c=4), func=AF.Copy)),
                      r=[("ps", b)], w=[("XT", ti)])

    def store_y(self):
        S, O = self.S, self.O
        for ti in range(self.Tp // 128):
            dst = O["yp"][self.tok0 + ti * 128:self.tok0 + (ti + 1) * 128, :] if ti < 8 else O["ys"][self.sq0 * 8:self.sq0 * 8 + 128, :]
            st = self.WD[:, ti % 2, 0:4096].bitcast(F32)
            key = ("WD", ti % 2)
            for c4 in range(4):
                b = self.psum()
                for c in range(4):
                    cc = c4 * 4 + c
                    S.add("pe", lambda h, b=b, c=c, cc=cc, ti=ti: h.transpose(
                        self.PS[:, b, c * 128:(c + 1) * 128], self.XT[:, cc, ti * 128:(ti + 1) * 128], self.IDF[:]),
                        r=[("XT", ti), "IDF"], w=[("ps", b)])
                S.add("dve", lambda h, b=b, c4=c4, st=st: h.tensor_copy(
                    out=st[:, c4 * 512:(c4 + 1) * 512], in_=self.PS[:, b, :]),
                    r=[("ps", b)], w=[key])
            S.dma("sp", dst, st, r=[key])

    def xt_keys(self, off, n):
        return [("XT", t) for t in range(off // 128, (off + n + 127) // 128)]

    def rstd_bcast(self, src_fn, nch, off, n, inv_d, slot, rkeys):
        S = self.S
        cb = [(0, n)] if n <= 512 else [(0, n // 2), (n // 2, n - n // 2)]
        banks = [self.psum() for _ in cb]
        for c in range(nch):
            sqt = self.SQ[:, c % 2, 0:n]
            S.add("pool" if c % 2 else "dve",
                  lambda h, c=c, sqt=sqt: h.tensor_tensor(out=sqt, in0=src_fn(c), in1=src_fn(c), op=ALU.mult),
                  r=rkeys, w=[("SQ", c % 2)])
            for (o, m), b in zip(cb, banks):
                S.add("pe", lambda h, c=c, b=b, sqt=sqt, o=o, m=m: h.matmul(
                    self.PS[:, b, 0:m], self.ONF[:], sqt[:, o:o + m], start=(c == 0), stop=(c == nch - 1)),
                    r=[("SQ", c % 2), "ONF"], w=[("ps", b)])
        st = self.STAT[:, slot, 0:n]
        for (o, m), b in zip(cb, banks):
            S.add("act", lambda h, b=b, o=o, m=m: h.activation(
                out=st[:, o:o + m], in_=self.PS[:, b, 0:m], func=AF.Ln, bias=self.EPSC[:, 0:1], scale=inv_d),
                r=[("ps", b), "EPSC"], w=[("STAT", slot)])
        S.add("act", lambda h: h.activation(out=st, in_=st, func=AF.Exp, scale=-0.5),
              r=[("STAT", slot)], w=[("STAT", slot)])
        return st

    def r1_keys(self, lo, hi):
        return [("R1", p) for p in range(lo // 2048, (hi + 2047) // 2048)]

    def ht_view(self, n):
        return self.R1[:, 0:KC * n].rearrange("p (c t) -> p c t", c=KC), ["HT"]

    def wa_load(self, src, base):
        nslots = (19456 - base) // 2048
        s = self.wa_i % nslots
        self.wa_i += 1
        lo = base + s * 2048
        view = self.R1[:, lo:lo + 2048].rearrange("p (c n) -> p c n", c=KC)
        keys = [("WA", lo)]
        self.S.dma("pool", view, src.rearrange("(c p) n -> p c n", p=128), r=["WAALL"], w=keys)
        return view, keys + ["WAALL"]

    def at_keys(self, lo, hi):
        return [("AT", j) for j in range(lo // 576, (hi + 575) // 576)]

    def prenorm(self, gidx, off, n, hview, hkeys):
        S = self.S
        xk = self.xt_keys(off, n)
        rstd = self.rstd_bcast(lambda c: self.XT[:, c, off:off + n], KC, off, n, 1.0 / D, 0, xk)
        for c in range(KC):
            S.add("dve", lambda h, c=c: h.scalar_tensor_tensor(
                out=hview[:, c, 0:n], in0=self.XT[:, c, off:off + n],
                scalar=self.GAIN[:, gidx * KC + c:gidx * KC + c + 1], in1=rstd, op0=ALU.mult, op1=ALU.mult),
                r=xk + [("STAT", 0), "GAIN"], w=hkeys)

    def postnorm_add(self, gidx, off, n, half):
        S = self.S
        outt = self.R1[:, 0:2 * KC * n].bitcast(F32).rearrange("p (c t) -> p c t", c=KC)
        okeys = ["HT", "WAALL"]
        xk = self.xt_keys(off, n)
        rstd = self.rstd_bcast(lambda c: outt[:, c, 0:n], KC, off, n, 1.0 / D, 1, okeys)
        if half != 1.0:
            S.add("pool", lambda h: h.tensor_scalar(out=rstd, in0=rstd, scalar1=half, scalar2=None, op0=ALU.mult),
                  r=[("STAT", 1)], w=[("STAT", 1)])
        for c in range(KC):
            S.add("dve", lambda h, c=c: h.scalar_tensor_tensor(
                out=outt[:, c, 0:n], in0=outt[:, c, 0:n],
                scalar=self.GAIN[:, gidx * KC + c:gidx * KC + c + 1], in1=rstd, op0=ALU.mult, op1=ALU.mult),
                r=okeys + [("STAT", 1), "GAIN"], w=okeys)
            S.add("pool", lambda h, c=c: h.tensor_tensor(
                out=self.XT[:, c, off:off + n], in0=self.XT[:, c, off:off + n], in1=outt[:, c, 0:n], op=ALU.add),
                r=okeys + xk, w=xk)

    def ffn(self, layer, which, sb):
        S, I = self.S, self.I
        off, n = sb
        S.barrier()
        blks = blocks_of((0, n))
        g0 = layer * 6 + (0 if which == 0 else 4)
        hview, hkeys = self.ht_view(n)
        self.prenorm(g0, off, n, hview, hkeys)
        wgu = I["w_ffn_gu"][layer, which]
        wdn = I["w_ffn_dn"][layer, which]
        actT = self.ARENA[:, 0:FC * n].rearrange("p (f t) -> p f t", f=FC)
        sil = self.SQ
        ring = [(self.R1[:, 9216:13312], [("WA", 9216), ]), (self.WD[:, 0, 0:4096], [("WD", 0)]),
                (self.R1[:, 13312:17408], [("WA", 13312)]), (self.WD[:, 1, 0:4096], [("WD", 1)])]

        def wload(src, ri):
            buf, keys = ring[ri % 4]
            view = buf.rearrange("p (c n) -> p c n", c=KC)
            S.dma("pool", view, src.rearrange("(c p) n -> p c n", p=128), r=["WAALL"], w=keys)
            return view, keys + ["WAALL"]

        for j2 in range(FC // 2):
            gv, gk = wload(wgu[:, j2 * 256:(j2 + 1) * 256], 2 * j2)
            uv, uk = wload(wgu[:, FF + j2 * 256:FF + (j2 + 1) * 256], 2 * j2 + 1)
            for sub in range(2):
                j = 2 * j2 + sub
                for bi, (bo, bn) in enumerate(blks):
                    bg, bu = self.psum(), self.psum()
                    for c in range(KC):
                        self.mm(self.PS[:, bg, 0:bn], gv[:, c, sub * 128:(sub + 1) * 128], hview[:, c, bo:bo + bn],
                                c == 0, c == KC - 1, gk + hkeys, [("ps", bg)])
                    for c in range(KC):
                        self.mm(self.PS[:, bu, 0:bn], uv[:, c, sub * 128:(sub + 1) * 128], hview[:, c, bo:bo + bn],
                                c == 0, c == KC - 1, uk + hkeys, [("ps", bu)])
                    self.af(sil[:, bi, 0:bn], self.PS[:, bg, 0:bn], AF.Silu, [("ps", bg)], [("SQ", bi)])
                    self.tt("dve", actT[:, j, bo:bo + bn], sil[:, bi, 0:bn], self.PS[:, bu, 0:bn], ALU.mult,
                            [("ps", bu), ("SQ", bi)], [("AT", j)])
        outt = self.R1[:, 0:2 * KC * n].bitcast(F32).rearrange("p (c t) -> p c t", c=KC)
        okeys = ["HT", "WAALL"]
        atk = [("AT", j) for j in range(FC)]
        HF = FC // 2
        for dcp in range(KC // 2):
            banks = [[self.psum() for _ in blks] for _ in range(2)]
            for half in range(2):
                s_ = self.wd_i % 2
                self.wd_i += 1
                wv = self.WD[:, s_, :].rearrange("p (f n) -> p f n", f=HF)
                S.dma("pool", wv, wdn[half * HF * 128:(half + 1) * HF * 128, dcp * 256:(dcp + 1) * 256].rearrange(
                    "(f p) n -> p f n", p=128), w=[("WD", s_)])
                for sub in range(2):
                    for bi, (bo, bn) in enumerate(blks):
                        b_ = banks[sub][bi]
                        for f in range(HF):
                            self.mm(self.PS[:, b_, 0:bn], wv[:, f, sub * 128:(sub + 1) * 128], actT[:, half * HF + f, bo:bo + bn],
                                    half == 0 and f == 0, half == 1 and f == HF - 1, [("WD", s_)] + atk, [("ps", b_)])
            for sub in range(2):
                for bi, (bo, bn) in enumerate(blks):
                    b_ = banks[sub][bi]
                    self.cp("act" if bi else "dve", outt[:, dcp * 2 + sub, bo:bo + bn], self.PS[:, b_, 0:bn], [("ps", b_)], okeys)
        self.postnorm_add(g0 + 1, off, n, 0.5)

    def ar(self, lo, n, dt=BF16):
        if dt == F32:
            return self.ARENA[:, lo:lo + 2 * n].bitcast(F32), self.at_keys(lo, lo + 2 * n)
        return self.ARENA[:, lo:lo + n], self.at_keys(lo, lo + n)

    def mem_kv(self, layer):
        S, I, O = self.S, self.I, self.O
        KT = self.WD[:, 0, 0:1024].rearrange("p (h n) -> p h n", h=4)
        V = self.WD[:, 0, 1024:2048].rearrange("p (c f) -> p c f", c=2)
        wdk = [("WD", 0)]
        st, stk = [], []
        for i in range(2):
            v, k = self.ar(10240 + i * 4096, 2048, F32)
            st.append(v)
            stk.append(k)
        mh, mhk = self.ar(18432, 4096)
        mh = mh.rearrange("p (c n) -> p c n", c=KC)
        rs = self.EPSC[:, 1:2]
        for i in range(2):
            S.dma("sp", st[i], I["mem"][i * 128:(i + 1) * 128, :], w=stk[i])
            junk = self.WD[:, 1, 0:4096].bitcast(F32)
            self.tt("pool", junk, st[i], st[i], ALU.mult, stk[i], [("WD", 1)])
            S.add("dve", lambda h, junk=junk: h.tensor_reduce(out=rs, in_=junk, axis=mybir.AxisListType.X, op=ALU.add),
                  r=[("WD", 1)], w=["EPSC2"])
            S.add("act", lambda h: h.activation(out=rs, in_=rs, func=AF.Ln, bias=self.EPSC[:, 0:1], scale=1.0 / D),
                  r=["EPSC2", "EPSC"], w=["EPSC2"])
            S.add("act", lambda h: h.activation(out=rs, in_=rs, func=AF.Exp, scale=-0.5), r=["EPSC2"], w=["EPSC2"])
            S.add("dve", lambda h, i=i: h.tensor_scalar(out=st[i], in0=st[i], scalar1=rs, scalar2=None, op0=ALU.mult),
                  r=stk[i] + ["EPSC2"], w=stk[i])
            for c4 in range(4):
                b = self.psum()
                for c in range(4):
                    cc = c4 * 4 + c
                    S.add("pe", lambda h, b=b, c=c, cc=cc, i=i: h.transpose(
                        self.PS[:, b, c * 128:(c + 1) * 128], st[i][:, cc * 128:(cc + 1) * 128], self.IDF[:]),
                        r=stk[i] + ["IDF"], w=[("ps", b)])
                for c in range(4):
                    cc = c4 * 4 + c
                    S.add("dve", lambda h, b=b, c=c, cc=cc, i=i: h.tensor_scalar(
                        out=mh[:, cc, i * 128:(i + 1) * 128], in0=self.PS[:, b, c * 128:(c + 1) * 128],
                        scalar1=self.MG[:, layer * KC + cc:layer * KC + cc + 1], scalar2=None, op0=ALU.mult),
                        r=[("ps", b), "MG"], w=mhk)
        kvf, kvfk = self.ar(10240, 2048, F32)
        kvf = kvf.rearrange("p (f n) -> p f n", f=8)
        for fc in range(8):
            wv, wk = self.wa_load(I["w_mem_kv"][layer][:, fc * 128:(fc + 1) * 128], 10240)
            b = self.psum()
            for c in range(KC):
                S.add("pe", lambda h, c=c, b=b, wv=wv: h.matmul(self.PS[:, b, 0:256], wv[:, c, :], mh[:, c, :],
                                                                 start=(c == 0), stop=(c == KC - 1)),
                      r=wk + mhk, w=[("ps", b)])
            S.add("dve", lambda h, b=b, fc=fc: h.tensor_copy(out=kvf[:, fc, :], in_=self.PS[:, b, 0:256]),
                  r=[("ps", b)], w=kvfk)
            if fc < 4:
                S.add("act", lambda h, b=b, fc=fc: h.activation(out=KT[:, fc, :], in_=self.PS[:, b, 0:256], func=AF.Copy),
                      r=[("ps", b)], w=wdk)
        tok, tokk = self.ar(14336, 1024, F32)
        for nc_ in range(2):
            for half in range(2):
                b = self.psum()
                for f in range(4):
                    S.add("pe", lambda h, b=b, f=f, half=half, nc_=nc_: h.transpose(
                        self.PS[:, b, f * 128:(f + 1) * 128], kvf[:, half * 4 + f, nc_ * 128:(nc_ + 1) * 128], self.IDF[:]),
                        r=kvfk + ["IDF"], w=[("ps", b)])
                S.add("dve", lambda h, b=b, half=half: h.tensor_copy(out=tok[:, half * 512:(half + 1) * 512], in_=self.PS[:, b, :]),
                      r=[("ps", b)], w=tokk)
                if half == 1:
                    S.add("act", lambda h, b=b, nc_=nc_: h.activation(out=V[:, nc_, :], in_=self.PS[:, b, :], func=AF.Copy),
                          r=[("ps", b)], w=wdk)
            if self.first_pass:
                S.dma("sp", O["memk"][layer, nc_ * 128:(nc_ + 1) * 128, :], tok[:, 0:512], r=tokk)
                S.dma("sp", O["memv"][layer, nc_ * 128:(nc_ + 1) * 128, :], tok[:, 512:1024], r=tokk)

    def mm(self, out, lhsT, rhs, start, stop, r, w):
        self.S.add("pe", lambda h: h.matmul(out, lhsT, rhs, start=start, stop=stop), r=r, w=w)

    def tr(self, out, in_, ident, r, w):
        self.S.add("pe", lambda h: h.transpose(out, in_, ident), r=r, w=w)

    def ts(self, eng, out, in0, s1, s2, op0, op1, r, w):
        if op1 is None:
            self.S.add(eng, lambda h: h.tensor_scalar(out=out, in0=in0, scalar1=s1, scalar2=None, op0=op0), r=r, w=w)
        else:
            self.S.add(eng, lambda h: h.tensor_scalar(out=out, in0=in0, scalar1=s1, scalar2=s2, op0=op0, op1=op1), r=r, w=w)

    def tt(self, eng, out, in0, in1, op, r, w):
        self.S.add(eng, lambda h: h.tensor_tensor(out=out, in0=in0, in1=in1, op=op), r=r, w=w)

    def stt(self, out, in0, scalar, in1, op0, op1, r, w):
        self.S.add("dve", lambda h: h.scalar_tensor_tensor(out=out, in0=in0, scalar=scalar, in1=in1, op0=op0, op1=op1), r=r, w=w)

    def af(self, out, in_, func, r, w, bias=None, scale=None):
        kw = {}
        if bias is not None:
            kw["bias"] = bias
        if scale is not None:
            kw["scale"] = scale
        self.S.add("act", lambda h: h.activation(out=out, in_=in_, func=func, **kw), r=r, w=w)

    def cp(self, eng, out, in_, r, w):
        if eng == "act":
            self.af(out, in_, AF.Copy, r, w)
        else:
            self.S.add(eng, lambda h: h.tensor_copy(out=out, in_=in_), r=r, w=w)

    def psb(self, b):
        return self.PS[:, b, :].bitcast(BF16)

    def sc_reset(self):
        self.sc_off = 8192
        self.sc2_off = 2048
        self.sc3_off = 18432
        self.sc4_off = 0

    def sc(self, name, n, dt=BF16, pool=0):
        ne = n * (2 if dt == F32 else 1)
        ne = (ne + 15) // 16 * 16
        if pool == 0:
            lo = self.sc_off
            self.sc_off += ne
            assert self.sc_off <= 25344, ("ARENA scratch overflow", name, self.sc_off)
            v = self.ARENA[:, lo:lo + ne]
        elif pool == 1:
            lo = self.sc2_off
            self.sc2_off += ne
            assert self.sc2_off <= 8192, ("R1 scratch overflow", name, self.sc2_off)
            v = self.R1[:, lo:lo + ne]
        elif pool == 2:
            lo = self.sc3_off
            self.sc3_off += ne
            assert self.sc3_off <= 19456, ("R1 tail overflow", name, self.sc3_off)
            v = self.R1[:, lo:lo + ne]
        else:
            lo = self.sc4_off
            self.sc4_off += ne
            assert self.sc4_off <= 5120, ("WD0 scratch overflow", name, self.sc4_off)
            v = self.WD[:, 0, lo:lo + ne]
        if dt == F32:
            v = v.bitcast(F32)
        return v[:, 0:n], [("sc", name)]

    def rows_to_featmajor(self, dst, dkeys, src_rows, nrows, stage, skeys):
        S = self.S
        S.dma("sp", stage[0:nrows, :], src_rows, w=skeys)
        b = self.psum()
        self.tr(self.PS[:, b, 0:nrows], stage[0:nrows, :], self.IDF[0:nrows, 0:nrows], skeys + ["IDF"], [("ps", b)])
        self.cp("dve", dst, self.PS[:, b, 0:nrows], [("ps", b)], dkeys)

    def mixer(self, layer, sbi, sb, commit=True):
        S, I, O = self.S, self.I, self.O
        off, n = sb
        nt = n // 128
        sample = (off >= TP)
        isA = (layer % 2 == 0)
        j = layer // 2
        S.barrier()
        self.sc_reset()
        if sbi == 0:
            self.mem_kv(layer)
            S.barrier()
            self.sc_reset()
        hview, hkeys = self.ht_view(n)
        self.prenorm(layer * 6 + 2, off, n, hview, hkeys)
        mixt = self.ARENA[:, 0:KC * n].rearrange("p (c t) -> p c t", c=KC)
        mk = lambda c: [("MIXT", c)]
        ctx = dict(layer=layer, j=j, off=off, n=n, nt=nt, sample=sample, hview=hview, hkeys=hkeys, mixt=mixt, mk=mk,
                   sbi=sbi, commit=commit)
        if "mix" in self.feat:
            if isA:
                self.delta_sb(ctx)
            else:
                self.gmlp_sb(ctx)
        else:
            for c in range(12):
                S.add("pool", lambda h, c=c: h.memset(mixt[:, c, :], 0.0), w=mk(c))
        if "attn" in self.feat:
            self.attn_sb(ctx)
        else:
            for c in range(12, 16):
                S.add("pool", lambda h, c=c: h.memset(mixt[:, c, :], 0.0), w=mk(c))
        self.outproj_sb(ctx)

    def outproj_sb(self, ctx):
        S, I = self.S, self.I
        layer, off, n, mixt, mk = ctx["layer"], ctx["off"], ctx["n"], ctx["mixt"], ctx["mk"]
        S.barrier()
        outt = self.R1[:, 0:2 * KC * n].bitcast(F32).rearrange("p (c t) -> p c t", c=KC)
        okeys = ["HT", "WAALL"]
        blks = blocks_of((0, n)) if n > 256 else [(0, n)]
        wo = I["w_out"][layer]
        allmk = [("MIXT", c) for c in range(KC)]
        for dc in range(KC):
            s = dc % 2
            lo = 1536 + s * 2048
            wv = self.WD[:, 1, lo:lo + 2048].rearrange("p (c n) -> p c n", c=KC)
            wk = [("WO", s)]
            S.dma("pool", wv, wo[:, dc * 128:(dc + 1) * 128].rearrange("(c p) n -> p c n", p=128), w=wk)
            for bi, (bo, bn) in enumerate(blks):
                b = self.psum()
                for c in range(KC):
                    self.mm(self.PS[:, b, 0:bn], wv[:, c, :], mixt[:, c, bo:bo + bn], c == 0, c == KC - 1,
                            wk + allmk, [("ps", b)])
                self.cp("act" if bi else "dve", outt[:, dc, bo:bo + bn], self.PS[:, b, 0:bn], [("ps", b)], okeys)
        self.postnorm_add(layer * 6 + 3, off, n, 1.0)

    def attn_sb(self, ctx):
        S, I = self.S, self.I
        layer, off, n, nt, sample = ctx["layer"], ctx["off"], ctx["n"], ctx["nt"], ctx["sample"]
        hview, hkeys, mixt, mk = ctx["hview"], ctx["hkeys"], ctx["mixt"], ctx["mk"]
        S.barrier()
        self.sc_reset()
        isA = (layer % 2 == 0)
        w_in = I["w_in_a"][layer // 2] if isA else I["w_in_b"][layer // 2]
        qcol0 = (IN_A - 512) if isA else (IN_B - 512)
        qT, qk = self.sc("qT", 4 * n)
        qT = qT.rearrange("p (h t) -> p h t", h=4)
        blks = blocks_of((0, n)) if n > 256 else [(0, n)]
        for hx in range(4):
            wv, wk = self.wa_load(w_in[:, qcol0 + hx * 128:qcol0 + (hx + 1) * 128], 8192)
            for bi, (bo, bn) in enumerate(blks):
                b = self.psum()
                for c in range(KC):
                    self.mm(self.PS[:, b, 0:bn], wv[:, c, :], hview[:, c, bo:bo + bn], c == 0, c == KC - 1,
                            wk + hkeys, [("ps", b)])
                self.cp("act", qT[:, hx, bo:bo + bn], self.PS[:, b, 0:bn], [("ps", b)], qk)
        P, Pk = self.sc("P", 512, F32)
        Pn, Pnk = self.sc("Pn", 512)
        PT, PTk = self.sc("PT", 512)
        sm, smk = self.sc("sm", 8, F32)
        P3 = P.rearrange("p (h n) -> p h n", h=2)
        Pn3 = Pn.rearrange("p (h n) -> p h n", h=2)
        scale = 128.0 ** -0.5

        def attend(col0, m, KT, V, kvk):
            for hp in range(2):
                b = self.psum()
                for hh in range(2):
                    self.mm(self.PS[0:m, b, hh * 256:(hh + 1) * 256], qT[:, 2 * hp + hh, col0:col0 + m], KT[:, 2 * hp + hh, :],
                            True, True, qk + kvk, [("ps", b)])
                ps3 = self.PS[0:m, b, :].rearrange("p (h n) -> p h n", h=2)
                S.add("dve", lambda h, ps3=ps3, m=m: h.tensor_reduce(out=sm[0:m, 0:2], in_=ps3, axis=mybir.AxisListType.X, op=ALU.max),
                      r=[("ps", b)], w=smk)
                self.tt("dve", P3[0:m], ps3, sm[0:m, 0:2].unsqueeze(2).broadcast_to([m, 2, 256]), ALU.subtract,
                        [("ps", b)] + smk, Pk)
                self.af(P[0:m], P[0:m], AF.Exp, Pk, Pk, scale=scale)
                S.add("dve", lambda h, m=m: h.tensor_reduce(out=sm[0:m, 2:4], in_=P3[0:m], axis=mybir.AxisListType.X, op=ALU.add),
                      r=Pk, w=smk)
                S.add("dve", lambda h, m=m: h.reciprocal(out=sm[0:m, 4:6], in_=sm[0:m, 2:4]), r=smk, w=smk)
                self.tt("dve", Pn3[0:m], P3[0:m], sm[0:m, 4:6].unsqueeze(2).broadcast_to([m, 2, 256]), ALU.mult,
                        Pk + smk, Pnk)
                b2 = self.psum()
                pb = self.psb(b2)
                for hh in range(2):
                    for ncnk in range(2):
                        q = hh * 2 + ncnk
                        self.tr(pb[:, q * m:(q + 1) * m], Pn3[0:m, hh, ncnk * 128:(ncnk + 1) * 128], self.IDB[0:m, 0:m],
                                Pnk + ["IDB"], [("ps", b2)])
                self.cp("act", PT[:, 0:4 * m], pb[:, 0:4 * m], [("ps", b2)], PTk)
                b3 = self.psum()
                for hh in range(2):
                    for ncnk in range(2):
                        q = hh * 2 + ncnk
                        hd = 2 * hp + hh
                        self.mm(self.PS[:, b3, hh * m:(hh + 1) * m], V[:, ncnk, hd * 128:(hd + 1) * 128], PT[:, q * m:(q + 1) * m],
                                ncnk == 0, ncnk == 1, PTk + kvk, [("ps", b3)])
                self.cp("dve", mixt[:, 12 + 2 * hp:12 + 2 * hp + 2, col0:col0 + m],
                        self.PS[:, b3, 0:2 * m].rearrange("p (h t) -> p h t", h=2), [("ps", b3)],
                        [("MIXT", 12 + 2 * hp), ("MIXT", 13 + 2 * hp)])

        if not sample:
            KT = self.WD[:, 0, 0:1024].rearrange("p (h n) -> p h n", h=4)
            V = self.WD[:, 0, 1024:2048].rearrange("p (c f) -> p c f", c=2)
            for t in range(nt):
                attend(t * 128, 128, KT, V, [("WD", 0)])
        else:
            bufs = []
            for i in range(2):
                kt_, ktk = self.sc("Ktok%d" % i, 1024)
                v_, vk = self.sc("Vs%d" % i, 1024)
                kT_, kTk = self.sc("KTs%d" % i, 1024)
                bufs.append((kt_, ktk, v_, vk, kT_, kTk))
            for s in range(16):
                kt_, ktk, v_, vk, kT_, kTk = bufs[s % 2]
                kt3 = kt_.rearrange("p (c f) -> p c f", c=2)
                v3 = v_.rearrange("p (c f) -> p c f", c=2)
                kT3 = kT_.rearrange("p (h n) -> p h n", h=4)
                S.dma("pool", kt3, I["ck"][layer, self.sq0 + s].rearrange("(c p) f -> p c f", p=128), w=ktk)
                S.dma("pool", v3, I["cv"][layer, self.sq0 + s].rearrange("(c p) f -> p c f", p=128), w=vk)
                b = self.psum()
                pb = self.psb(b)
                for hd in range(4):
                    for ncnk in range(2):
                        self.tr(pb[:, hd * 256 + ncnk * 128:hd * 256 + (ncnk + 1) * 128], kt3[:, ncnk, hd * 128:(hd + 1) * 128],
                                self.IDB[:], ktk + ["IDB"], [("ps", b)])
                self.cp("act", kT_, pb[:, 0:1024], [("ps", b)], kTk)
                attend(s * 8, 8, kT3, v3, kTk + vk)

    def gmlp_sb(self, ctx):
        S, I, O = self.S, self.I, self.O
        j, off, n, nt, sample = ctx["j"], ctx["off"], ctx["n"], ctx["nt"], ctx["sample"]
        hview, hkeys, mixt, mk = ctx["hview"], ctx["hkeys"], ctx["mixt"], ctx["mk"]
        w_in = I["w_in_b"][j]
        blks = blocks_of((0, n)) if n > 256 else [(0, n)]
        G = 12
        VG, VGk = self.sc("VG", G * n, F32)
        VG = VG.rearrange("p (g t) -> p g t", g=G)
        LG, LGk = self.sc("LG", 16, F32)
        LB, LBk = self.sc("LB", 16, F32)
        stg, stgk = self.sc("wsf", 128, F32)
        self.rows_to_featmajor(LG[:, 0:G], LGk, I["gmlp_ln_gain"][j].rearrange("(g p) -> g p", p=128), G, stg, stgk)
        self.rows_to_featmajor(LB[:, 0:G], LBk, I["gmlp_ln_bias"][j].rearrange("(g p) -> g p", p=128), G, stg, stgk)
        BROW, BRk = self.sc("BROW", 1536)
        S.dma("pool", BROW[0:1, :], I["b_spatial"][j:j + 1].rearrange("a g i -> a (g i)"), w=BRk)
        sq, sqk = self.sc("tmpA", n, F32)
        bsum = [self.psum() for _ in blks]
        bsq = [self.psum() for _ in blks]
        self.ps_reserved = set(bsum + bsq)
        for g in range(G):
            wv, wk = self.wa_load(w_in[:, 1536 + g * 128:1536 + (g + 1) * 128], 8192)
            for bi, (bo, bn) in enumerate(blks):
                b = self.psum()
                for c in range(KC):
                    self.mm(self.PS[:, b, 0:bn], wv[:, c, :], hview[:, c, bo:bo + bn], c == 0, c == KC - 1, wk + hkeys, [("ps", b)])
                self.af(VG[:, g, bo:bo + bn], self.PS[:, b, 0:bn], AF.Gelu, [("ps", b)], VGk)
            self.tt("pool", sq, VG[:, g, :], VG[:, g, :], ALU.mult, VGk, sqk)
            for bi, (bo, bn) in enumerate(blks):
                self.mm(self.PS[:, bsum[bi], 0:bn], self.ONF[:], VG[:, g, bo:bo + bn], g == 0, g == G - 1, VGk + ["ONF"], [("ps", bsum[bi])])
                self.mm(self.PS[:, bsq[bi], 0:bn], self.ONF[:], sq[:, bo:bo + bn], g == 0, g == G - 1, sqk + ["ONF"], [("ps", bsq[bi])])
        mean = self.STAT[:, 0, 0:n]
        rstd = self.STAT[:, 1, 0:n]
        for bi, (bo, bn) in enumerate(blks):
            self.ts("dve", mean[:, bo:bo + bn], self.PS[:, bsum[bi], 0:bn], 1.0 / 1536, None, ALU.mult, None, [("ps", bsum[bi])], [("STAT", 0)])
            self.tt("dve", sq[:, bo:bo + bn], mean[:, bo:bo + bn], mean[:, bo:bo + bn], ALU.mult, [("STAT", 0)], sqk)
            self.stt(rstd[:, bo:bo + bn], self.PS[:, bsq[bi], 0:bn], 1.0 / 1536, sq[:, bo:bo + bn], ALU.mult, ALU.subtract,
                     [("ps", bsq[bi])] + sqk, [("STAT", 1)])
        self.af(rstd, rstd, AF.Ln, [("STAT", 1), "EPSC"], [("STAT", 1)], bias=self.EPSC[:, 0:1], scale=1.0)
        self.af(rstd, rstd, AF.Exp, [("STAT", 1)], [("STAT", 1)], scale=-0.5)
        self.ps_reserved = set()
        vn, vnk = self.sc("vn", n, F32)
        vnb, vnbk = self.sc("vnb", n, BF16, 2)
        ug, ugk = sq, sqk
        wsf, wsfk = stg, stgk
        wsb, wsbk = self.sc("wsb", 128, BF16, 2)
        vtok, vtokk = self.sc("vtok", 128, BF16, 2)
        if sample:
            m8, m8k = self.sc("m8", 128, F32)
            rep, repk = self.sc("rep", 128, F32)
            S.dma("sp", m8, I["c_mask8"], w=m8k)
            S.dma("sp", rep, I["c_rep8"], w=repk)
            wst, wstk = self.sc("wst", 128, F32)
            wrep, wrepk = self.sc("wrep", 8, F32)
            brow8, br8k = self.sc("brow8", 128)
            gvst, gvstk = self.sc("gvst", 1536, F32)
        else:
            trl, trlk = self.sc("trl", 128, F32, 2)
            S.dma("sp", trl, I["c_tril_t"], w=trlk)
        for g in range(G):
            self.tt("dve", vn, VG[:, g, :], mean, ALU.subtract, VGk + [("STAT", 0)], vnk)
            self.tt("pool", vn, vn, rstd, ALU.mult, vnk + [("STAT", 1)], vnk)
            self.ts("dve", vn, vn, LG[:, g:g + 1], LB[:, g:g + 1], ALU.mult, ALU.add, vnk + LGk + LBk, vnk)
            self.cp("act", vnb, vn, vnk, vnbk)
            S.dma("sp", wsf, I["w_spatial"][j, g], w=wsfk)
            b = self.psum()
            self.tr(self.PS[:, b, 0:128], wsf, self.IDF[:], wsfk + ["IDF"], [("ps", b)])
            wv, wk = self.wa_load(w_in[:, g * 128:(g + 1) * 128], 8192)
            for bi, (bo, bn) in enumerate(blks):
                bu = self.psum()
                for c in range(KC):
                    self.mm(self.PS[:, bu, 0:bn], wv[:, c, :], hview[:, c, bo:bo + bn], c == 0, c == KC - 1, wk + hkeys, [("ps", bu)])
                self.af(ug[:, bo:bo + bn], self.PS[:, bu, 0:bn], AF.Gelu, [("ps", bu)], ugk)
            if not sample:
                self.tt("dve", wsb, self.PS[:, b, 0:128], trl, ALU.mult, [("ps", b)] + trlk, wsbk)
                for t in range(nt):
                    b2 = self.psum()
                    pb = self.psb(b2)
                    self.tr(pb[:, 0:128], vnb[:, t * 128:(t + 1) * 128], self.IDB[:], vnbk + ["IDB"], [("ps", b2)])
                    self.cp("act", vtok, pb[:, 0:128], [("ps", b2)], vtokk)
                    b3 = self.psum()
                    self.mm(self.PS[:, b3, 0:128], vtok, wsb, True, False, vtokk + wsbk, [("ps", b3)])
                    self.mm(self.PS[:, b3, 0:128], self.ONB[0:1, :], BROW[0:1, g * 128:(g + 1) * 128], False, True,
                            BRk + ["ONB"], [("ps", b3)])
                    self.tt("dve", mixt[:, g, t * 128:(t + 1) * 128], ug[:, t * 128:(t + 1) * 128], self.PS[:, b3, 0:128], ALU.mult,
                            ugk + [("ps", b3)], mk(g))
            else:
                self.cp("dve", wst, self.PS[:, b, 0:128], [("ps", b)], wstk)
                b4 = self.psum()
                self.mm(self.PS[:, b4, 0:8], rep, wst[:, 0:8], True, True, repk + wstk, [("ps", b4)])
                self.cp("dve", wrep, self.PS[:, b4, 0:8], [("ps", b4)], wrepk)
                self.tt("dve", wsb.rearrange("p (s i) -> p s i", s=16), wrep.unsqueeze(1).broadcast_to([128, 16, 8]),
                        m8.rearrange("p (s i) -> p s i", s=16), ALU.mult, wrepk + m8k, wsbk)
                self.cp("dve", brow8[0:1, :].rearrange("p (s i) -> p s i", s=16),
                        BROW[0:1, g * 128:g * 128 + 8].unsqueeze(1).broadcast_to([1, 16, 8]), BRk, br8k)
                b2 = self.psum()
                pb = self.psb(b2)
                self.tr(pb[:, 0:128], vnb[:, 0:128], self.IDB[:], vnbk + ["IDB"], [("ps", b2)])
                self.cp("act", vtok, pb[:, 0:128], [("ps", b2)], vtokk)
                b3 = self.psum()
                self.mm(self.PS[:, b3, 0:128], vtok, wsb, True, False, vtokk + wsbk, [("ps", b3)])
                self.mm(self.PS[:, b3, 0:128], self.ONB[0:1, :], brow8[0:1, :], False, True, br8k + ["ONB"], [("ps", b3)])
                self.tt("dve", mixt[:, g, 0:128], ug[:, 0:128], self.PS[:, b3, 0:128], ALU.mult, ugk + [("ps", b3)], mk(g))
                b5 = self.psum()
                self.tr(self.PS[:, b5, 0:128], vn[:, 0:128], self.IDF[:], vnk + ["IDF"], [("ps", b5)])
                self.cp("act", gvst[:, g * 128:(g + 1) * 128], self.PS[:, b5, 0:128], [("ps", b5)], gvstk)
        if sample:
            S.dma("sp", O["gv_s"][j, self.sq0 * 8:self.sq0 * 8 + 128, :], gvst, r=gvstk)

    def delta_sb(self, ctx):
        S, I, O = self.S, self.I, self.O
        j, off, n, nt, sample, sbi = ctx["j"], ctx["off"], ctx["n"], ctx["nt"], ctx["sample"], ctx["sbi"]
        hview, hkeys, mixt, mk, commit = ctx["hview"], ctx["hkeys"], ctx["mixt"], ctx["mk"], ctx["commit"]
        w_in = I["w_in_a"][j]
        B = 8 if sample else 64
        nb = 128 // B
        nlev = 2 if sample else 5
        nseq, L = (16, 8) if sample else (1, n)
        blks = blocks_of((0, n)) if n > 256 else [(0, n)]
        Sf = self.WD[:, 0, 2048:5120].bitcast(F32).rearrange("p (h v) -> p h v", h=NH)
        Sb = self.WD[:, 1, 0:1536].rearrange("p (h v) -> p h v", h=NH)
        CT = self.WD[:, 0, 5120:5336].bitcast(F32).rearrange("p (c r) -> p c r", c=36)
        Sfk = lambda h: [("Sf", h)]
        Sbk = lambda h: [("Sb", h)]
        CTk = ["CT"]
        allSf = [("Sf", h) for h in range(NH)]
        allSb = [("Sb", h) for h in range(NH)]
        if sbi == 0 and self.first_pass:
            S.add("pool", lambda h: h.memset(self.WD[:, 0, 2048:5120].bitcast(F32), 0.0), w=allSf)
            S.add("pool", lambda h: h.memset(self.WD[:, 1, 0:1536], 0.0), w=allSb)
            S.add("pool", lambda h: h.memset(self.WD[:, 0, 5120:5336].bitcast(F32), 0.0), w=CTk)
        elif sbi == 0:
            S.dma("sp", self.WD[:, 0, 2048:5120].bitcast(F32), self.scrS[j], r=[("scrS", j)], w=allSf)
            S.dma("pool", self.WD[:, 1, 0:1536], self.scrS[j], r=[("scrS", j)], w=allSb)
            S.dma("sp", self.WD[:, 0, 5120:5336].bitcast(F32), self.scrCT[j], r=[("scrCT", j)], w=CTk)
        if sample:
            NEG, NEGk = self.sc("NEG8", 128, F32)
            STR, STRk = self.sc("STR8", 128, F32)
            UBL, UBLk = self.sc("UBL8", 128, F32)
            BBL, BBLk = self.sc("BBL8", 128, F32)
            COLM, COLMk = self.sc("COLM8", nb * 128)
            ROWM, ROWMk = self.sc("ROWM8", nb, F32)
            for v_, k_, nm in ((NEG, NEGk, "neg8"), (STR, STRk, "strict8"), (UBL, UBLk, "ublk8"), (BBL, BBLk, "bblk8"),
                               (ROWM, ROWMk, "rowmask8")):
                S.dma("sp", v_, I["c_" + nm], w=k_)
            S.dma("pool", COLM, I["c_colmask8"], w=COLMk)
        else:
            NEG, STR, UBL, BBL, COLM, ROWM = self.M64
            NEGk = STRk = UBLk = BBLk = COLMk = ROWMk = ["M64"]
        COLM3 = COLM.rearrange("p (s t) -> p s t", s=nb)
        stg, stgk = self.sc("stg", 128, F32)
        CW = []
        for r_ in range(4):
            cw, cwk = self.sc("CW%d" % r_, 36, F32)
            self.rows_to_featmajor(cw, cwk, I["conv_w"][j, r_].rearrange("(c p) -> c p", p=128), 36, stg, stgk)
            CW.append((cw, cwk))
        DNG, DNGk = self.sc("DNG", 1, F32)
        self.rows_to_featmajor(DNG, DNGk, I["delta_norm_gain"][j:j + 1, :], 1, stg, stgk)
        DTB, DTBk = self.sc("DTB", NH, F32)
        NEGA, NEGAk = self.sc("NEGA", NH, F32)
        S.dma("sp", DTB, I["dt_bias"][j:j + 1, :].broadcast_to([128, NH]), w=DTBk)
        S.dma("sp", NEGA, I["a_log"][j:j + 1, :].broadcast_to([128, NH]), w=NEGAk)
        self.af(NEGA, NEGA, AF.Exp, NEGAk, NEGAk)
        self.ts("dve", NEGA, NEGA, -1.0, None, ALU.mult, None, NEGAk, NEGAk)
        Wab, Wabk = self.sc("Wab", KC * 24)
        Wab = Wab.rearrange("p (c n) -> p c n", c=KC)
        S.dma("pool", Wab, w_in[:, 6144:6168].rearrange("(c p) n -> p c n", p=128), w=Wabk)

        def scal(name):
            v_, k_ = self.sc(name, nt * NH, F32)
            return v_.rearrange("p (t h) -> p t h", t=nt), k_
        BETA, BETAk = scal("BETA")
        G, Gk = scal("G")
        NG, NGk = scal("NG")
        GC, GCk = scal("GC")
        EDEC, EDECk = scal("EDEC")
        EGC, EGCk = scal("EGC")
        BE, BEk = scal("BE")
        GLB, GLBk = self.sc("GLB", nt * nb * NH, F32)
        GLB = GLB.rearrange("p (t s h) -> p t s h", t=nt, s=nb)
        G2, G2k = self.sc("G2", nb * NH, F32)
        for t in range(nt):
            b = self.psum()
            for c in range(KC):
                self.mm(self.PS[:, b, 0:24], hview[:, c, t * 128:(t + 1) * 128], Wab[:, c, :], c == 0, c == KC - 1,
                        hkeys + Wabk, [("ps", b)])
            self.af(BETA[:, t, :], self.PS[:, b, 12:24], AF.Sigmoid, [("ps", b)], BETAk)
            self.tt("dve", G[:, t, :], self.PS[:, b, 0:12], DTB, ALU.add, [("ps", b)] + DTBk, Gk)
            self.af(G[:, t, :], G[:, t, :], AF.Exp, Gk, Gk)
            self.af(G[:, t, :], G[:, t, :], AF.Ln, Gk + ["ONEC"], Gk, bias=self.ONEC[:, 0:1], scale=1.0)
            self.tt("dve", G[:, t, :], G[:, t, :], NEGA, ALU.mult, Gk + NEGAk, Gk)
            self.ts("dve", NG[:, t, :], G[:, t, :], -1.0, None, ALU.mult, None, Gk, NGk)
            b1 = self.psum()
            self.mm(self.PS[:, b1, 0:12], UBL, G[:, t, :], True, True, UBLk + Gk, [("ps", b1)])
            self.mm(self.PS[:, b1, 16:28], BBL, G[:, t, :], True, True, BBLk + Gk, [("ps", b1)])
            self.cp("dve", GC[:, t, :], self.PS[:, b1, 0:12], [("ps", b1)], GCk)
            self.af(EGC[:, t, :], self.PS[:, b1, 0:12], AF.Exp, [("ps", b1)], EGCk)
            self.tt("dve", EDEC[:, t, :], self.PS[:, b1, 16:28], GC[:, t, :], ALU.subtract, [("ps", b1)] + GCk, EDECk)
            self.af(EDEC[:, t, :], EDEC[:, t, :], AF.Exp, EDECk, EDECk)
            self.tt("dve", BE[:, t, :], BETA[:, t, :], EGC[:, t, :], ALU.mult, BETAk + EGCk, BEk)
            self.tt("dve", G2.rearrange("p (s h) -> p s h", s=nb), G[:, t, :].unsqueeze(1).broadcast_to([128, nb, NH]),
                    ROWM.unsqueeze(2).broadcast_to([128, nb, NH]), ALU.mult, Gk + ROWMk, G2k)
            b2 = self.psum()
            self.mm(self.PS[:, b2, 0:nb * NH], self.ONF[:], G2, True, True, G2k + ["ONF"], [("ps", b2)])
            self.af(GLB[:, t].rearrange("p s h -> p (s h)"), self.PS[:, b2, 0:nb * NH], AF.Exp, [("ps", b2)], GLBk)
        if sample:
            SCT = self.WD[:, 1, 1536:1536 + 2 * 36 * 48].bitcast(F32).rearrange("p (c q) -> p c q", c=36)
            SCTk = ["SCT"]
            cst, cstk = self.sc("cst", 512, F32)
            src = I["sconv"][j, self.sq0:self.sq0 + 16].rearrange("s r f -> (s r) f")
            for c9 in range(9):
                S.dma("sp", cst[0:48, :], src[:, c9 * 512:(c9 + 1) * 512], w=cstk)
                b = self.psum()
                for c in range(4):
                    self.tr(self.PS[:, b, c * 48:(c + 1) * 48], cst[0:48, c * 128:(c + 1) * 128], self.IDF[0:48, 0:48],
                            cstk + ["IDF"], [("ps", b)])
                self.cp("dve", SCT[:, c9 * 4:(c9 + 1) * 4, :], self.PS[:, b, 0:192].rearrange("p (c q) -> p c q", c=4),
                        [("ps", b)], SCTk)
        W3 = L + 3
        PQ = []
        for part in range(3):
            v_, k_ = self.sc("PQ%d" % part, nseq * W3 + 8, F32)
            PQ.append((v_[:, 0:nseq * W3].rearrange("p (s w) -> p s w", s=nseq), k_))
        YC, YCk = self.sc("YC", n, F32)
        YC3 = YC.rearrange("p (s l) -> p s l", s=nseq)
        QT, QTk = self.sc("QT", n)
        KTb, KTbk = self.sc("KTb", n)
        VTb, VTbk = self.sc("VTf", n, F32)
        KTf, KTfk = self.sc("KTf", n, F32)
        SZ, SZk = self.sc("SZ", n, F32)
        if sample:
            S0f, S0fk = self.sc("S0f", 16 * 128, F32, 1)
            S0b, S0bk = self.sc("S0b", 16 * 128, BF16, 3)
            S0f3 = S0f.rearrange("p (s v) -> p s v", s=16)
            S0b3 = S0b.rearrange("p (s v) -> p s v", s=16)
        nbuf = 1
        TB = []
        for i in range(nbuf):
            d = {}
            for nm, sz, dt in (("KBE", 128, F32), ("KDEC", 128, BF16), ("VB", 128, F32), ("E", 128, F32), ("L1", 128, F32),
                               ("Lb0", 128, F32), ("Rb0", 128, F32), ("Lb1", 128, F32), ("Rb1", 128, F32),
                               ("Ab", 128, BF16), ("ATb", 128, BF16), ("Xf", 128, F32), ("U", 128, F32),
                               ("WT", 128, BF16), ("QD", 128, BF16), ("GUN", 128, F32),
                               ("WM", nb * 128, BF16), ("QM", nb * 128, BF16), ("KDM", nb * 128, BF16),
                               ("VN", 128, BF16), ("O", 128, F32), ("ON", 128, F32)):
                pl = 0
                if sample and nm == "KDM":
                    pl = 3
                if sample and nm == "QM":
                    pl = 1
                d[nm] = self.sc("%s_%d" % (nm, i), sz, dt, pl)
            if not sample:
                d["ATM"] = self.sc("ATM_%d" % i, nb * 128, BF16)
            TB.append(d)

        def issue_w(hh):
            lst = [self.wa_load(w_in[:, part * 1536 + hh * 128:part * 1536 + (hh + 1) * 128], 8192) for part in range(3)]
            lst.append(self.wa_load(w_in[:, 4608 + hh * 128:4608 + (hh + 1) * 128], 8192))
            return lst

        wnext = issue_w(0)
        for h in range(NH):
            wcur = wnext
            for part in range(3):
                cidx = part * NH + h
                pq, pqk = PQ[part]
                wv, wk = wcur[part]
                if sample:
                    self.cp("pool", pq[:, :, 0:3], SCT[:, cidx, :].rearrange("p (s r) -> p s r", s=16), SCTk, pqk)
                else:
                    self.cp("pool", pq[:, 0, 0:3], CT[:, cidx, :], CTk, pqk)
                for bi, (bo, bn) in enumerate(blks):
                    b = self.psum()
                    for c in range(KC):
                        self.mm(self.PS[:, b, 0:bn], wv[:, c, :], hview[:, c, bo:bo + bn], c == 0, c == KC - 1, wk + hkeys, [("ps", b)])
                    if sample:
                        self.cp("act", pq[:, :, 3:3 + L], self.PS[:, b, 0:128].rearrange("p (s l) -> p s l", s=16), [("ps", b)], pqk)
                    else:
                        self.cp("act", pq[:, 0, 3 + bo:3 + bo + bn], self.PS[:, b, 0:bn], [("ps", b)], pqk)
                cw = lambda r_: CW[r_][0][:, cidx:cidx + 1]
                cwk_all = CW[0][1] + CW[1][1] + CW[2][1] + CW[3][1]
                self.ts("dve", YC3, pq[:, :, 0:L], cw(0), None, ALU.mult, None, pqk + cwk_all, YCk)
                for r_ in range(1, 4):
                    self.stt(YC3, pq[:, :, r_:r_ + L], cw(r_), YC3, ALU.mult, ALU.add, pqk + cwk_all + YCk, YCk)
                if sample:
                    self.cp("pool", SCT[:, cidx, :].rearrange("p (s r) -> p s r", s=16), pq[:, :, L:L + 3], pqk, SCTk)
                else:
                    self.cp("pool", CT[:, cidx, :], pq[:, 0, L:L + 3], pqk, CTk)
                self.af(YC, YC, AF.Silu, YCk, YCk)
                if part < 2:
                    sq = self.SQ[:, 0, 0:n]
                    self.tt("pool", sq, YC, YC, ALU.mult, YCk, [("SQ", 0)])
                    rn = self.STAT[:, 0, 0:n]
                    for bi, (bo, bn) in enumerate(blks):
                        b = self.psum()
                        self.mm(self.PS[:, b, 0:bn], self.ONF[:], sq[:, bo:bo + bn], True, True, [("SQ", 0), "ONF"], [("ps", b)])
                        self.af(rn[:, bo:bo + bn], self.PS[:, b, 0:bn], AF.Ln, [("ps", b), "EPSC"], [("STAT", 0)],
                                bias=self.EPSC[:, 0:1], scale=1.0)
                    self.af(rn, rn, AF.Exp, [("STAT", 0)], [("STAT", 0)], scale=-0.5)
                    if part == 0:
                        self.stt(QT, YC, 128.0 ** -0.5, rn, ALU.mult, ALU.mult, YCk + [("STAT", 0)], QTk)
                    else:
                        self.tt("dve", KTf, YC, rn, ALU.mult, YCk + [("STAT", 0)], KTfk)
                        self.cp("pool", KTb, KTf, KTfk, KTbk)
                else:
                    self.cp("dve", VTb, YC, YCk, VTbk)
            wv, wk = wcur[3]
            for bi, (bo, bn) in enumerate(blks):
                b = self.psum()
                for c in range(KC):
                    self.mm(self.PS[:, b, 0:bn], wv[:, c, :], hview[:, c, bo:bo + bn], c == 0, c == KC - 1, wk + hkeys, [("ps", b)])
                self.af(SZ[:, bo:bo + bn], self.PS[:, b, 0:bn], AF.Silu, [("ps", b)], SZk)
            if sample:
                S.dma("sp", S0f3, I["sdelta"][j, self.sq0:self.sq0 + 16, h].rearrange("s k v -> k s v"), w=S0fk)
                S.dma("pool", S0b3, I["sdelta"][j, self.sq0:self.sq0 + 16, h].rearrange("s k v -> k s v"), w=S0bk)
            wnext = issue_w(h + 1) if h + 1 < NH else None
            for t in range(nt):
                d = TB[t % nbuf]
                tc = slice(t * 128, (t + 1) * 128)
                col = lambda arr: arr[:, t, h:h + 1]
                KBE, KBEk = d["KBE"]; KDEC, KDECk = d["KDEC"]; VB, VBk = d["VB"]
                E, Ek = d["E"]; L1, L1k = d["L1"]; Ab, Abk = d["Ab"]; ATb, ATbk = d["ATb"]
                Xf, Xfk = d["Xf"]; Xb, Xbk = Xf, Xfk; U, Uk = d["U"]; WT, WTk = d["WT"]; QD, QDk = d["QD"]
                GUN, GUNk = d["GUN"]; DG, DGk = GUN, GUNk; WM, WMk = d["WM"]; QM, QMk = d["QM"]; KDM, KDMk = d["KDM"]
                VN, VNk = d["VN"]; Ot, Otk = d["O"]; ON, ONk = d["ON"]
                WM3 = WM.rearrange("p (s t) -> p s t", s=nb)
                QM3 = QM.rearrange("p (s t) -> p s t", s=nb)
                KDM3 = KDM.rearrange("p (s t) -> p s t", s=nb)
                b = self.psum()
                pb = self.PS[:, b, :]
                self.tr(pb[:, 0:128], KTf[:, tc], self.IDF[:], KTfk + ["IDF"], [("ps", b)])
                self.tr(pb[:, 128:256], VTb[:, tc], self.IDF[:], VTbk + ["IDF"], [("ps", b)])
                self.ts("dve", KBE, pb[:, 0:128], col(BE), None, ALU.mult, None, [("ps", b)] + BEk, KBEk)
                self.ts("dve", KDEC, pb[:, 0:128], col(EDEC), None, ALU.mult, None, [("ps", b)] + EDECk, KDECk)
                self.af(VB, pb[:, 128:256], AF.Copy, [("ps", b)] + BETAk, VBk, scale=col(BETA))
                self.ts("pool", GUN, UBL, col(NG), None, ALU.mult, None, UBLk + NGk, GUNk)
                b = self.psum()
                self.mm(self.PS[:, b, 0:128], self.ONF[:], GUN, True, False, GUNk + ["ONF"], [("ps", b)])
                self.mm(self.PS[:, b, 0:128], self.IDF[:], NEG, False, True, NEGk + ["IDF"], [("ps", b)])
                self.af(E, self.PS[:, b, 0:128], AF.Exp, [("ps", b)] + GCk, Ek, bias=col(GC), scale=1.0)
                bk = self.psum()
                self.mm(self.PS[:, bk, 0:128], KTb[:, tc], KTb[:, tc], True, True, KTbk, [("ps", bk)])
                self.mm(self.PS[:, bk, 128:256], QT[:, tc], KTb[:, tc], True, True, QTk + KTbk, [("ps", bk)])
                self.stt(L1, self.PS[:, bk, 0:128], col(BETA), E, ALU.mult, ALU.mult, [("ps", bk)] + BETAk + Ek, L1k)
                Lb, Lbk = d["Lb0"]
                Rb, Rbk = d["Rb0"]
                self.tt("pool", Lb, L1, STR, ALU.mult, L1k + STRk, Lbk)
                self.tt("dve", Ab, self.PS[:, bk, 128:256], E, ALU.mult, [("ps", bk)] + Ek, Abk)
                b = self.psum()
                self.tr(self.PS[:, b, 0:128], Lb, self.IDF[:], Lbk + ["IDF"], [("ps", b)])
                self.cp("act", Rb, self.PS[:, b, 0:128], [("ps", b)], Rbk)
                self.tt("dve", Xf, self.IDF[:], Rb, ALU.subtract, Rbk + ["IDF"], Xfk)
                b = self.psum()
                pb = self.psb(b)
                self.tr(pb[:, 0:128], Ab, self.IDB[:], Abk + ["IDB"], [("ps", b)])
                self.cp("act", ATb, pb[:, 0:128], [("ps", b)], ATbk)
                for lev in range(nlev):
                    L2, L2k = d["Lb%d" % ((lev + 1) % 2)]
                    R2, R2k = d["Rb%d" % ((lev + 1) % 2)]
                    b = self.psum()
                    self.mm(self.PS[:, b, 0:128], Rb, Lb, True, True, Rbk + Lbk, [("ps", b)])
                    if lev < nlev - 1:
                        self.mm(self.PS[:, b, 128:256], Lb, Rb, True, True, Rbk + Lbk, [("ps", b)])
                    self.cp("act", L2, self.PS[:, b, 0:128], [("ps", b)], L2k)
                    if lev < nlev - 1:
                        self.cp("dve", R2, self.PS[:, b, 128:256], [("ps", b)], R2k)
                    b2 = self.psum()
                    self.mm(self.PS[:, b2, 0:128], L2, Xb, True, True, L2k + Xbk, [("ps", b2)])
                    self.tt("dve", Xf, Xf, self.PS[:, b2, 0:128], ALU.add, Xfk + [("ps", b2)], Xfk)
                    Lb, Lbk, Rb, Rbk = L2, L2k, R2, R2k
                b = self.psum()
                self.mm(self.PS[:, b, 0:128], Xb, VB, True, True, Xbk + VBk, [("ps", b)])
                self.mm(self.PS[:, b, 128:256], KBE, Xb, True, True, Xbk + KBEk, [("ps", b)])
                self.cp("act", U, self.PS[:, b, 0:128], [("ps", b)], Uk)
                self.cp("dve", WT, self.PS[:, b, 128:256], [("ps", b)], WTk)
                self.ts("pool", DG, self.IDF[:], col(EGC), None, ALU.mult, None, EGCk + ["IDF"], DGk)
                b = self.psum()
                self.mm(self.PS[:, b, 0:128], self.ONF[:], DG, True, True, DGk + ["ONF"], [("ps", b)])
                self.tt("dve", QD, QT[:, tc], self.PS[:, b, 0:128], ALU.mult, QTk + [("ps", b)], QDk)
                self.tt("pool", WM3, WT.unsqueeze(1).broadcast_to([128, nb, 128]), COLM3, ALU.mult, WTk + COLMk, WMk)
                self.tt("pool", QM3, QD.unsqueeze(1).broadcast_to([128, nb, 128]), COLM3, ALU.mult, QDk + COLMk, QMk)
                self.tt("pool", KDM3, KDEC.unsqueeze(1).broadcast_to([128, nb, 128]),
                        ROWM.unsqueeze(2).broadcast_to([128, nb, 128]), ALU.mult, KDECk + ROWMk, KDMk)
                if not sample:
                    ATM, ATMk = d["ATM"]
                    ATM3 = ATM.rearrange("p (s t) -> p s t", s=nb)
                    self.tt("pool", ATM3, ATb.unsqueeze(1).broadcast_to([128, nb, 128]), COLM3, ALU.mult, ATbk + COLMk, ATMk)
                    for s in range(nb):
                        b = self.psum()
                        self.mm(self.PS[:, b, 0:128], WM3[:, s, :], Sb[:, h, :], True, True, WMk + Sbk(h), [("ps", b)])
                        self.tt("dve", VN, U, self.PS[:, b, 0:128], ALU.subtract, Uk + [("ps", b)], VNk)
                        bo_ = self.psum()
                        self.mm(self.PS[:, bo_, 0:128], QM3[:, s, :], Sb[:, h, :], True, False, QMk + Sbk(h), [("ps", bo_)])
                        self.mm(self.PS[:, bo_, 0:128], ATM3[:, s, :], VN, False, True, ATMk + VNk, [("ps", bo_)])
                        if s == 0:
                            self.cp("act", Ot, self.PS[:, bo_, 0:128], [("ps", bo_)], Otk)
                        else:
                            self.tt("dve", Ot, Ot, self.PS[:, bo_, 0:128], ALU.add, Otk + [("ps", bo_)], Otk)
                        bs = self.psum()
                        self.mm(self.PS[:, bs, 0:128], KDM3[:, s, :], VN, True, True, KDMk + VNk, [("ps", bs)])
                        self.stt(Sf[:, h, :], Sf[:, h, :], GLB[:, t, s, h:h + 1], self.PS[:, bs, 0:128], ALU.mult, ALU.add,
                                 Sfk(h) + GLBk + [("ps", bs)], Sfk(h))
                        self.cp("act", Sb[:, h, :], Sf[:, h, :], Sfk(h), Sbk(h))
                else:
                    b = self.psum()
                    for s in range(nb):
                        self.mm(self.PS[:, b, 0:128], WM3[:, s, :], S0b3[:, s, :], s == 0, s == nb - 1, WMk + S0bk, [("ps", b)])
                    self.tt("dve", VN, U, self.PS[:, b, 0:128], ALU.subtract, Uk + [("ps", b)], VNk)
                    bo_ = self.psum()
                    for s in range(nb):
                        self.mm(self.PS[:, bo_, 0:128], QM3[:, s, :], S0b3[:, s, :], s == 0, False, QMk + S0bk, [("ps", bo_)])
                    self.mm(self.PS[:, bo_, 0:128], ATb, VN, False, True, ATbk + VNk, [("ps", bo_)])
                    self.cp("act", Ot, self.PS[:, bo_, 0:128], [("ps", bo_)], Otk)
                    for s in range(nb):
                        bs = self.psum()
                        self.mm(self.PS[:, bs, 0:128], KDM3[:, s, :], VN, True, True, KDMk + VNk, [("ps", bs)])
                        self.stt(S0f3[:, s, :], S0f3[:, s, :], GLB[:, t, s, h:h + 1], self.PS[:, bs, 0:128], ALU.mult, ALU.add,
                                 S0fk + GLBk + [("ps", bs)], S0fk)
                    if commit:
                        S.dma("sp", O["delta_s"][j, self.sq0:self.sq0 + 16, h].rearrange("s k v -> k s v"), S0f3, r=S0fk)
                rs = self.SM[:, 0:1]
                self.tt("pool", L1, Ot, Ot, ALU.mult, Otk, L1k)
                S.add("dve", lambda hh, L1=L1: hh.tensor_reduce(out=self.SM[:, 0:1], in_=L1, axis=mybir.AxisListType.X, op=ALU.add),
                      r=L1k, w=["SM"])
                self.af(rs, rs, AF.Ln, ["SM", "EPSC"], ["SM"], bias=self.EPSC[:, 0:1], scale=1.0 / 128)
                self.af(rs, rs, AF.Exp, ["SM"], ["SM"], scale=-0.5)
                self.ts("dve", ON, Ot, rs, None, ALU.mult, None, Otk + ["SM"], ONk)
                b = self.psum()
                self.tr(self.PS[:, b, 0:128], ON, self.IDF[:], ONk + ["IDF"], [("ps", b)])
                self.stt(mixt[:, h, tc], self.PS[:, b, 0:128], DNG[:, 0:1], SZ[:, tc], ALU.mult, ALU.mult, [("ps", b)] + DNGk + SZk, mk(h))
        if commit and sample:
            cst2, cst2k = cst, cstk
            dst = O["conv_s"][j, self.sq0:self.sq0 + 16].rearrange("s r f -> (s r) f")
            for c9 in range(9):
                b = self.psum()
                for c in range(4):
                    self.tr(self.PS[0:48, b, c * 128:(c + 1) * 128], SCT[:, c9 * 4 + c, :], self.IDF[:], SCTk + ["IDF"], [("ps", b)])
                self.cp("dve", cst2[0:48, :], self.PS[0:48, b, :], [("ps", b)], cst2k)
                S.dma("sp", dst[:, c9 * 512:(c9 + 1) * 512], cst2[0:48, :], r=cst2k)
        if (not sample) and sbi == 1 and not self.last_pass:
            S.dma("sp", self.scrS[j], self.WD[:, 0, 2048:5120].bitcast(F32), r=allSf, w=[("scrS", j)])
            S.dma("sp", self.scrCT[j], self.WD[:, 0, 5120:5336].bitcast(F32), r=CTk, w=[("scrCT", j)])
        if commit and (not sample) and sbi == 1 and self.last_pass:
            S.dma("sp", O["delta_p"][j].rearrange("h k v -> k h v"), Sf, r=allSf)
            cst2, cst2k = YC, YCk
            for c9 in range(9):
                b = self.psum()
                for c in range(4):
                    self.tr(self.PS[0:3, b, c * 128:(c + 1) * 128], CT[:, c9 * 4 + c, :], self.IDF[:], CTk + ["IDF"], [("ps", b)])
                self.cp("dve", cst2[0:3, :], self.PS[0:3, b, :], [("ps", b)], cst2k)
                S.dma("sp", O["conv_p"][j][:, c9 * 512:(c9 + 1) * 512], cst2[0:3, :], r=cst2k)


_PROG = None


def get_prog():
    global _PROG
    if _PROG is None:
        _PROG = Prog()
    return _PROG


def kernel(**inp):
    prog = get_prog()
    return run_prog(prog, inp, 4)


def run_prog(prog, inp, ncores):
    consts = prog.consts
    f32 = lambda a: np.ascontiguousarray(a, dtype=np.float32)
    in_maps = []
    shared = {k: f32(inp[k]) for k in ("norm_gains", "w_ffn_gu", "w_ffn_dn", "w_in_a", "conv_w", "a_log", "dt_bias",
                                        "delta_norm_gain", "w_in_b", "gmlp_ln_gain", "gmlp_ln_bias", "w_spatial",
                                        "b_spatial", "mem_norm_gain", "w_mem_kv", "w_out")}
    for k, v in consts.items():
        shared["c_" + k] = f32(v)
    for c in range(ncores):
        b = c % 4
        m = dict(shared)
        m["xp"] = f32(inp["x_prompt"][b])
        m["xs"] = f32(inp["x_sample"][c * 32:(c + 1) * 32].reshape(2 * TS, D))
        m["mem"] = f32(inp["mem_prompt"][b])
        m["ck"] = f32(inp["cache_mem_k"][:, c * 32:(c + 1) * 32].reshape(DEPTH, 32, NMEM, 512))
        m["cv"] = f32(inp["cache_mem_v"][:, c * 32:(c + 1) * 32].reshape(DEPTH, 32, NMEM, 512))
        m["sdelta"] = f32(inp["state_delta"][:, c * 32:(c + 1) * 32])
        m["sconv"] = f32(inp["state_conv"][:, c * 32:(c + 1) * 32])
        in_maps.append(m)
    res = run_bass_kernel_spmd(prog.nc, in_maps, core_ids=list(range(ncores)))
    R = res.results
    if ncores < 4:
        return R
    y_prompt = np.stack([R[b]["yp"] for b in range(4)])
    y_sample = np.concatenate([R[c]["ys"].reshape(32, 8, D) for c in range(4)], 0)
    mem_k = np.stack([R[b]["memk"] for b in range(4)], 1).reshape(DEPTH, 4, NMEM, XH, 128)
    mem_v = np.stack([R[b]["memv"] for b in range(4)], 1).reshape(DEPTH, 4, NMEM, XH, 128)
    delta_p = np.stack([R[b]["delta_p"] for b in range(4)], 1)
    conv_p = np.stack([R[b]["conv_p"] for b in range(4)], 1)
    delta_s = np.concatenate([R[c]["delta_s"] for c in range(4)], 1)
    conv_s = np.concatenate([R[c]["conv_s"] for c in range(4)], 1)
    gv_s = np.concatenate([R[c]["gv_s"].reshape(2, 32, 8, 1536) for c in range(4)], 1)
    return tuple(np.ascontiguousarray(a, dtype=np.float32) for a in
                 (y_prompt, y_sample, mem_k, mem_v, delta_p, conv_p, delta_s, conv_s, gv_s))
```
